# Optimizing a Trainium2 kernel written in Bass

```python
import math
import jax, jax.numpy as jnp
from jax import lax
import numpy as np

D_MODEL = 1024
BATCH = 8
SEQ = 4096
DEPTH = 4

GRID_W = 64
CTX_LEN = 256
N_MIXERS = 3
D_FF = 4 * D_MODEL
DEEPNORM_ALPHA = (2 * DEPTH) ** 0.25
DEEPNORM_BETA = (8 * DEPTH) ** -0.25
LN_EPS = 1e-5
RMS_EPS = 1e-6

GLA_HEADS = 4
GLA_DK = D_MODEL // 2
GLA_DV = D_MODEL
GLA_HK = GLA_DK // GLA_HEADS
GLA_HV = GLA_DV // GLA_HEADS
GLA_RANK = 16
GLA_GATE_NORM = 16.0
GLA_CHUNK = 64
GLA_IN = 2 * GLA_DK + 2 * GLA_DV + 2 * GLA_RANK

SSD_DI = 2 * D_MODEL
SSD_HEADDIM = 64
SSD_HEADS = SSD_DI // SSD_HEADDIM
SSD_GROUPS = 8
SSD_REP = SSD_HEADS // SSD_GROUPS
SSD_STATE = 128
SSD_CONV = 5
SSD_CHUNK = 64
SSD_GN = SSD_GROUPS * SSD_STATE
SSD_CONV_DIM = SSD_DI + 2 * SSD_GN
SSD_IN = SSD_DI + SSD_CONV_DIM + 2 * SSD_HEADS

HY_ORDER = 2
HY_SHORT = 3
HY_EMB = 33
HY_FW = 64
HY_DECAY_TARGET = 1e-2
HY_FAST_DECAY = 0.3
HY_SLOW_DECAY = 1.5
HY_FILTER_GAIN = 0.05

kernel_name = 'hybrid_gla_ssd_hyena_prefix_dit'


def layer_norm(h, g, b):
    hf = h.astype(jnp.float32)
    mu = jnp.mean(hf, -1, keepdims=True)
    var = jnp.mean(jnp.square(hf - mu), -1, keepdims=True)
    return ((hf - mu) * lax.rsqrt(var + LN_EPS) * g + b).astype(h.dtype)


def rms_norm(h, g):
    hf = h.astype(jnp.float32)
    return (hf * lax.rsqrt(jnp.mean(hf * hf, -1, keepdims=True) + RMS_EPS) * g).astype(h.dtype)


def modulate(h, shift, scale):
    return h * (1 + scale) + shift


def snake(h):
    b, l, ch = h.shape
    rows = l // GRID_W
    g = h.reshape(b, rows, GRID_W, ch)
    odd = (jnp.arange(rows) % 2 == 1)[None, :, None, None]
    return jnp.where(odd, g[:, :, ::-1], g).reshape(b, l, ch)


def dwconv(u, w, bias):
    k, ch = w.shape
    out = lax.conv_general_dilated(u, w.astype(u.dtype)[:, None, :], window_strides=(1,),
                                   padding=[(k // 2, k // 2)],
                                   dimension_numbers=('NWC', 'WIO', 'NWC'),
                                   feature_group_count=ch)
    return out + bias


def to_heads(t, dh):
    b, l, _ = t.shape
    return t.reshape(b, l, -1, dh).transpose(0, 2, 1, 3)


def sq_relu_mlp(h, w1, w2):
    return jnp.square(jax.nn.relu(h @ w1)) @ w2


def gla_chunk_scan(q, k, v, log_a, s0):
    b, h, l, _ = q.shape
    nc = l // GLA_CHUNK

    def chunks(t):
        return jnp.moveaxis(t.reshape(b, h, nc, GLA_CHUNK, t.shape[-1]), 2, 0)

    mask = jnp.tril(jnp.ones((GLA_CHUNK, GLA_CHUNK), dtype=bool))[:, :, None]

    def step(s, inp):
        qi, ki, vi, gi = inp
        cum = jnp.cumsum(gi, axis=2)
        diff = cum[:, :, :, None, :] - cum[:, :, None, :, :]
        decay = jnp.exp(jnp.where(mask, diff, -jnp.inf))
        attn = jnp.einsum('bhid,bhjd,bhijd->bhij', qi, ki, decay)
        o = jnp.einsum('bhij,bhjv->bhiv', attn, vi) + jnp.einsum('bhid,bhdv->bhiv', qi * jnp.exp(cum), s)
        last = cum[:, :, -1:, :]
        s_new = s * jnp.exp(last[:, :, 0, :, None]) + jnp.einsum('bhjd,bhjv->bhdv', ki * jnp.exp(last - cum), vi)
        return s_new, o

    s_fin, oc = lax.scan(step, s0, tuple(map(chunks, (q, k, v, log_a))))
    return jnp.moveaxis(oc, 0, 2).reshape(b, h, l, v.shape[-1]), s_fin


def gla_mixer(uc, ul, w_in, w_a2, b_a2, norm_g, w_out, need_ctx):
    split_at = [GLA_DK, 2 * GLA_DK, 2 * GLA_DK + GLA_DV, 2 * GLA_DK + 2 * GLA_DV]

    def run(h, s_init, need_y):
        b, l, _ = h.shape
        q, k, v, g, a = jnp.split(h @ w_in, split_at, axis=-1)
        q = to_heads(q, GLA_HK) * GLA_HK ** -0.5
        k = to_heads(k, GLA_HK)
        v = to_heads(v, GLA_HV)
        logit = jnp.einsum('blzr,zrd->zbld', a.reshape(b, l, 2, GLA_RANK), w_a2) + b_a2[:, None, None, :]
        log_a = jax.nn.log_sigmoid(logit.astype(jnp.float32)) / GLA_GATE_NORM
        o_f, s_f = gla_chunk_scan(q, k, v, to_heads(log_a[0], GLA_HK), s_init[0])
        o_b, s_b = gla_chunk_scan(q[:, :, ::-1], k[:, :, ::-1], v[:, :, ::-1],
                                  to_heads(log_a[1], GLA_HK)[:, :, ::-1], s_init[1])
        if not need_y:
            return None, (s_f, s_b)
        o = rms_norm(o_f + o_b[:, :, ::-1], norm_g)
        o = o.transpose(0, 2, 1, 3).reshape(b, l, GLA_DV)
        return (o * jax.nn.silu(g)) @ w_out, (s_f, s_b)

    zeros = jnp.zeros((uc.shape[0], GLA_HEADS, GLA_HK, GLA_HV), jnp.float32)
    yc, ctx_states = run(uc, (zeros, zeros), need_ctx)
    yl, _ = run(ul, ctx_states, True)
    return yc, yl


def ssd_chunk_scan(xs, dt, a, bm, cm, s0):
    b, l = xs.shape[:2]
    nc = l // SSD_CHUNK

    def chunks(t):
        return jnp.moveaxis(t.reshape(b, nc, SSD_CHUNK, *t.shape[2:]), 1, 0)

    mask = jnp.tril(jnp.ones((SSD_CHUNK, SSD_CHUNK), dtype=bool))[None, :, :, None, None]

    def step(s, inp):
        xi, dti, bi, ci = inp
        cum = jnp.cumsum(dti * a, axis=1)
        seg = cum[:, :, None] - cum[:, None, :]
        decay = jnp.exp(jnp.where(mask, seg, -jnp.inf))
        xdt = xi * dti[..., None]
        cb = jnp.einsum('bign,bjgn->bijg', ci, bi)
        y = jnp.einsum('bijg,bijgr,bjgrp->bigrp', cb, decay, xdt)
        y = y + jnp.einsum('bign,bigr,bgrpn->bigrp', ci, jnp.exp(cum), s)
        last = cum[:, -1]
        s_new = s * jnp.exp(last)[..., None, None] + jnp.einsum(
            'bjgn,bjgr,bjgrp->bgrpn', bi, jnp.exp(last[:, None] - cum), xdt)
        return s_new, y

    s_fin, ys = lax.scan(step, s0, tuple(map(chunks, (xs, dt, bm, cm))))
    return jnp.moveaxis(ys, 0, 1).reshape(xs.shape), s_fin


def ssd_mixer(uc, ul, w_in, conv_w, conv_b, dt_bias, a_log, d_skip, norm_g, w_out, need_ctx):
    a = (-jnp.exp(a_log.astype(jnp.float32))).reshape(2, SSD_GROUPS, SSD_REP)
    d_gr = d_skip.reshape(SSD_GROUPS, SSD_REP, 1)

    def run(h, s_init, need_y):
        b, l, _ = h.shape
        z, xbc, dt = jnp.split(h @ w_in, [SSD_DI, SSD_DI + SSD_CONV_DIM], axis=-1)
        xbc = jax.nn.silu(dwconv(xbc, conv_w, conv_b))
        xs, bm, cm = jnp.split(xbc, [SSD_DI, SSD_DI + SSD_GN], axis=-1)
        xs = xs.reshape(b, l, SSD_GROUPS, SSD_REP, SSD_HEADDIM)
        bm = bm.reshape(b, l, SSD_GROUPS, SSD_STATE)
        cm = cm.reshape(b, l, SSD_GROUPS, SSD_STATE)
        dt = jax.nn.softplus(dt.astype(jnp.float32).reshape(b, l, 2, SSD_GROUPS, SSD_REP)
                             + dt_bias.reshape(2, SSD_GROUPS, SSD_REP))
        y_f, s_f = ssd_chunk_scan(xs, dt[:, :, 0], a[0], bm, cm, s_init[0])
        y_b, s_b = ssd_chunk_scan(xs[:, ::-1], dt[:, ::-1, 1], a[1], bm[:, ::-1], cm[:, ::-1], s_init[1])
        if not need_y:
            return None, (s_f, s_b)
        y = (y_f + y_b[:, ::-1] + xs * d_gr).reshape(b, l, SSD_DI)
        return rms_norm(y * jax.nn.silu(z), norm_g) @ w_out, (s_f, s_b)

    zeros = jnp.zeros((uc.shape[0], SSD_GROUPS, SSD_REP, SSD_HEADDIM, SSD_STATE), jnp.float32)
    yc, ctx_states = run(uc, (zeros, zeros), need_ctx)
    yl, _ = run(ul, ctx_states, True)
    return yc, yl


def hyena_pos_emb(l):
    bands = (HY_EMB - 1) // 2
    t = jnp.linspace(0.0, 1.0, l)[:, None]
    w = 2 * math.pi * jnp.arange(l, dtype=jnp.float32)[:, None] / l
    ang = jnp.linspace(1e-4, bands - 1, bands)[None, :] * w
    return jnp.concatenate([t, jnp.cos(ang), -jnp.sin(ang)], axis=-1)


def hyena_filters(l, f_w1, f_b1, f_w2, f_b2, f_w3, f_b3, f_w4, f_freq):
    act = lambda t: jnp.sin(f_freq * t)
    hh = act(hyena_pos_emb(l) @ f_w1 + f_b1)
    hh = act(hh @ f_w2 + f_b2)
    hh = act(hh @ f_w3 + f_b3)
    hh = (hh @ f_w4).reshape(l, HY_ORDER, 2, D_MODEL)
    t = jnp.linspace(0.0, 1.0, l)[:, None]
    deltas = jnp.abs(jnp.linspace(math.log(HY_DECAY_TARGET) / HY_FAST_DECAY,
                                  math.log(HY_DECAY_TARGET) / HY_SLOW_DECAY, D_MODEL))
    hh = hh * jnp.exp(-t * deltas)[:, None, None, :]
    fwd, bwd = hh[:, :, 0], hh[:, :, 1]
    return jnp.concatenate([fwd, jnp.zeros_like(fwd[:1]), bwd[:0:-1]], axis=0)


def fft_long_conv(u, h2l, bias):
    l = u.shape[1]
    uf = jnp.fft.rfft(u.astype(jnp.float32), n=2 * l, axis=1)
    hf = jnp.fft.rfft(h2l.astype(jnp.float32), n=2 * l, axis=0)
    y = jnp.fft.irfft(uf * hf, n=2 * l, axis=1)[:, :l]
    return (y + u * bias).astype(u.dtype)


def hyena_mixer(uc, ul, w_in, conv_w, conv_b, f_w1, f_b1, f_w2, f_b2, f_w3, f_b3, f_w4, f_freq,
                h_bias, w_out, need_ctx):
    def run(h):
        l = h.shape[1]
        v, x1, x2 = jnp.split(dwconv(h @ w_in, conv_w, conv_b), 3, axis=-1)
        filt = hyena_filters(l, f_w1, f_b1, f_w2, f_b2, f_w3, f_b3, f_w4, f_freq)
        z = x1 * fft_long_conv(v, filt[:, 0], h_bias[0])
        return (x2 * fft_long_conv(z, filt[:, 1], h_bias[1])) @ w_out

    yc = run(uc) if need_ctx else None
    return yc, run(ul)


def setup_inputs(seed: int = 0) -> dict:
    key = jax.random.key(seed)
    ks = iter(jax.random.split(key, 40))
    nrm = lambda shape, s: jax.random.normal(next(ks), shape, jnp.float32) * s
    uni = lambda shape, lo, hi: jax.random.uniform(next(ks), shape, jnp.float32, minval=lo, maxval=hi)
    inv_softplus = lambda y: y + jnp.log(-jnp.expm1(-y))
    D = D_MODEL
    ng, ns, nh = (len(range(m, DEPTH, N_MIXERS)) for m in range(N_MIXERS))
    return {
        'x': nrm((BATCH, SEQ, D), 1.0),
        'c': nrm((BATCH, D), 1.0),
        'ctx': nrm((BATCH, CTX_LEN, D), 1.0),
        'c_ctx': nrm((D,), 1.0),
        'ada_w': nrm((DEPTH, D, 6 * D), D ** -0.5),
        'ada_b': nrm((DEPTH, 6 * D), 0.02),
        'ln_g': 1.0 + nrm((DEPTH, 2, D), 0.02),
        'ln_b': nrm((DEPTH, 2, D), 0.02),
        'ffn_w1': nrm((DEPTH, D, D_FF), D ** -0.5),
        'ffn_w2': nrm((DEPTH, D_FF, D), D_FF ** -0.5 * DEEPNORM_BETA),
        'gla_w_in': nrm((ng, D, GLA_IN), D ** -0.5),
        'gla_w_a2': nrm((ng, 2, GLA_RANK, GLA_DK), GLA_RANK ** -0.5),
        'gla_b_a2': nrm((ng, 2, GLA_DK), 0.1),
        'gla_norm': 1.0 + nrm((ng, GLA_HV), 0.02),
        'gla_w_out': nrm((ng, GLA_DV, D), GLA_DV ** -0.5 * DEEPNORM_BETA),
        'ssd_w_in': nrm((ns, D, SSD_IN), D ** -0.5),
        'ssd_conv_w': nrm((ns, SSD_CONV, SSD_CONV_DIM), SSD_CONV ** -0.5),
        'ssd_conv_b': nrm((ns, SSD_CONV_DIM), 0.02),
        'ssd_dt_bias': inv_softplus(jnp.exp(uni((ns, 2, SSD_HEADS), math.log(1e-3), math.log(1e-1)))),
        'ssd_a_log': jnp.log(uni((ns, 2, SSD_HEADS), 1.0, 16.0)),
        'ssd_d': 1.0 + nrm((ns, SSD_HEADS), 0.1),
        'ssd_norm': 1.0 + nrm((ns, SSD_DI), 0.02),
        'ssd_w_out': nrm((ns, SSD_DI, D), SSD_DI ** -0.5 * DEEPNORM_BETA),
        'hy_w_in': nrm((nh, D, 3 * D), D ** -0.5),
        'hy_conv_w': nrm((nh, HY_SHORT, 3 * D), HY_SHORT ** -0.5),
        'hy_conv_b': nrm((nh, 3 * D), 0.02),
        'hy_f_w1': nrm((nh, HY_EMB, HY_FW), HY_EMB ** -0.5),
        'hy_f_b1': nrm((nh, HY_FW), 0.1),
        'hy_f_w2': nrm((nh, HY_FW, HY_FW), HY_FW ** -0.5),
        'hy_f_b2': nrm((nh, HY_FW), 0.1),
        'hy_f_w3': nrm((nh, HY_FW, HY_FW), HY_FW ** -0.5),
        'hy_f_b3': nrm((nh, HY_FW), 0.1),
        'hy_f_w4': nrm((nh, HY_FW, HY_ORDER * 2 * D), HY_FW ** -0.5 * HY_FILTER_GAIN),
        'hy_f_freq': 1.0 + nrm((nh, HY_FW), 0.1),
        'hy_bias': nrm((nh, HY_ORDER, D), 1.0),
        'hy_w_out': nrm((nh, D, D), D ** -0.5 * DEEPNORM_BETA),
    }


def reference(x, c, ctx, c_ctx, ada_w, ada_b, ln_g, ln_b, ffn_w1, ffn_w2,
              gla_w_in, gla_w_a2, gla_b_a2, gla_norm, gla_w_out,
              ssd_w_in, ssd_conv_w, ssd_conv_b, ssd_dt_bias, ssd_a_log, ssd_d, ssd_norm, ssd_w_out,
              hy_w_in, hy_conv_w, hy_conv_b, hy_f_w1, hy_f_b1, hy_f_w2, hy_f_b2, hy_f_w3, hy_f_b3,
              hy_f_w4, hy_f_freq, hy_bias, hy_w_out):
    hl, hc = x, ctx
    s_lat = jax.nn.silu(c)
    s_ctx = jax.nn.silu(c_ctx)
    for i in range(DEPTH):
        kind, j = i % N_MIXERS, i // N_MIXERS
        need_ctx = i < DEPTH - 1
        m_l = jnp.split((s_lat @ ada_w[i] + ada_b[i])[:, None, :], 6, axis=-1)
        m_c = jnp.split(s_ctx @ ada_w[i] + ada_b[i], 6, axis=-1)
        ul = snake(modulate(hl, m_l[0], m_l[1]))
        uc = modulate(hc, m_c[0], m_c[1])
        if kind == 0:
            yc, yl = gla_mixer(uc, ul, gla_w_in[j], gla_w_a2[j], gla_b_a2[j], gla_norm[j], gla_w_out[j], need_ctx)
        elif kind == 1:
            yc, yl = ssd_mixer(uc, ul, ssd_w_in[j], ssd_conv_w[j], ssd_conv_b[j], ssd_dt_bias[j],
                               ssd_a_log[j], ssd_d[j], ssd_norm[j], ssd_w_out[j], need_ctx)
        else:
            yc, yl = hyena_mixer(uc, ul, hy_w_in[j], hy_conv_w[j], hy_conv_b[j], hy_f_w1[j], hy_f_b1[j],
                                 hy_f_w2[j], hy_f_b2[j], hy_f_w3[j], hy_f_b3[j], hy_f_w4[j], hy_f_freq[j],
                                 hy_bias[j], hy_w_out[j], need_ctx)
        hl = layer_norm(DEEPNORM_ALPHA * hl + m_l[2] * snake(yl), ln_g[i, 0], ln_b[i, 0])
        hl = layer_norm(DEEPNORM_ALPHA * hl + m_l[5] * sq_relu_mlp(modulate(hl, m_l[3], m_l[4]), ffn_w1[i], ffn_w2[i]),
                        ln_g[i, 1], ln_b[i, 1])
        if need_ctx:
            hc = layer_norm(DEEPNORM_ALPHA * hc + m_c[2] * yc, ln_g[i, 0], ln_b[i, 0])
            hc = layer_norm(DEEPNORM_ALPHA * hc + m_c[5] * sq_relu_mlp(modulate(hc, m_c[3], m_c[4]), ffn_w1[i], ffn_w2[i]),
                            ln_g[i, 1], ln_b[i, 1])
    return hl
```

```python
import numpy as np
import concourse.bass as bass
import concourse.mybir as mybir
from contextlib import ExitStack

F32 = mybir.dt.float32
BF16 = mybir.dt.bfloat16
ALU = mybir.AluOpType
AF = mybir.ActivationFunctionType
AX = mybir.AxisListType


class Sched:
    EPOCH = 20000
    NSLOT = 12

    def __init__(self, nc, es):
        self.nc = nc
        self.es = es
        self.eng = {'pe': nc.tensor, 'act': nc.scalar, 'dve': nc.vector,
                    'pool': nc.gpsimd, 'sp': nc.sync}
        self.esem = {}
        self.ecnt = {}
        self.nsem = 0
        for k in ['pe', 'act', 'dve', 'pool']:
            self._new_epoch(k)
        self.dsem = {q: [self._sem(f'd_{q}{i}') for i in range(self.NSLOT)]
                     for q in ['sp', 'pool', 'act']}
        self.dcnt = {q: [0] * self.NSLOT for q in self.dsem}
        self.dnext = {q: 0 for q in self.dsem}
        self.known = {e: {} for e in self.eng}
        self.lastw = {}
        self.readers = {}
        self.ninst = 0

    def _sem(self, name):
        self.nsem += 1
        return self.es.enter_context(self.nc.semaphore(f'{name}_{self.nsem}'))

    def _new_epoch(self, k):
        self.esem[k] = self._sem('e_' + k)
        self.ecnt[k] = 0

    def _deps(self, me, reads, writes):
        deps = []
        for k in reads:
            w = self.lastw.get(k)
            if w is not None:
                deps.append((w, False))
        for k in writes:
            w = self.lastw.get(k)
            if w is not None:
                deps.append((w, False))
            for r in self.readers.get(k, ()):
                deps.append((r, True))
        out = []
        for (ev, war) in deps:
            sem, val, owner = ev
            if owner == me:
                if me == 'pe':
                    continue
            out.append(ev)
        return out

    def _wait(self, me, evs):
        kn = self.known[me]
        e = self.eng[me]
        for (sem, val, owner) in evs:
            sid = id(sem)
            if kn.get(sid, 0) >= val:
                continue
            e.wait_ge(sem, val)
            kn[sid] = val

    def _record(self, ev, reads, writes):
        for k in writes:
            self.lastw[k] = ev
            self.readers[k] = []
        for k in reads:
            if k in writes:
                continue
            lst = self.readers.setdefault(k, [])
            if ev[2] is not None:
                lst[:] = [r for r in lst if r[2] != ev[2]]
            lst.append(ev)

    def op(self, me, fn, reads=(), writes=()):
        self._wait(me, self._deps(me, reads, writes))
        if self.ecnt[me] >= self.EPOCH:
            self._new_epoch(me)
        ins = fn()
        self.ecnt[me] += 1
        ins.then_inc(self.esem[me], 1)
        ev = (self.esem[me], self.ecnt[me], me)
        self._record(ev, reads, writes)
        self.ninst += 1
        return ev

    def dma(self, q, out, in_, reads=(), writes=(), **kw):
        s = self.dnext[q]
        self.dnext[q] = (s + 1) % self.NSLOT
        sem = self.dsem[q][s]
        evs = self._deps(q, reads, writes)
        if self.dcnt[q][s] > 0:
            evs.append((sem, self.dcnt[q][s], None))
        self._wait(q, evs)
        ins = self.eng[q].dma_start(out=out, in_=in_, **kw)
        self.dcnt[q][s] += 16
        ins.then_inc(sem, 16)
        ev = (sem, self.dcnt[q][s], None)
        self._record(ev, reads, writes)
        self.ninst += 1
        return ev

    def barrier(self):
        evs = []
        for k in self.esem:
            if self.ecnt[k] > 0:
                evs.append((self.esem[k], self.ecnt[k], k))
        for q in self.dsem:
            for s in range(self.NSLOT):
                if self.dcnt[q][s] > 0:
                    evs.append((self.dsem[q][s], self.dcnt[q][s], None))
        for me in self.eng:
            self._wait(me, evs)
        self.lastw = {}
        self.readers = {}
import math
from concourse.bass_utils import run_bass_kernel_spmd

D = 1024
SEQ = 4096
CTXL = 256
T = SEQ + CTXL
NTT = T // 128
DEPTH = 4
ALPHA = (2 * DEPTH) ** 0.25
LN_EPS = 1e-5
BLOCKS = [(i * 512, 512) for i in range(8)] + [(SEQ, CTXL)]
IDENT = AF.Identity


def seg_of(t0):
    return 0 if t0 < SEQ else 1


def col_layout(v):
    v = np.asarray(v, np.float32).reshape(-1, 128)
    return np.ascontiguousarray(v.T)


class ColPack:
    def __init__(self):
        self.cols = []
        self.off = {}
        self.n = 0

    def add(self, name, v):
        c = col_layout(v)
        self.off[name] = self.n
        self.cols.append(c)
        self.n += c.shape[1]

    def array(self):
        return np.ascontiguousarray(np.concatenate(self.cols, axis=1))


class K:
    pass


_UNIQ = [0]


def sb(k, es, name, shape, dt):
    _UNIQ[0] += 1
    return es.enter_context(k.nc.sbuf_tensor(f's{_UNIQ[0]}_{name}', list(shape), dt))


def ps(k, es, name, shape, dt=F32):
    _UNIQ[0] += 1
    return es.enter_context(k.nc.psum_tensor(f'p{_UNIQ[0]}_{name}', list(shape), dt))


class Ring:
    def __init__(self, k, es, name, n, shape, dt, psum=False):
        self.name = name
        self.n = n
        self.i = 0
        self.t = [(ps if psum else sb)(k, es, f'{name}{j}', shape, dt) for j in range(n)]

    def next(self):
        j = self.i % self.n
        self.i += 1
        return self.t[j], (self.name, j)


def act(k, out, in_, func, reads, writes, bias=None, scale=None, accum_out=None):
    kw = {}
    if bias is not None:
        kw['bias'] = bias
    if scale is not None:
        kw['scale'] = scale
    if accum_out is not None:
        kw['accum_out'] = accum_out
    return k.S.op('act', lambda: k.nc.scalar.activation(out=out, in_=in_, func=func, **kw), reads, writes)


def mm(k, out, lhsT, rhs, start, stop, reads, writes):
    return k.S.op('pe', lambda: k.nc.tensor.matmul(out, lhsT, rhs, start=start, stop=stop), reads, writes)


def tt(k, eng, out, in0, in1, op, reads, writes):
    e = k.nc.vector if eng == 'dve' else k.nc.gpsimd
    return k.S.op(eng, lambda: e.tensor_tensor(out=out, in0=in0, in1=in1, op=op), reads, writes)


def ts(k, eng, out, in0, s1, s2, op0, op1, reads, writes):
    e = k.nc.vector if eng == 'dve' else k.nc.gpsimd
    if s2 is None:
        return k.S.op(eng, lambda: e.tensor_scalar(out=out, in0=in0, scalar1=s1, scalar2=None, op0=op0), reads, writes)
    return k.S.op(eng, lambda: e.tensor_scalar(out=out, in0=in0, scalar1=s1, scalar2=s2, op0=op0, op1=op1), reads, writes)


def stt(k, eng, out, in0, scalar, in1, op0, op1, reads, writes):
    e = k.nc.vector if eng == 'dve' else k.nc.gpsimd
    return k.S.op(eng, lambda: e.scalar_tensor_tensor(out=out, in0=in0, scalar=scalar, in1=in1, op0=op0, op1=op1), reads, writes)


def cp(k, eng, out, in_, reads, writes):
    if eng == 'act':
        return k.S.op('act', lambda: k.nc.scalar.copy(out=out, in_=in_), reads, writes)
    e = k.nc.vector if eng == 'dve' else k.nc.gpsimd
    return k.S.op(eng, lambda: e.tensor_copy(out=out, in_=in_), reads, writes)


def memset(k, eng, ap, val, writes):
    e = k.nc.vector if eng == 'dve' else k.nc.gpsimd
    return k.S.op(eng, lambda: e.memset(ap, val), (), writes)


def transpose(k, out, in_, ident, reads, writes):
    return k.S.op('pe', lambda: k.nc.tensor.transpose(out, in_, ident), reads, writes)


def modcol(k, l, seg, m, dc):
    return k.MOD[:, l, seg, m * 8 + dc:m * 8 + dc + 1]


def prologue(k):
    nc, S = k.nc, k.S
    with ExitStack() as es:
        S.dma('sp', k.colp[:], k.din['colp'], writes=['colp'])
        S.dma('sp', k.ident_f[:], k.din['c_ident'], writes=['ident_f'])
        S.dma('pool', k.ident_b[:], k.din['c_ident'], writes=['ident_b'])
        memset(k, 'dve', k.ones_b[:], 1.0 / 1024.0, ['ones_b'])
        memset(k, 'dve', k.ones_f[:], 1.0, ['ones_f'])
        cv = sb(k, es, 'cv', [128, 16], F32)
        sT = sb(k, es, 'sT', [128, 8, 2], BF16)
        S.dma('sp', cv[:], k.din['cvec'], writes=['cv'])
        act(k, sT[:, :, 0], cv[:, 0:8], AF.Silu, ['cv'], ['sT'])
        act(k, sT[:, :, 1], cv[:, 8:16], AF.Silu, ['cv'], ['sT'])
        wr = Ring(k, es, 'adw', 3, [128, 8, 512], BF16)
        pr = Ring(k, es, 'adp', 2, [128, 4, 2], F32, psum=True)
        for l in range(k.nlayers):
            wv = k.din['ada_w'][l].rearrange("(kc p) f -> p kc f", p=128)
            for fg in range(12):
                w, wk = wr.next()
                S.dma('pool', w[:], wv[:, :, fg * 512:(fg + 1) * 512], writes=[wk])
                p, pk = pr.next()
                for fc in range(4):
                    for kc in range(8):
                        mm(k, p[:, fc, :], w[:, kc, fc * 128:(fc + 1) * 128], sT[:, kc, :],
                           kc == 0, kc == 7, [wk, 'sT'], [pk])
                boff = k.cp.off[f'ada_b{l}'] + fg * 4
                for seg in range(2):
                    tt(k, 'dve', k.MOD[:, l, seg, fg * 4:(fg + 1) * 4], p[:, :, seg],
                       k.colp[:, boff:boff + 4], ALU.add, [pk, 'colp'], ['MOD'])
            for seg in range(2):
                for m in (1, 4):
                    ts(k, 'dve', k.MOD[:, l, seg, m * 8:(m + 1) * 8], k.MOD[:, l, seg, m * 8:(m + 1) * 8],
                       1.0, None, ALU.add, None, ['MOD'], ['MOD'])
        psn = sb(k, es, 'psn', [128, 128], F32)
        S.dma('sp', psn[:], k.din['c_psnake'], writes=['psn'])
        xr = Ring(k, es, 'xin', 2, [128, 4, 1024], F32)
        st = Ring(k, es, 'xst', 2, [128, 8, 512], F32)
        pp = Ring(k, es, 'xps', 4, [128, 512], F32, psum=True)
        hv = k.hT.rearrange("(dc p) t -> p dc t", p=128)
        for (t0, n) in BLOCKS:
            nt = n // 128
            xi, xk = xr.next()
            if t0 < SEQ:
                src = k.din['x'][t0:t0 + n, :]
                perm, permk = psn, 'psn'
            else:
                src = k.din['ctx']
                perm, permk = k.ident_f, 'ident_f'
            S.dma('sp', xi[:, 0:nt, :], src.rearrange("(a p) d -> p a d", p=128), writes=[xk])
            so, sk = st.next()
            for dc in range(8):
                p, pk = pp.next()
                for a in range(nt):
                    mm(k, p[:, a * 128:(a + 1) * 128], xi[:, a, dc * 128:(dc + 1) * 128], perm[:],
                       True, True, [xk, permk], [pk])
                cp(k, 'act' if dc % 2 else 'dve', so[:, dc, 0:n], p[:, 0:n], [pk], [sk])
            S.dma('sp', hv[:, :, t0:t0 + n], so[:, :, 0:n], reads=[sk], writes=[('hT', t0)])
        S.barrier()


def epilogue(k):
    nc, S = k.nc, k.S
    with ExitStack() as es:
        psn = sb(k, es, 'psn2', [128, 128], F32)
        S.dma('sp', psn[:], k.din['c_psnake'], writes=['psn'])
        hr = Ring(k, es, 'eh', 2, [128, 8, 512], F32)
        tr = Ring(k, es, 'et', 2, [128, 1024], F32)
        orr = Ring(k, es, 'eo', 2, [128, 4, 1024], F32)
        p1 = Ring(k, es, 'ep1', 4, [128, 512], F32, psum=True)
        p2 = Ring(k, es, 'ep2', 4, [128, 512], F32, psum=True)
        hv = k.hT.rearrange("(dc p) t -> p dc t", p=128)
        for (t0, n) in BLOCKS:
            if t0 >= SEQ:
                continue
            h, hk = hr.next()
            S.dma('sp', h[:], hv[:, :, t0:t0 + n], reads=[('hT', t0)], writes=[hk])
            o, ok = orr.next()
            for a in range(4):
                tm, tk = tr.next()
                for half in range(2):
                    p, pk = p1.next()
                    for q in range(4):
                        dc = half * 4 + q
                        mm(k, p[:, q * 128:(q + 1) * 128], h[:, dc, a * 128:(a + 1) * 128], k.ident_f[:],
                           True, True, [hk, 'ident_f'], [pk])
                    cp(k, 'act' if half else 'dve', tm[:, half * 512:(half + 1) * 512], p[:], [pk], [tk])
                for half in range(2):
                    p, pk = p2.next()
                    mm(k, p[:], psn[:], tm[:, half * 512:(half + 1) * 512], True, True, ['psn', tk], [pk])
                    cp(k, 'act' if half else 'dve', o[:, a, half * 512:(half + 1) * 512], p[:], [pk], [ok])
            S.dma('sp', k.dout[t0:t0 + n, :].rearrange("(a p) d -> p a d", p=128), o[:], reads=[ok], writes=[('out', t0)])
        S.barrier()


def build_uT(k, es, l):
    S = k.S
    uT = sb(k, es, 'uT', [128, 8, T], BF16)
    hr = Ring(k, es, 'uh', 2, [128, 8, 512], F32)
    hv = k.hT.rearrange("(dc p) t -> p dc t", p=128)
    for (t0, n) in BLOCKS:
        seg = seg_of(t0)
        h, hk = hr.next()
        S.dma('sp', h[:, :, 0:n], hv[:, :, t0:t0 + n], reads=[('hT', t0)], writes=[hk])
        for dc in range(8):
            act(k, uT[:, dc, t0:t0 + n], h[:, dc, 0:n], IDENT, [hk, 'MOD'], [('uT', t0)],
                bias=modcol(k, l, seg, 0, dc), scale=modcol(k, l, seg, 1, dc))
    return uT


def inproj(k, es, uT, W, specs):
    S = k.S
    Wv = W.rearrange("(kc p) f -> p kc f", p=128)
    wr = Ring(k, es, 'ipw', 3, [128, 8, 512], BF16)
    pr = Ring(k, es, 'ipp', 4, [128, 512], F32, psum=True)
    ukeys = [('uT', t0) for (t0, n) in BLOCKS]
    for (mode, f0, fsz, handler) in specs:
        w, wk = wr.next()
        S.dma('pool', w[:, :, 0:fsz], Wv[:, :, f0:f0 + fsz], writes=[wk])
        if mode == 'tm':
            for a in range(NTT):
                p, pk = pr.next()
                uk = ('uT', BLOCKS[min(a // 4, 8)][0])
                for kc in range(8):
                    mm(k, p[:, 0:fsz], uT[:, kc, a * 128:(a + 1) * 128], w[:, kc, 0:fsz], kc == 0, kc == 7,
                       [wk, uk], [pk])
                handler(p[:, 0:fsz], pk, a)
        else:
            nfc = (fsz + 127) // 128
            for fc in range(nfc):
                fw = min(128, fsz - fc * 128)
                for (t0, n) in BLOCKS:
                    p, pk = pr.next()
                    for kc in range(8):
                        mm(k, p[0:fw, 0:n], w[:, kc, fc * 128:fc * 128 + fw], uT[:, kc, t0:t0 + n], kc == 0, kc == 7,
                           [wk, ('uT', t0)], [pk])
                    handler(p[0:fw, 0:n], pk, f0 + fc * 128, fw, t0, n)


def ln_block(k, L, r, rk, n, gcol, bcol, out, outk):
    S = k.S
    rb, sq, pst, mean, msq, rstd = L['rb'], L['sq'], L['pst'], L['mean'], L['msq'], L['rstd']
    act(k, rb[:, :, 0:n], r[:, :, 0:n], IDENT, [rk], ['ln_rb'])
    act(k, sq[:, :, 0:n], r[:, :, 0:n], AF.Square, [rk], ['ln_sq'])
    p1, p1k = pst.next()
    p2, p2k = pst.next()
    for dc in range(8):
        mm(k, p1[:, 0:n], k.ones_b[:], rb[:, dc, 0:n], dc == 0, dc == 7, ['ones_b', 'ln_rb'], [p1k])
    for dc in range(8):
        mm(k, p2[:, 0:n], k.ones_b[:], sq[:, dc, 0:n], dc == 0, dc == 7, ['ones_b', 'ln_sq'], [p2k])
    cp(k, 'act', mean[:, 0:n], p1[:, 0:n], [p1k], ['ln_mean'])
    tt(k, 'dve', msq[:, 0:n], mean[:, 0:n], mean[:, 0:n], ALU.mult, ['ln_mean'], ['ln_msq'])
    tt(k, 'dve', msq[:, 0:n], p2[:, 0:n], msq[:, 0:n], ALU.subtract, [p2k, 'ln_msq'], ['ln_msq'])
    ts(k, 'dve', msq[:, 0:n], msq[:, 0:n], LN_EPS, None, ALU.add, None, ['ln_msq'], ['ln_msq'])
    act(k, msq[:, 0:n], msq[:, 0:n], AF.Ln, ['ln_msq'], ['ln_msq'])
    act(k, rstd[:, 0:n], msq[:, 0:n], AF.Exp, ['ln_msq'], ['ln_rstd'], scale=-0.5)
    mb = mean[:, 0:n].unsqueeze(1).to_broadcast([128, 8, n])
    rsb = rstd[:, 0:n].unsqueeze(1).to_broadcast([128, 8, n])
    tt(k, 'dve', r[:, :, 0:n], r[:, :, 0:n], mb, ALU.subtract, [rk, 'ln_mean'], [rk])
    tt(k, 'pool', r[:, :, 0:n], r[:, :, 0:n], rsb, ALU.mult, [rk, 'ln_rstd'], [rk])
    for dc in range(8):
        act(k, out[:, dc, 0:n], r[:, dc, 0:n], IDENT, [rk, 'colp'], [outk],
            bias=k.colp[:, bcol + dc:bcol + dc + 1], scale=k.colp[:, gcol + dc:gcol + dc + 1])


def post(k, l, W_out, KC):
    nc, S = k.nc, k.S
    with ExitStack() as es:
        hv = k.hT.rearrange("(dc p) t -> p dc t", p=128)
        yv = k.yT[0:KC * 128, :].rearrange("(kc p) t -> p kc t", p=128)
        Wo = W_out.rearrange("(kc p) f -> p kc f", p=128)
        W1 = k.din['ffn_w1'][l].rearrange("(kc p) f -> p kc f", p=128)
        W2 = k.din['ffn_w2'][l].rearrange("(fc p) d -> p fc d", p=128)
        wr = Ring(k, es, 'pw', 4, [128, 8, 512], BF16)
        pr = Ring(k, es, 'pp', 6, [128, 512], F32, psum=True)
        L = dict(rb=sb(k, es, 'ln_rb', [128, 8, 512], BF16), sq=sb(k, es, 'ln_sq', [128, 8, 512], BF16),
                 pst=Ring(k, es, 'lnp', 2, [128, 512], F32, psum=True),
                 mean=sb(k, es, 'ln_mean', [128, 512], F32), msq=sb(k, es, 'ln_msq', [128, 512], F32),
                 rstd=sb(k, es, 'ln_rstd', [128, 512], F32))
        hb = sb(k, es, 'p_h', [128, 8, 512], F32)
        yb = sb(k, es, 'p_y', [128, KC, 512], BF16)
        r = sb(k, es, 'p_r', [128, 8, 512], F32)
        h1 = sb(k, es, 'p_h1', [128, 8, 512], F32)
        u2 = sb(k, es, 'p_u2', [128, 8, 512], BF16)
        hh = sb(k, es, 'p_hh', [128, 32, 512], BF16)
        rl = Ring(k, es, 'p_rl', 3, [128, 512], F32)
        g0, b0 = k.cp.off[f'ln_g{l}_0'], k.cp.off[f'ln_b{l}_0']
        g1, b1 = k.cp.off[f'ln_g{l}_1'], k.cp.off[f'ln_b{l}_1']
        for (t0, n) in BLOCKS:
            seg = seg_of(t0)
            S.dma('sp', hb[:, :, 0:n], hv[:, :, t0:t0 + n], reads=[('hT', t0)], writes=['p_h'])
            S.dma('sp', yb[:, :, 0:n], yv[:, :, t0:t0 + n], reads=[('yT', t0)], writes=['p_y'])
            S.op('act', lambda: nc.scalar.mul(out=hb[:, :, 0:n], in_=hb[:, :, 0:n], mul=ALPHA), ['p_h'], ['p_h'])
            for fh in range(2):
                pieces = []
                for kh in range(KC // 8):
                    w, wk = wr.next()
                    S.dma('pool', w[:], Wo[:, kh * 8:(kh + 1) * 8, fh * 512:(fh + 1) * 512], writes=[wk])
                    pieces.append((w, wk))
                for q in range(4):
                    dco = fh * 4 + q
                    p, pk = pr.next()
                    for kc in range(KC):
                        w, wk = pieces[kc // 8]
                        mm(k, p[:, 0:n], w[:, kc % 8, q * 128:(q + 1) * 128], yb[:, kc, 0:n], kc == 0, kc == KC - 1,
                           [wk, 'p_y'], [pk])
                    stt(k, 'dve', r[:, dco, 0:n], p[:, 0:n], modcol(k, l, seg, 2, dco), hb[:, dco, 0:n],
                        ALU.mult, ALU.add, [pk, 'MOD', 'p_h'], ['p_r'])
            ln_block(k, L, r, 'p_r', n, g0, b0, h1, 'p_h1')
            for dc in range(8):
                act(k, u2[:, dc, 0:n], h1[:, dc, 0:n], IDENT, ['p_h1', 'MOD'], ['p_u2'],
                    bias=modcol(k, l, seg, 3, dc), scale=modcol(k, l, seg, 4, dc))
            S.op('act', lambda: nc.scalar.mul(out=h1[:, :, 0:n], in_=h1[:, :, 0:n], mul=ALPHA), ['p_h1'], ['p_h1'])
            for pw in range(8):
                w, wk = wr.next()
                S.dma('pool', w[:], W1[:, :, pw * 512:(pw + 1) * 512], writes=[wk])
                for q in range(4):
                    fc = pw * 4 + q
                    p, pk = pr.next()
                    for kc in range(8):
                        mm(k, p[:, 0:n], w[:, kc, q * 128:(q + 1) * 128], u2[:, kc, 0:n], kc == 0, kc == 7,
                           [wk, 'p_u2'], [pk])
                    rt, rtk = rl.next()
                    act(k, rt[:, 0:n], p[:, 0:n], AF.Relu, [pk], [rtk])
                    tt(k, 'pool' if q % 2 else 'dve', hh[:, fc, 0:n], rt[:, 0:n], rt[:, 0:n], ALU.mult, [rtk], [('p_hh', fc)])
            for dh in range(2):
                accs = [pr.next() for _ in range(4)]
                for g in range(4):
                    w, wk = wr.next()
                    S.dma('pool', w[:], W2[:, g * 8:(g + 1) * 8, dh * 512:(dh + 1) * 512], writes=[wk])
                    for q in range(4):
                        p, pk = accs[q]
                        for fcl in range(8):
                            fc = g * 8 + fcl
                            mm(k, p[:, 0:n], w[:, fcl, q * 128:(q + 1) * 128], hh[:, fc, 0:n],
                               fc == 0, fc == 31, [wk, ('p_hh', fc)], [pk])
                for q in range(4):
                    dco = dh * 4 + q
                    p, pk = accs[q]
                    stt(k, 'dve', r[:, dco, 0:n], p[:, 0:n], modcol(k, l, seg, 5, dco), h1[:, dco, 0:n],
                        ALU.mult, ALU.add, [pk, 'MOD', 'p_h1'], ['p_r'])
            ln_block(k, L, r, 'p_r', n, g1, b1, hb, 'p_h')
            S.dma('sp', hv[:, :, t0:t0 + n], hb[:, :, 0:n], reads=['p_h'], writes=[('hT', t0)])
        S.barrier()

RMS_EPS = 1e-6


def scr(k, name, shape, dt):
    if name not in k.dbg:
        k.dbg[name] = k.nc.dram_tensor('scr_' + name, list(shape), dt).ap()
    return k.dbg[name]


def rstd_from_ss(k, ss, nfeat, key):
    ts(k, 'dve', ss, ss, 1.0 / nfeat, RMS_EPS, ALU.mult, ALU.add, [key], [key])
    act(k, ss, ss, AF.Ln, [key], [key])
    act(k, ss, ss, AF.Exp, [key], [key], scale=-0.5)


def fm_store(k, st, dst, base, scale=None, eng='dve'):
    def h(p, pk, f_lo, fw, t0, n):
        s, sk = st.next()
        if scale is not None:
            act(k, s[0:fw, 0:n], p, IDENT, [pk], [sk], scale=scale)
        else:
            cp(k, eng, s[0:fw, 0:n], p, [pk], [sk])
        k.S.dma('sp', dst[f_lo - base:f_lo - base + fw, t0:t0 + n], s[0:fw, 0:n], reads=[sk])
    return h


def tm_store(k, st, dst, c0, func=None):
    cnt = [0]

    def h(p, pk, a):
        s, sk = st.next()
        w = p.shape[1]
        if func is not None:
            act(k, s[:, 0:w], p, func, [pk], [sk])
        else:
            cnt[0] += 1
            cp(k, 'dve' if cnt[0] % 2 else 'act', s[:, 0:w], p, [pk], [sk])
        k.S.dma('sp', dst[a * 128:(a + 1) * 128, c0:c0 + w], s[:, 0:w], reads=[sk])
    return h


def transpose_out(k, tp, yb, ybk, ytile_ring, nq, yT_view, c):
    for q0 in range(0, nq, 8):
        for q in range(8):
            transpose(k, tp[:, q, :], yb[:, (q0 + q) * 128:(q0 + q + 1) * 128], k.ident_b[:], [ybk, 'ident_b'], ['tp'])
        yt, ytk = ytile_ring.next()
        cp(k, 'act', yt[:], tp[:], ['tp'], [ytk])
        k.S.dma('sp', yT_view[:, q0:q0 + 8, c * 128:(c + 1) * 128], yt[:], reads=[ytk])


def gla_mixer(k, l, j):
    nc, S = k.nc, k.S
    qT = scr(k, 'qT', [512, T], BF16)
    kT = scr(k, 'kT', [512, T], BF16)
    aT = scr(k, 'aT', [32, T], BF16)
    ktm = scr(k, 'ktm', [T, 512], BF16)
    vtm = scr(k, 'vtm', [T, 1024], BF16)
    gtm = scr(k, 'gtm', [T, 1024], BF16)
    oacc = scr(k, 'oacc', [T, 2048], F32)
    W = k.din['gla_w_in'][j]
    with ExitStack() as es:
        uT = build_uT(k, es, l)
        st = Ring(k, es, 'gst', 4, [128, 512], BF16)
        specs = [
            ('fm', 0, 512, fm_store(k, st, qT, 0, scale=128.0 ** -0.5)),
            ('fm', 512, 512, fm_store(k, st, kT, 512)),
            ('fm', 3072, 32, fm_store(k, st, aT, 3072)),
            ('tm', 512, 512, tm_store(k, st, ktm, 0)),
            ('tm', 1024, 512, tm_store(k, st, vtm, 0)),
            ('tm', 1536, 512, tm_store(k, st, vtm, 512)),
            ('tm', 2048, 512, tm_store(k, st, gtm, 0, AF.Silu)),
            ('tm', 2560, 512, tm_store(k, st, gtm, 512, AF.Silu)),
        ]
        inproj(k, es, uT, W, specs)
        S.barrier()
    with ExitStack() as es:
        ctri = sb(k, es, 'gtri', [128, 2, 128], F32)
        crev = sb(k, es, 'grev', [128, 2, 128], F32)
        cmask = sb(k, es, 'gmask', [128, 2, 128], F32)
        gnorm = sb(k, es, 'gnorm', [128, 256], F32)
        w2a = sb(k, es, 'w2a', [33, 2, 512], BF16)
        for d in range(2):
            S.dma('sp', ctri[:, d, :], k.din['c_tri'][d], writes=['gtri'])
            S.dma('sp', crev[:, d, :], k.din['c_rev'][d], writes=['grev'])
            S.dma('sp', cmask[:, d, :], k.din['c_mask'][d], writes=['gmask'])
        S.op('act', lambda: nc.scalar.mul(out=ctri[:], in_=ctri[:], mul=-1.0 / 16.0), ['gtri'], ['gtri'])
        S.op('act', lambda: nc.scalar.mul(out=crev[:], in_=crev[:], mul=-1.0 / 16.0), ['grev'], ['grev'])
        S.dma('sp', gnorm[:], k.din['gla_norm'][j].partition_broadcast(128), writes=['gnorm'])
        memset(k, 'dve', w2a[:], 0.0, ['w2a'])
        for z in range(2):
            S.dma('pool', w2a[z * 16:(z + 1) * 16, z, :], k.din['gla_w_a2'][j, z], writes=['w2a'])
            S.dma('pool', w2a[32:33, z, :], k.din['gla_b_a2'][j, z:z + 1, :], writes=['w2a'])
        aR = Ring(k, es, 'g_a', 2, [33, 128], BF16)
        for t_, tk_ in [aR.next(), aR.next()]:
            memset(k, 'dve', t_[:], 1.0, [tk_])
        qR = Ring(k, es, 'g_q', 2, [128, 4, 128], BF16)
        kR = Ring(k, es, 'g_k', 2, [128, 4, 128], BF16)
        kmR = Ring(k, es, 'g_km', 2, [128, 512], BF16)
        vR = Ring(k, es, 'g_v', 2, [128, 1024], BF16)
        gR = Ring(k, es, 'g_g', 2, [128, 1024], BF16)
        oaR = Ring(k, es, 'g_oa', 2, [128, 1024], F32)
        e_sb = sb(k, es, 'g_e', [128, 512], F32)
        sp_sb = sb(k, es, 'g_sp', [128, 512], F32)
        Eq = sb(k, es, 'g_Eq', [128, 512], F32)
        Ek = sb(k, es, 'g_Ek', [128, 512], F32)
        Er = sb(k, es, 'g_Er', [128, 512], F32)
        qt = sb(k, es, 'g_qt', [128, 512], BF16)
        kt = sb(k, es, 'g_kt', [128, 512], BF16)
        kp = sb(k, es, 'g_kp', [128, 512], BF16)
        AT = sb(k, es, 'g_AT', [128, 4, 128], BF16)
        Sst = sb(k, es, 'g_S', [128, 4, 256], F32)
        Sbf = sb(k, es, 'g_Sb', [128, 4, 256], BF16)
        osR = Ring(k, es, 'g_os', 2, [128, 4, 256], F32)
        ss = sb(k, es, 'g_ss', [128, 4], F32)
        junk = sb(k, es, 'g_junk', [128, 256], F32)
        yb = sb(k, es, 'g_yb', [128, 1024], BF16)
        ytR = Ring(k, es, 'g_yt', 2, [128, 8, 128], BF16)
        lg_ps = ps(k, es, 'g_lg', [128, 512])
        cum_ps = ps(k, es, 'g_cum', [128, 512])
        sc_ps = ps(k, es, 'g_sc', [128, 512])
        o_ps = ps(k, es, 'g_o', [128, 1024])
        st_ps = ps(k, es, 'g_stp', [128, 1024])
        tp = ps(k, es, 'g_tp', [128, 8, 128], BF16)
        qv = qT.rearrange("(h p) t -> p h t", p=128)
        kv = kT.rearrange("(h p) t -> p h t", p=128)
        yv = k.yT[0:1024, :].rearrange("(dc p) t -> p dc t", p=128)
        for d in range(2):
            memset(k, 'dve', Sst[:], 0.0, ['g_S'])
            memset(k, 'dve', Sbf[:], 0.0, ['g_Sb'])
            order = [32, 33] + list(range(32)) if d == 0 else [33, 32] + list(range(31, -1, -1))
            for c in order:
                tsl = slice(c * 128, (c + 1) * 128)
                a_t, ak = aR.next()
                S.dma('sp', a_t[0:32, :], aT[:, tsl], writes=[ak])
                q_t, qk = qR.next()
                S.dma('sp', q_t[:], qv[:, :, tsl], writes=[qk])
                k_t, kk = kR.next()
                S.dma('sp', k_t[:], kv[:, :, tsl], writes=[kk])
                km, kmk = kmR.next()
                S.dma('sp', km[:], ktm[tsl, :], writes=[kmk])
                v_t, vk = vR.next()
                S.dma('sp', v_t[:], vtm[tsl, :], writes=[vk])
                if d == 1:
                    g_t, gk = gR.next()
                    S.dma('sp', g_t[:], gtm[tsl, :], writes=[gk])
                    oa, oak = oaR.next()
                    S.dma('sp', oa[:], oacc[tsl, 0:1024], writes=[oak])
                mm(k, lg_ps[:], a_t[:], w2a[:, d, :], True, True, [ak, 'w2a'], ['g_lg'])
                act(k, e_sb[:], lg_ps[:], AF.Exp, ['g_lg'], ['g_e'], scale=-1.0)
                act(k, sp_sb[:], e_sb[:], AF.Ln, ['g_e'], ['g_sp'], bias=1.0)
                for h in range(4):
                    mm(k, cum_ps[:, h * 128:(h + 1) * 128], sp_sb[:, h * 128:(h + 1) * 128], ctri[:, d, :],
                       True, True, ['g_sp', 'gtri'], ['g_cum'])
                mm(k, lg_ps[:], crev[:, d, :], sp_sb[:], True, True, ['g_sp', 'grev'], ['g_lg'])
                act(k, Eq[:], cum_ps[:], AF.Exp, ['g_cum'], ['g_Eq'])
                act(k, Ek[:], cum_ps[:], AF.Exp, ['g_cum'], ['g_Ek'], scale=-1.0)
                act(k, Er[:], lg_ps[:], AF.Exp, ['g_lg'], ['g_Er'])
                tt(k, 'dve', qt[:], q_t[:].rearrange("p h t -> p (h t)"), Eq[:], ALU.mult, [qk, 'g_Eq'], ['g_qt'])
                tt(k, 'pool', kt[:], k_t[:].rearrange("p h t -> p (h t)"), Ek[:], ALU.mult, [kk, 'g_Ek'], ['g_kt'])
                tt(k, 'dve', kp[:], km[:], Er[:], ALU.mult, [kmk, 'g_Er'], ['g_kp'])
                for h in range(4):
                    mm(k, sc_ps[:, h * 128:(h + 1) * 128], kt[:, h * 128:(h + 1) * 128], qt[:, h * 128:(h + 1) * 128],
                       True, True, ['g_kt', 'g_qt'], ['g_sc'])
                tt(k, 'dve', AT[:], sc_ps[:].rearrange("p (h t) -> p h t", h=4),
                   cmask[:, d, :].unsqueeze(1).to_broadcast([128, 4, 128]), ALU.mult, ['g_sc', 'gmask'], ['g_AT'])
                for h in range(4):
                    mm(k, o_ps[:, h * 256:(h + 1) * 256], AT[:, h, :], v_t[:, h * 256:(h + 1) * 256], True, False,
                       ['g_AT', vk], ['g_o'])
                    mm(k, o_ps[:, h * 256:(h + 1) * 256], qt[:, h * 128:(h + 1) * 128], Sbf[:, h, :], False, True,
                       ['g_qt', 'g_Sb'], ['g_o'])
                for h in range(4):
                    mm(k, st_ps[:, h * 256:(h + 1) * 256], kp[:, h * 128:(h + 1) * 128], v_t[:, h * 256:(h + 1) * 256],
                       True, True, ['g_kp', vk], ['g_stp'])
                last = 127 if d == 0 else 0
                for h in range(4):
                    col = h * 128 + last
                    stt(k, 'dve', Sst[:, h, :], Sst[:, h, :], Eq[:, col:col + 1], st_ps[:, h * 256:(h + 1) * 256],
                        ALU.mult, ALU.add, ['g_S', 'g_Eq', 'g_stp'], ['g_S'])
                cp(k, 'act', Sbf[:], Sst[:], ['g_S'], ['g_Sb'])
                os_, osk = osR.next()
                if d == 0:
                    cp(k, 'act', os_[:].rearrange("p h v -> p (h v)"), o_ps[:], ['g_o'], [osk])
                    S.dma('sp', oacc[tsl, 0:1024], os_[:].rearrange("p h v -> p (h v)"), reads=[osk])
                else:
                    tt(k, 'dve', os_[:].rearrange("p h v -> p (h v)"), o_ps[:], oa[:], ALU.add, ['g_o', oak], [osk])
                    memset(k, 'dve', ss[:], 0.0, ['g_ss'])
                    for h in range(4):
                        act(k, junk[:], os_[:, h, :], AF.Square, [osk], ['g_junk', 'g_ss'], accum_out=ss[:, h:h + 1])
                    rstd_from_ss(k, ss[:], 256.0, 'g_ss')
                    tt(k, 'dve', os_[:], os_[:], ss[:].unsqueeze(2).to_broadcast([128, 4, 256]), ALU.mult,
                       [osk, 'g_ss'], [osk])
                    tt(k, 'pool', os_[:], os_[:], gnorm[:].unsqueeze(1).to_broadcast([128, 4, 256]), ALU.mult,
                       [osk, 'gnorm'], [osk])
                    tt(k, 'dve', yb[:], os_[:].rearrange("p h v -> p (h v)"), g_t[:], ALU.mult, [osk, gk], ['g_yb'])
                    transpose_out(k, tp, yb, 'g_yb', ytR, 8, yv, c)
            S.barrier()


def conv_fm(k, es, src, dst, nchunks, ntap, wname, bname, func, c_off=0):
    S = k.S
    pad = ntap // 2
    xr = Ring(k, es, 'cv_x', 2, [128, SEQ + 2 * pad], BF16)
    dg = Ring(k, es, 'cv_d', 2, [128, ntap, 128], BF16)
    sr = Ring(k, es, 'cv_s', 3, [128, 512], BF16)
    pr = Ring(k, es, 'cv_p', 3, [128, 512], F32, psum=True)
    boff = k.cp.off[bname]
    for cc in range(nchunks):
        d, dk = dg.next()
        for tap in range(ntap):
            woff = k.cp.off[f'{wname}{tap}'] + c_off + cc
            ts(k, 'dve', d[:, tap, :], k.ident_f[:], k.colp[:, woff:woff + 1], None, ALU.mult, None,
               ['ident_f', 'colp'], [dk])
        for (s0, slen) in ((0, SEQ), (SEQ, CTXL)):
            x, xk = xr.next()
            memset(k, 'dve', x[:, 0:pad], 0.0, [xk])
            memset(k, 'dve', x[:, pad + slen:2 * pad + slen], 0.0, [xk])
            S.dma('sp', x[:, pad:pad + slen], src[cc * 128:(cc + 1) * 128, s0:s0 + slen], writes=[xk])
            for t0 in range(0, slen, 512):
                n = min(512, slen - t0)
                p, pk = pr.next()
                for tap in range(ntap):
                    mm(k, p[:, 0:n], d[:, tap, :], x[:, t0 + tap:t0 + tap + n], tap == 0, tap == ntap - 1, [dk, xk], [pk])
                s, sk = sr.next()
                act(k, s[:, 0:n], p[:, 0:n], func, [pk, 'colp'], [sk],
                    bias=k.colp[:, boff + c_off + cc:boff + c_off + cc + 1])
                S.dma('sp', dst[cc * 128:(cc + 1) * 128, s0 + t0:s0 + t0 + n], s[:, 0:n], reads=[sk])


def fm_to_tm(k, es, src, c0, nch, dst, d0):
    S = k.S
    ir = Ring(k, es, 'f2t_i', 2, [128, 8, 128], BF16)
    orr = Ring(k, es, 'f2t_o', 2, [128, 8, 128], BF16)
    pr = Ring(k, es, 'f2t_p', 2, [128, 8, 128], BF16, psum=True)
    sv = src.rearrange("(c p) t -> p c t", p=128)
    for a in range(NTT):
        for q0 in range(0, nch, 8):
            i_, ik = ir.next()
            S.dma('sp', i_[:], sv[:, c0 + q0:c0 + q0 + 8, a * 128:(a + 1) * 128], writes=[ik])
            p, pk = pr.next()
            for q in range(8):
                transpose(k, p[:, q, :], i_[:, q, :], k.ident_b[:], [ik, 'ident_b'], [pk])
            o, ok = orr.next()
            cp(k, 'act' if (q0 // 8) % 2 else 'dve', o[:], p[:], [pk], [ok])
            S.dma('sp', dst[a * 128:(a + 1) * 128, d0 + q0 * 128:d0 + (q0 + 8) * 128],
                  o[:].rearrange("p q t -> p (q t)"), reads=[ok])


def ssd_mixer(k, l, j):
    nc, S = k.nc, k.S
    ztm = scr(k, 'ztm', [T, 2048], BF16)
    xbcT = scr(k, 'xbcT', [4096, T], BF16)
    cvT = scr(k, 'cvT', [4096, T], BF16)
    xbtm = scr(k, 'xbtm', [T, 3072], BF16)
    dtm = scr(k, 'dtm', [T, 64], F32)
    dtam = scr(k, 'dtam', [T, 64], F32)
    yacc = scr(k, 'oacc', [T, 2048], F32)
    W = k.din['ssd_w_in'][j]
    with ExitStack() as es:
        uT = build_uT(k, es, l)
        st = Ring(k, es, 'sst', 4, [128, 512], BF16)
        dtb = sb(k, es, 's_dtb', [128, 64], F32)
        abc = sb(k, es, 's_abc', [128, 64], F32)
        S.dma('sp', dtb[:], k.din['ssd_dt_bias'][j].rearrange("a h -> (a h)").partition_broadcast(128), writes=['s_dtb'])
        S.dma('sp', abc[:], k.din['ssd_a_log'][j].rearrange("a h -> (a h)").partition_broadcast(128), writes=['s_abc'])
        act(k, abc[:], abc[:], AF.Exp, ['s_abc'], ['s_abc'])
        ts(k, 'dve', abc[:], abc[:], -1.0, None, ALU.mult, None, ['s_abc'], ['s_abc'])
        dr = Ring(k, es, 's_dt', 2, [128, 2, 64], F32)

        def h_dt(p, pk, a):
            d, dk = dr.next()
            tt(k, 'dve', d[:, 0, :], p, dtb[:], ALU.add, [pk, 's_dtb'], [dk])
            act(k, d[:, 0, :], d[:, 0, :], AF.Exp, [dk], [dk])
            act(k, d[:, 0, :], d[:, 0, :], AF.Ln, [dk], [dk], bias=1.0)
            tt(k, 'dve', d[:, 1, :], d[:, 0, :], abc[:], ALU.mult, [dk, 's_abc'], [dk])
            S.dma('sp', dtm[a * 128:(a + 1) * 128, :], d[:, 0, :], reads=[dk])
            S.dma('sp', dtam[a * 128:(a + 1) * 128, :], d[:, 1, :], reads=[dk])
        specs = [('tm', i * 512, 512, tm_store(k, st, ztm, i * 512, AF.Silu)) for i in range(4)]
        specs += [('fm', 2048 + i * 512, 512, fm_store(k, st, xbcT, 2048, eng=('dve' if i % 2 else 'act'))) for i in range(8)]
        specs += [('tm', 6144, 64, h_dt)]
        inproj(k, es, uT, W, specs)
        S.barrier()
    import os as _os
    _stop = int(_os.environ.get('SSD_STOP', '99'))
    if _stop <= 1:
        return
    with ExitStack() as es:
        conv_fm(k, es, xbcT, cvT, 32, 5, 'ssd_conv_w', 'ssd_conv_b', AF.Silu)
        S.barrier()
    if _stop <= 2:
        return
    with ExitStack() as es:
        fm_to_tm(k, es, cvT, 0, 24, xbtm, 0)
        S.barrier()
    if _stop <= 3:
        return
    with ExitStack() as es:
        stri = sb(k, es, 'stri', [128, 2, 128], F32)
        srev = sb(k, es, 'srev', [128, 2, 128], F32)
        mbias = sb(k, es, 'smb', [128, 2, 128], F32)
        selh = sb(k, es, 'selh', [32, 32 * 128], F32)
        for d in range(2):
            S.dma('sp', stri[:, d, :], k.din['c_tri'][d], writes=['stri'])
            S.dma('sp', srev[:, d, :], k.din['c_rev'][d], writes=['srev'])
            S.dma('sp', mbias[:, d, :], k.din['c_mbias'][d], writes=['smb'])
        S.dma('sp', selh[:], k.din['c_selh'], writes=['selh'])
        dsk = sb(k, es, 's_dsk', [128, 32], F32)
        gn = sb(k, es, 's_gn', [128, 2048], F32)
        S.dma('sp', dsk[:], k.din['ssd_d'][j].partition_broadcast(128), writes=['s_dsk'])
        S.dma('sp', gn[:], k.din['ssd_norm'][j].partition_broadcast(128), writes=['s_gn'])
        BR = Ring(k, es, 's_B', 2, [128, 8, 128], BF16)
        CR = Ring(k, es, 's_C', 2, [128, 8, 128], BF16)
        XR = Ring(k, es, 's_X', 2, [128, 3072], BF16)
        dR = Ring(k, es, 's_d', 2, [128, 64], F32)
        daR = Ring(k, es, 's_da', 2, [128, 64], F32)
        zR = Ring(k, es, 's_z', 2, [128, 2048], BF16)
        yaR = Ring(k, es, 's_ya', 2, [128, 2048], F32)
        exps = sb(k, es, 's_ex', [128, 96], F32)
        negc = sb(k, es, 's_nc', [128, 32], F32)
        cumT = sb(k, es, 's_cT', [32, 128], F32)
        dtw = sb(k, es, 's_dtw', [128, 32], F32)
        xdt = sb(k, es, 's_xdt', [128, 2048], BF16)
        xdtw = sb(k, es, 's_xdtw', [128, 2048], BF16)
        DsR = Ring(k, es, 's_Ds', 2, [128, 4, 128], F32)
        LR = Ring(k, es, 's_L', 2, [128, 4, 128], BF16)
        ysb = sb(k, es, 's_y', [128, 2048], F32)
        Sst = sb(k, es, 's_S', [128, 8, 256], F32)
        Sbf = sb(k, es, 's_Sb', [128, 8, 256], BF16)
        ss = sb(k, es, 's_ss', [128, 1], F32)
        junk = sb(k, es, 's_junk', [128, 2048], F32)
        yb = sb(k, es, 's_yb', [128, 2048], BF16)
        ytR = Ring(k, es, 's_yt', 2, [128, 8, 128], BF16)
        sm_ps = ps(k, es, 's_sm', [128, 512])
        cb_ps = ps(k, es, 's_cb', [128, 512])
        DpR = Ring(k, es, 's_Dp', 2, [128, 4, 128], F32, psum=True)
        YpR = Ring(k, es, 's_Yp', 2, [128, 512], F32, psum=True)
        st_ps = ps(k, es, 's_stp', [128, 512])
        tp = ps(k, es, 's_tp', [128, 8, 128], BF16)
        Bv = cvT[2048:3072, :].rearrange("(g p) t -> p g t", p=128)
        Cv = cvT[3072:4096, :].rearrange("(g p) t -> p g t", p=128)
        yv = k.yT[0:2048, :].rearrange("(dc p) t -> p dc t", p=128)
        for d in range(2):
            dsl = slice(d * 32, (d + 1) * 32)
            memset(k, 'dve', Sst[:], 0.0, ['s_S'])
            memset(k, 'dve', Sbf[:], 0.0, ['s_Sb'])
            order = [32, 33] + list(range(32)) if d == 0 else [33, 32] + list(range(31, -1, -1))
            order = order[:int(_os.environ.get('SSD_NCH', '99'))]
            for c in order:
                tsl = slice(c * 128, (c + 1) * 128)
                B_, Bk = BR.next()
                S.dma('sp', B_[:], Bv[:, :, tsl], writes=[Bk])
                C_, Ck = CR.next()
                S.dma('sp', C_[:], Cv[:, :, tsl], writes=[Ck])
                X_, Xk = XR.next()
                S.dma('sp', X_[:], xbtm[tsl, :], writes=[Xk])
                dt_, dtk = dR.next()
                S.dma('sp', dt_[:], dtm[tsl, :], writes=[dtk])
                da_, dak = daR.next()
                S.dma('sp', da_[:], dtam[tsl, :], writes=[dak])
                if d == 1:
                    z_, zk = zR.next()
                    S.dma('sp', z_[:], ztm[tsl, :], writes=[zk])
                    ya, yak = yaR.next()
                    S.dma('sp', ya[:], yacc[tsl, :], writes=[yak])
                if int(_os.environ.get('SSD_P2', '9')) <= 0:
                    continue
                mm(k, sm_ps[:, 0:32], stri[:, d, :], da_[:, dsl], True, True, ['stri', dak], ['s_sm'])
                mm(k, sm_ps[:, 32:64], srev[:, d, :], da_[:, dsl], True, True, ['srev', dak], ['s_sm'])
                mm(k, sm_ps[:, 64:96], k.ones_f[:], da_[:, dsl], True, True, ['ones_f', dak], ['s_sm'])
                mm(k, sm_ps[0:32, 128:256], da_[:, dsl], stri[:, d, :], True, True, ['stri', dak], ['s_sm'])
                if _os.environ.get('SSD_SUB') == 'b':
                    continue
                act(k, exps[:], sm_ps[:, 0:96], AF.Exp, ['s_sm'], ['s_ex'])
                if _os.environ.get('SSD_SUB') == 'c':
                    continue
                act(k, negc[:], sm_ps[:, 0:32], IDENT, ['s_sm'], ['s_nc'], scale=-1.0)
                if _os.environ.get('SSD_SUB') == 'd':
                    continue
                cp(k, 'act', cumT[:], sm_ps[0:32, 128:256], ['s_sm'], ['s_cT'])
                if _os.environ.get('SSD_SUB') == 'e':
                    continue
                tt(k, 'dve', dtw[:], dt_[:, dsl], exps[:, 32:64], ALU.mult, [dtk, 's_ex'], ['s_dtw'])
                xs3 = X_[:, 0:2048].rearrange("p (h q) -> p h q", q=64)
                if _os.environ.get('SSD_SUB') == 'a':
                    continue
                tt(k, 'dve', xdt[:].rearrange("p (h q) -> p h q", q=64), xs3,
                   dt_[:, dsl].unsqueeze(2).to_broadcast([128, 32, 64]), ALU.mult, [Xk, dtk], ['s_xdt'])
                tt(k, 'dve', xdtw[:].rearrange("p (h q) -> p h q", q=64), xs3,
                   dtw[:].unsqueeze(2).to_broadcast([128, 32, 64]), ALU.mult, [Xk, 's_dtw'], ['s_xdtw'])
                _lvl = int(_os.environ.get('SSD_P2', '9'))
                if _lvl <= 1:
                    continue
                for g in range(8):
                    mm(k, cb_ps[:, 0:128], B_[:, g, :], C_[:, g, :], True, True, [Bk, Ck], ['s_cb'])
                    Dp, Dpk = DpR.next()
                    for r in range(4):
                        h = g * 4 + r
                        mm(k, Dp[:, r, :], selh[:, h * 128:(h + 1) * 128], cumT[:], True, False, ['selh', 's_cT'], [Dpk])
                        mm(k, Dp[:, r, :], k.ident_f[:], mbias[:, d, :], False, True, ['ident_f', 'smb'], [Dpk])
                    Ds, Dsk = DsR.next()
                    for r in range(4):
                        h = g * 4 + r
                        act(k, Ds[:, r, :], Dp[:, r, :], AF.Exp, [Dpk, 's_nc'], [Dsk], bias=negc[:, h:h + 1])
                    L_, Lk = LR.next()
                    tt(k, 'dve', L_[:], Ds[:], cb_ps[:, 0:128].unsqueeze(1).to_broadcast([128, 4, 128]), ALU.mult,
                       [Dsk, 's_cb'], [Lk])
                    if _lvl <= 2:
                        continue
                    Yp, Ypk = YpR.next()
                    for r in range(4):
                        h = g * 4 + r
                        mm(k, Yp[:, r * 64:(r + 1) * 64], L_[:, r, :], xdt[:, h * 64:(h + 1) * 64], True, True,
                           [Lk, 's_xdt'], [Ypk])
                    mm(k, Yp[:, 256:512], C_[:, g, :], Sbf[:, g, :], True, True, [Ck, ('s_Sb', g)], [Ypk])
                    mm(k, st_ps[:, 0:256], X_[:, 2048 + g * 128:2048 + (g + 1) * 128], xdtw[:, g * 256:(g + 1) * 256],
                       True, True, [Xk, 's_xdtw'], ['s_stp'])
                    yg = ysb[:, g * 256:(g + 1) * 256]
                    tt(k, 'dve', yg.rearrange("p (r q) -> p r q", q=64), Yp[:, 256:512].rearrange("p (r q) -> p r q", q=64),
                       exps[:, g * 4:(g + 1) * 4].unsqueeze(2).to_broadcast([128, 4, 64]), ALU.mult,
                       [Ypk, 's_ex'], [('s_y', g)])
                    tt(k, 'dve', yg, yg, Yp[:, 0:256], ALU.add, [Ypk, ('s_y', g)], [('s_y', g)])
                    if _lvl <= 3:
                        continue
                    sg = Sst[:, g, :]
                    tt(k, 'dve', sg.rearrange("p (r q) -> p r q", q=64), sg.rearrange("p (r q) -> p r q", q=64),
                       exps[:, 64 + g * 4:64 + (g + 1) * 4].unsqueeze(2).to_broadcast([128, 4, 64]), ALU.mult,
                       [('s_S', g), 's_S', 's_ex'], [('s_S', g)])
                    tt(k, 'dve', sg, sg, st_ps[:, 0:256], ALU.add, [('s_S', g), 's_stp'], [('s_S', g)])
                    cp(k, 'act', Sbf[:, g, :], sg, [('s_S', g)], [('s_Sb', g)])
                ykeys = [('s_y', g) for g in range(8)]
                if _lvl <= 4:
                    continue
                if d == 0:
                    S.dma('sp', yacc[tsl, :], ysb[:], reads=ykeys)
                else:
                    tt(k, 'dve', ysb[:], ysb[:], ya[:], ALU.add, ykeys + [yak], ['s_yf'])
                    tt(k, 'dve', junk[:].rearrange("p (h q) -> p h q", q=64), xs3,
                       dsk[:].unsqueeze(2).to_broadcast([128, 32, 64]), ALU.mult, [Xk, 's_dsk'], ['s_junk'])
                    tt(k, 'dve', ysb[:], ysb[:], junk[:], ALU.add, ['s_yf', 's_junk'] + ykeys, ['s_yf'] + ykeys)
                    tt(k, 'dve', ysb[:], ysb[:], z_[:], ALU.mult, ['s_yf', zk] + ykeys, ['s_yf'] + ykeys)
                    memset(k, 'dve', ss[:], 0.0, ['s_ss'])
                    act(k, junk[:], ysb[:], AF.Square, ['s_yf'] + ykeys, ['s_junk', 's_ss'], accum_out=ss[:])
                    rstd_from_ss(k, ss[:], 2048.0, 's_ss')
                    ts(k, 'dve', ysb[:], ysb[:], ss[:, 0:1], None, ALU.mult, None, ['s_yf', 's_ss'] + ykeys, ['s_yf'] + ykeys)
                    tt(k, 'dve', yb[:], ysb[:], gn[:], ALU.mult, ['s_yf', 's_gn'] + ykeys, ['s_yb'])
                    transpose_out(k, tp, yb, 's_yb', ytR, 16, yv, c)
            S.barrier()

HY_SEGS = {0: dict(s0=0, L=SEQ), 1: dict(s0=SEQ, L=CTXL)}
for _s in HY_SEGS.values():
    _s['N'] = 2 * _s['L']
    _s['N1'] = _s['N'] // 128
    _s['M1'] = _s['N1'] // 2
    _s['K1'] = _s['N1'] // 2 + 1
    _s['R'] = 2 * _s['K1']


def hyena_constants():
    c = {}
    a = np.arange(128)
    th = 2 * np.pi * ((a[:, None] * a[None, :]) % 128) / 128.0
    c['c_cs'] = np.cos(th).astype(np.float32)
    c['c_sn'] = np.sin(th).astype(np.float32)
    c['c_deltas'] = np.abs(np.linspace(math.log(1e-2) / 0.3, math.log(1e-2) / 1.5, 1024)).astype(np.float32)
    for s, g in HY_SEGS.items():
        L, N, N1, M1, K1, R = g['L'], g['N'], g['N1'], g['M1'], g['K1'], g['R']
        m = (128 * np.arange(M1)[:, None] + np.arange(128)[None, :]).astype(np.int64)
        k1 = np.arange(K1, dtype=np.int64)
        th = 2 * np.pi * ((m[:, :, None] * k1[None, None, :]) % N) / float(N)
        ef = np.zeros((M1, 128, R), np.float64)
        ef[:, :, 0::2] = np.cos(th)
        ef[:, :, 1::2] = -np.sin(th)
        c[f'c_efwd{s}'] = ef.astype(np.float32)
        w = np.full(K1, 2.0)
        w[0] = 1.0
        w[-1] = 1.0
        ei = np.zeros((R, 128, M1), np.float64)
        ei[0::2] = (np.cos(th) * w[None, None, :] / N).transpose(2, 1, 0)
        ei[1::2] = (-np.sin(th) * w[None, None, :] / N).transpose(2, 1, 0)
        c[f'c_einv{s}'] = ei.astype(np.float32)
        t = np.linspace(0.0, 1.0, L, dtype=np.float32)[:, None]
        wv = (2 * math.pi * np.arange(L, dtype=np.float32)[:, None] / L).astype(np.float32)
        ang = np.linspace(1e-4, 15, 16, dtype=np.float32)[None, :] * wv
        emb = np.concatenate([t, np.cos(ang), -np.sin(ang)], axis=-1).astype(np.float32)
        c[f'c_emb{s}'] = np.ascontiguousarray(emb.T)
        c[f'c_tcol{s}'] = np.ascontiguousarray(t[:, 0][m.reshape(-1)].reshape(M1, 128))
    return c


def hyena_host_layout(inputs, m):
    cols = [inputs['hy_f_b1'][0], inputs['hy_f_b2'][0], inputs['hy_f_b3'][0], inputs['hy_f_freq'][0]]
    m['hycol'] = np.ascontiguousarray(np.stack([np.asarray(c, np.float32) for c in cols], axis=1))


def hy_common(k, es):
    S = k.S
    H = {}
    for nm, src, neg in (('cs', 'c_cs', False), ('sn', 'c_sn', False), ('ncs', 'c_cs', True), ('nsn', 'c_sn', True)):
        t = sb(k, es, 'hy_' + nm, [128, 128], BF16)
        tf = sb(k, es, 'hyf_' + nm, [128, 128], F32)
        S.dma('sp', tf[:], k.din[src], writes=['hyf_' + nm])
        S.op('act', lambda: k.nc.scalar.mul(out=t[:], in_=tf[:], mul=(-1.0 if neg else 1.0)), ['hyf_' + nm], ['hy_' + nm])
        H[nm] = t
    return H


def load_E(k, es, s):
    g = HY_SEGS[s]
    ef = sb(k, es, f'hy_ef{s}', [g['M1'], 128, g['R']], BF16)
    ei = sb(k, es, f'hy_ei{s}', [g['R'], 128, g['M1']], BF16)
    k.S.dma('pool', ef[:], k.din[f'c_efwd{s}'], writes=['hy_ef'])
    k.S.dma('pool', ei[:], k.din[f'c_einv{s}'], writes=['hy_ei'])
    return ef, ei


def hy_filters(k, j, s, Hspec, BdF):
    nc, S = k.nc, k.S
    g = HY_SEGS[s]
    L, M1, K1, R = g['L'], g['M1'], g['K1'], g['R']
    with ExitStack() as es:
        H = hy_common(k, es)
        ef, ei = load_E(k, es, s)
        hycol = sb(k, es, 'hycol', [64, 4], F32)
        S.dma('sp', hycol[:], k.din['hycol'], writes=['hycol'])
        negpi = sb(k, es, 'negpi', [128, 1], F32)
        memset(k, 'dve', negpi[:], -math.pi, ['negpi'])
        ki = sb(k, es, 'hy_ki', [64, 512], mybir.dt.int32)
        kf = sb(k, es, 'hy_kf', [64, 512], F32)
        embT = sb(k, es, 'hy_emb', [33, L], F32)
        S.dma('sp', embT[:], k.din[f'c_emb{s}'], writes=['hy_emb'])
        w1 = sb(k, es, 'hy_w1', [33, 64], F32)
        w2 = sb(k, es, 'hy_w2', [64, 64], F32)
        w3 = sb(k, es, 'hy_w3', [64, 64], F32)
        w4 = sb(k, es, 'hy_w4', [64, 4096], BF16)
        S.dma('sp', w1[:], k.din['hy_f_w1'][j], writes=['hy_w1'])
        S.dma('sp', w2[:], k.din['hy_f_w2'][j], writes=['hy_w2'])
        S.dma('sp', w3[:], k.din['hy_f_w3'][j], writes=['hy_w3'])
        S.dma('pool', w4[:], k.din['hy_f_w4'][j], writes=['hy_w4'])
        hA = sb(k, es, 'hy_hA', [64, L], F32)
        hB = sb(k, es, 'hy_hB', [64, L], F32)
        h3 = sb(k, es, 'hy_h3', [64, L], BF16)
        arg = Ring(k, es, 'hy_arg', 2, [64, 512], F32)
        mp = Ring(k, es, 'hy_mp', 2, [64, 512], F32, psum=True)
        layers = [(w1, 'hy_w1', embT, 'hy_emb', hA, 'hy_hA', 0), (w2, 'hy_w2', hA, 'hy_hA', hB, 'hy_hB', 1),
                  (w3, 'hy_w3', hB, 'hy_hB', h3, 'hy_h3', 2)]
        for (w, wk, src, srck, dst, dstk, li) in layers:
            for t0 in range(0, L, 512):
                n = min(512, L - t0)
                p, pk = mp.next()
                mm(k, p[:, 0:n], w[:], src[:, t0:t0 + n], True, True, [wk, srck], [pk])
                a_, ak = arg.next()
                ts(k, 'dve', a_[:, 0:n], p[:, 0:n], hycol[:, li:li + 1], hycol[:, 3:4], ALU.add, ALU.mult, [pk, 'hycol'], [ak])
                ts(k, 'dve', a_[:, 0:n], a_[:, 0:n], 1.0 / (2.0 * math.pi), 8.0, ALU.mult, ALU.add, [ak], [ak])
                cp(k, 'dve', ki[:, 0:n], a_[:, 0:n], [ak], ['hy_ki'])
                cp(k, 'dve', kf[:, 0:n], ki[:, 0:n], ['hy_ki'], ['hy_kf'])
                tt(k, 'dve', a_[:, 0:n], a_[:, 0:n], kf[:, 0:n], ALU.subtract, [ak, 'hy_kf'], [ak])
                ts(k, 'dve', kf[:, 0:n], a_[:, 0:n], 0.5, None, ALU.is_gt, None, [ak], ['hy_kf'])
                tt(k, 'dve', a_[:, 0:n], a_[:, 0:n], kf[:, 0:n], ALU.subtract, [ak, 'hy_kf'], [ak])
                act(k, dst[:, t0:t0 + n], a_[:, 0:n], AF.Sin, [ak], [dstk], scale=2.0 * math.pi)
        dl = sb(k, es, 'hy_dl', [M1, 1024], F32)
        tcol = sb(k, es, 'hy_tc', [M1, 128], F32)
        S.dma('sp', dl[:], k.din['c_deltas'].partition_broadcast(M1), writes=['hy_dl'])
        S.dma('sp', tcol[:], k.din[f'c_tcol{s}'], writes=['hy_tc'])
        ts(k, 'dve', tcol[:], tcol[:], -1.0, None, ALU.mult, None, ['hy_tc'], ['hy_tc'])
        wn = Ring(k, es, 'hy_wn', 2, [M1, 1024], F32)
        ft = Ring(k, es, 'hy_ft', 2, [M1, 4096], BF16)
        fp = Ring(k, es, 'hy_fp', 3, [M1, 512], F32, psum=True)
        bp = Ring(k, es, 'hy_bp', 3, [R, 512], F32, psum=True)
        bs = Ring(k, es, 'hy_bs', 2, [R, 4096], BF16)
        h3v = h3[:].rearrange("r (a b) -> r b a", b=128)
        for m2 in range(128):
            wt, wtk = wn.next()
            act(k, wt[:], dl[:], AF.Exp, ['hy_dl', 'hy_tc'], [wtk], scale=tcol[:, m2:m2 + 1])
            f_, fk = ft.next()
            for q in range(8):
                p, pk = fp.next()
                mm(k, p[:], h3v[:, m2, :], w4[:, q * 512:(q + 1) * 512], True, True, ['hy_h3', 'hy_w4'], [pk])
                cw = (q % 2) * 512
                tt(k, 'dve', f_[:, q * 512:(q + 1) * 512], p[:], wt[:, cw:cw + 512], ALU.mult, [pk, wtk], [fk])
            if m2 == 0:
                for o in range(2):
                    memset(k, 'dve', f_[0:1, o * 2048 + 1024:(o + 1) * 2048], 0.0, [fk])
            b_, bk = bs.next()
            for q in range(8):
                p, pk = bp.next()
                mm(k, p[:], ef[:, m2, :], f_[:, q * 512:(q + 1) * 512], True, True, ['hy_ef', fk], [pk])
                cp(k, 'act' if q % 2 else 'dve', b_[:, q * 512:(q + 1) * 512], p[:], [pk], [bk])
            S.dma('sp', BdF[m2, 0:R, :], b_[:], reads=[bk])
        S.barrier()
    with ExitStack() as es:
        H = hy_common(k, es)
        fr = Ring(k, es, 'hy_fr', 2, [128, 2, 2048], BF16)
        hs = Ring(k, es, 'hy_hs', 2, [128, 2, 1024], F32)
        hp = Ring(k, es, 'hy_hp', 4, [128, 512], F32, psum=True)
        for o in range(2):
            for k1 in range(K1):
                f_, fk = fr.next()
                S.dma('sp', f_[:], BdF[:, 2 * k1:2 * k1 + 2, o * 2048:(o + 1) * 2048], writes=[fk])
                h_, hk = hs.next()
                for ch in range(2):
                    Fr = f_[:, 0, ch * 512:(ch + 1) * 512]
                    Fi = f_[:, 1, ch * 512:(ch + 1) * 512]
                    Br = f_[:, 0, 1024 + ch * 512:1024 + (ch + 1) * 512]
                    Bi = f_[:, 1, 1024 + ch * 512:1024 + (ch + 1) * 512]
                    p, pk = hp.next()
                    for i_, (mat, rhs) in enumerate((('cs', Fr), ('sn', Fi), ('cs', Br), ('sn', Bi))):
                        mm(k, p[:], H[mat][:], rhs, i_ == 0, i_ == 3, ['hy_' + mat, fk], [pk])
                    cp(k, 'dve', h_[:, 0, ch * 512:(ch + 1) * 512], p[:], [pk], [hk])
                    p, pk = hp.next()
                    for i_, (mat, rhs) in enumerate((('cs', Fi), ('nsn', Fr), ('ncs', Bi), ('sn', Br))):
                        mm(k, p[:], H[mat][:], rhs, i_ == 0, i_ == 3, ['hy_' + mat, fk], [pk])
                    cp(k, 'act', h_[:, 1, ch * 512:(ch + 1) * 512], p[:], [pk], [hk])
                S.dma('sp', Hspec[s][o][k1], h_[:], reads=[hk])
        S.barrier()


def hy_conv(k, s, o, Hspec, Bd, Gd, xbtm, hb, src_col, mul_col, dst, dst_col, first_B_from=None):
    nc, S = k.nc, k.S
    g = HY_SEGS[s]
    s0, L, M1, K1, R = g['s0'], g['L'], g['M1'], g['K1'], g['R']

    def strided(tm, c0):
        return tm[s0:s0 + L, c0:c0 + 1024].rearrange("(a b) c -> b a c", b=128)
    srcv = strided(xbtm if src_col >= 0 else dst, src_col if src_col >= 0 else 0)
    mulv = strided(xbtm, mul_col)
    dstv = strided(dst, dst_col)
    with ExitStack() as es:
        ef, ei = load_E(k, es, s)
        xr = Ring(k, es, 'hc_x', 3, [M1, 1024], BF16)
        bp = Ring(k, es, 'hc_bp', 4, [R, 512], F32, psum=True)
        bs = Ring(k, es, 'hc_bs', 3, [R, 1024], BF16)
        for m2 in range(128):
            x_, xk = xr.next()
            S.dma('sp', x_[:], srcv[m2], writes=[xk])
            b_, bk = bs.next()
            for ch in range(2):
                p, pk = bp.next()
                mm(k, p[:], ef[:, m2, :], x_[:, ch * 512:(ch + 1) * 512], True, True, ['hy_ef', xk], [pk])
                cp(k, 'act' if ch else 'dve', b_[:, ch * 512:(ch + 1) * 512], p[:], [pk], [bk])
            S.dma('sp', Bd[m2, 0:R, :], b_[:], reads=[bk])
        S.barrier()
    with ExitStack() as es:
        H = hy_common(k, es)
        br = Ring(k, es, 'hc_b', 2, [128, 2, 1024], BF16)
        hr = Ring(k, es, 'hc_h', 2, [128, 2, 1024], F32)
        t1 = sb(k, es, 'hc_t1', [128, 1024], F32)
        t2 = sb(k, es, 'hc_t2', [128, 1024], F32)
        t3 = sb(k, es, 'hc_t3', [128, 1024], F32)
        t4 = sb(k, es, 'hc_t4', [128, 1024], F32)
        Y = sb(k, es, 'hc_Y', [128, 2, 1024], BF16)
        gs = Ring(k, es, 'hc_gs', 2, [128, 2, 1024], BF16)
        xp = ps(k, es, 'hc_xp', [128, 2, 1024])
        gp = ps(k, es, 'hc_gp', [128, 2, 1024])
        for k1 in range(K1):
            b_, bk = br.next()
            S.dma('sp', b_[:], Bd[:, 2 * k1:2 * k1 + 2, :], writes=[bk])
            h_, hk = hr.next()
            S.dma('sp', h_[:], Hspec[s][o][k1], writes=[hk])
            for ch in range(2):
                cs_ = slice(ch * 512, (ch + 1) * 512)
                mm(k, xp[:, 0, cs_], H['cs'][:], b_[:, 0, cs_], True, False, ['hy_cs', bk], ['hc_xp'])
                mm(k, xp[:, 0, cs_], H['sn'][:], b_[:, 1, cs_], False, True, ['hy_sn', bk], ['hc_xp'])
                mm(k, xp[:, 1, cs_], H['cs'][:], b_[:, 1, cs_], True, False, ['hy_cs', bk], ['hc_xp'])
                mm(k, xp[:, 1, cs_], H['nsn'][:], b_[:, 0, cs_], False, True, ['hy_nsn', bk], ['hc_xp'])
            tt(k, 'dve', t1[:], xp[:, 0, :], h_[:, 0, :], ALU.mult, ['hc_xp', hk], ['hc_t1'])
            tt(k, 'dve', t2[:], xp[:, 1, :], h_[:, 1, :], ALU.mult, ['hc_xp', hk], ['hc_t2'])
            tt(k, 'dve', t3[:], xp[:, 0, :], h_[:, 1, :], ALU.mult, ['hc_xp', hk], ['hc_t3'])
            tt(k, 'dve', t4[:], xp[:, 1, :], h_[:, 0, :], ALU.mult, ['hc_xp', hk], ['hc_t4'])
            tt(k, 'pool', Y[:, 0, :], t1[:], t2[:], ALU.subtract, ['hc_t1', 'hc_t2'], ['hc_Y'])
            tt(k, 'pool', Y[:, 1, :], t3[:], t4[:], ALU.add, ['hc_t3', 'hc_t4'], ['hc_Y'])
            for ch in range(2):
                cs_ = slice(ch * 512, (ch + 1) * 512)
                mm(k, gp[:, 0, cs_], H['cs'][:], Y[:, 0, cs_], True, False, ['hy_cs', 'hc_Y'], ['hc_gp'])
                mm(k, gp[:, 0, cs_], H['nsn'][:], Y[:, 1, cs_], False, True, ['hy_nsn', 'hc_Y'], ['hc_gp'])
                mm(k, gp[:, 1, cs_], H['sn'][:], Y[:, 0, cs_], True, False, ['hy_sn', 'hc_Y'], ['hc_gp'])
                mm(k, gp[:, 1, cs_], H['cs'][:], Y[:, 1, cs_], False, True, ['hy_cs', 'hc_Y'], ['hc_gp'])
            g_, gk = gs.next()
            cp(k, 'act', g_[:, 0, :], gp[:, 0, :], ['hc_gp'], [gk])
            cp(k, 'act', g_[:, 1, :], gp[:, 1, :], ['hc_gp'], [gk])
            S.dma('sp', Gd[:, 2 * k1:2 * k1 + 2, :], g_[:], reads=[gk])
        S.barrier()
    with ExitStack() as es:
        ef, ei = load_E(k, es, s)
        bias = sb(k, es, 'hc_bias', [M1, 1024], F32)
        S.dma('sp', bias[:], k.din['hy_bias'][0, o].partition_broadcast(M1), writes=['hc_bias'])
        gr = Ring(k, es, 'hc_g', 3, [R, 1024], BF16)
        vr = Ring(k, es, 'hc_v', 3, [M1, 1024], BF16)
        mr = Ring(k, es, 'hc_m', 3, [M1, 1024], BF16)
        tr = Ring(k, es, 'hc_t', 2, [M1, 1024], F32)
        zr = Ring(k, es, 'hc_z', 3, [M1, 1024], BF16)
        yp = Ring(k, es, 'hc_yp', 3, [M1, 1024], F32, psum=True)
        for n2 in range(128):
            g_, gk = gr.next()
            S.dma('sp', g_[:], Gd[n2, 0:R, :], writes=[gk])
            v_, vk = vr.next()
            S.dma('sp', v_[:], srcv[n2], writes=[vk])
            m_, mk = mr.next()
            S.dma('sp', m_[:], mulv[n2], writes=[mk])
            p, pk = yp.next()
            for ch in range(2):
                mm(k, p[:, ch * 512:(ch + 1) * 512], ei[:, n2, :], g_[:, ch * 512:(ch + 1) * 512], True, True,
                   ['hy_ei', gk], [pk])
            t_, tk = tr.next()
            tt(k, 'pool', t_[:], v_[:], bias[:], ALU.mult, [vk, 'hc_bias'], [tk])
            tt(k, 'dve', t_[:], t_[:], p[:], ALU.add, [tk, pk], [tk])
            z_, zk = zr.next()
            tt(k, 'dve', z_[:], t_[:], m_[:], ALU.mult, [tk, mk], [zk])
            S.dma('sp', dstv[n2], z_[:], reads=[zk])
        S.barrier()


def hyena_mixer(k, l, j):
    nc, S = k.nc, k.S
    preT = scr(k, 'xbcT', [4096, T], BF16)
    cvT = scr(k, 'cvT', [4096, T], BF16)
    xbtm = scr(k, 'xbtm', [T, 3072], BF16)
    otm = scr(k, 'hy_otm', [T, 2048], BF16)
    BdF = scr(k, 'hy_BdF', [128, 66, 4096], BF16)
    Bd = scr(k, 'hy_Bd', [128, 66, 1024], BF16)
    Gd = scr(k, 'hy_Gd', [128, 66, 1024], BF16)
    Hspec = {s: [[scr(k, f'hy_H{s}_{o}_{k1}', [128, 2, 1024], F32) for k1 in range(HY_SEGS[s]['K1'])]
                 for o in range(2)] for s in HY_SEGS}
    W = k.din['hy_w_in'][j]
    with ExitStack() as es:
        uT = build_uT(k, es, l)
        st = Ring(k, es, 'hst', 4, [128, 512], BF16)
        specs = [('fm', i * 512, 512, fm_store(k, st, preT, 0, eng=('dve' if i % 2 else 'act'))) for i in range(6)]
        inproj(k, es, uT, W, specs)
        S.barrier()
    with ExitStack() as es:
        conv_fm(k, es, preT, cvT, 24, 3, 'hy_conv_w', 'hy_conv_b', IDENT)
        S.barrier()
    with ExitStack() as es:
        fm_to_tm(k, es, cvT, 0, 24, xbtm, 0)
        S.barrier()
    for s in HY_SEGS:
        hy_filters(k, j, s, Hspec, BdF)
        hy_conv(k, s, 0, Hspec, Bd, Gd, xbtm, None, 0, 1024, otm, 0)
        hy_conv(k, s, 1, Hspec, Bd, Gd, xbtm, None, -1, 2048, otm, 1024)
    with ExitStack() as es:
        ir = Ring(k, es, 'ho_i', 2, [128, 1024], BF16)
        ytR = Ring(k, es, 'ho_yt', 2, [128, 8, 128], BF16)
        tp = ps(k, es, 'ho_tp', [128, 8, 128], BF16)
        yv = k.yT[0:1024, :].rearrange("(dc p) t -> p dc t", p=128)
        for a in range(NTT):
            i_, ik = ir.next()
            S.dma('sp', i_[:], otm[a * 128:(a + 1) * 128, 1024:2048], writes=[ik])
            transpose_out(k, tp, i_, ik, ytR, 8, yv, a)
        S.barrier()

def mixer_copy(k, l):
    S = k.S
    with ExitStack() as es:
        uT = build_uT(k, es, l)
        yv = k.yT[0:1024, :].rearrange("(kc p) t -> p kc t", p=128)
        for (t0, n) in BLOCKS:
            S.dma('sp', yv[:, :, t0:t0 + n], uT[:, :, t0:t0 + n], reads=[('uT', t0)], writes=[('yT', t0)])
        S.barrier()


WEIGHT_NAMES = ['ada_w', 'ffn_w1', 'ffn_w2', 'gla_w_in', 'gla_w_a2', 'gla_b_a2', 'gla_norm', 'gla_w_out',
                'ssd_w_in', 'ssd_dt_bias', 'ssd_a_log', 'ssd_d', 'ssd_norm', 'ssd_w_out',
                'hy_w_in', 'hy_f_w1', 'hy_f_w2', 'hy_f_w3', 'hy_f_w4', 'hy_bias', 'hy_w_out']


def host_constants():
    c = {}
    c['c_ident'] = np.eye(128, dtype=np.float32)
    perm = np.arange(128)
    perm[64:] = 64 + (127 - perm[64:])
    P = np.zeros((128, 128), np.float32)
    P[perm, np.arange(128)] = 1.0
    c['c_psnake'] = P
    j = np.arange(128)[:, None]
    i = np.arange(128)[None, :]
    le = (j <= i).astype(np.float32)
    ge = (j >= i).astype(np.float32)
    gt = (j > i).astype(np.float32)
    lt = (j < i).astype(np.float32)
    c['c_mask'] = np.stack([le, ge])
    c['c_tri'] = np.stack([le, ge])
    c['c_rev'] = np.stack([gt, lt])
    c['c_mbias'] = np.stack([(le - 1.0) * 30000.0, (ge - 1.0) * 30000.0]).astype(np.float32)
    sel = np.zeros((32, 32, 128), np.float32)
    for h in range(32):
        sel[h, h, :] = 1.0
    c['c_selh'] = sel.reshape(32, 32 * 128)
    c.update(hyena_constants())
    return c


def build_program(shapes, cp, nlayers=DEPTH, mixers=None, debug_out=None):
    nc = bass.Bass("TRN2", target_bir_lowering=False)
    k = K()
    k.nc = nc
    k.cp = cp
    k.nlayers = nlayers
    k.din = {}
    for name, (shape, dt) in shapes.items():
        k.din[name] = nc.dram_tensor(name, list(shape), dt, kind="ExternalInput").ap()
    k.dout = nc.dram_tensor("out", [SEQ, D], F32, kind="ExternalOutput").ap()
    k.hT = nc.dram_tensor("hT", [D, T], F32).ap()
    k.yT = nc.dram_tensor("yT", [2048, T], BF16).ap()
    k.dbg = {}
    with ExitStack() as es:
        k.S = Sched(nc, es)
        k.colp = sb(k, es, 'colp', [128, cp.n], F32)
        k.MOD = sb(k, es, 'MOD', [128, DEPTH, 2, 48], F32)
        k.ident_f = sb(k, es, 'ident_f', [128, 128], F32)
        k.ident_b = sb(k, es, 'ident_b', [128, 128], BF16)
        k.ones_b = sb(k, es, 'ones_b', [128, 128], BF16)
        k.ones_f = sb(k, es, 'ones_f', [128, 128], F32)
        prologue(k)
        for l in range(nlayers):
            kind = (mixers[l] if mixers else ['gla', 'ssd', 'hy'][l % 3])
            j = l // 3
            if kind == 'copy':
                mixer_copy(k, l)
                post(k, l, k.din['gla_w_out'][0], 8)
            elif kind == 'gla':
                gla_mixer(k, l, j)
                post(k, l, k.din['gla_w_out'][j], 8)
            elif kind == 'ssd':
                ssd_mixer(k, l, j)
                post(k, l, k.din['ssd_w_out'][j], 16)
            else:
                hyena_mixer(k, l, j)
                post(k, l, k.din['hy_w_out'][j], 8)
        epilogue(k)
    k.ninst = k.S.ninst
    return nc, k


def host_inputs(inputs, b):
    m = {}
    m['x'] = np.ascontiguousarray(inputs['x'][b])
    m['ctx'] = np.ascontiguousarray(inputs['ctx'][b])
    m['cvec'] = np.ascontiguousarray(np.concatenate([col_layout(inputs['c'][b]), col_layout(inputs['c_ctx'])], axis=1))
    for n in WEIGHT_NAMES:
        m[n] = np.ascontiguousarray(np.asarray(inputs[n], np.float32))
    return m


def make_colpack(inputs):
    cp = ColPack()
    for l in range(DEPTH):
        cp.add(f'ada_b{l}', inputs['ada_b'][l])
        for s in range(2):
            cp.add(f'ln_g{l}_{s}', inputs['ln_g'][l, s])
            cp.add(f'ln_b{l}_{s}', inputs['ln_b'][l, s])
    cp.add('ssd_conv_b', inputs['ssd_conv_b'][0])
    for t in range(5):
        cp.add(f'ssd_conv_w{t}', inputs['ssd_conv_w'][0, t])
    cp.add('hy_conv_b', inputs['hy_conv_b'][0])
    for t in range(3):
        cp.add(f'hy_conv_w{t}', inputs['hy_conv_w'][0, t])
    return cp


_CACHE = {}


def kernel(**inputs):
    inputs = {n: np.asarray(v) for n, v in inputs.items()}
    cp = make_colpack(inputs)
    consts = host_constants()
    maps = []
    for b in range(8):
        m = host_inputs(inputs, b)
        m['colp'] = cp.array()
        m.update(consts)
        hyena_host_layout(inputs, m)
        maps.append(m)
    if 'nc' not in _CACHE:
        shapes = {n: (v.shape, F32) for n, v in maps[0].items()}
        _CACHE['nc'] = build_program(shapes, cp)[0]
    res = run_bass_kernel_spmd(_CACHE['nc'], maps, core_ids=list(range(8)))
    return np.stack([np.asarray(r['out'], np.float32) for r in res.results], axis=0)
```

```python
import numpy as np
import concourse.bass as bass
import concourse.mybir as mybir
from contextlib import ExitStack

F32 = mybir.dt.float32
BF16 = mybir.dt.bfloat16
ALU = mybir.AluOpType
AF = mybir.ActivationFunctionType
AX = mybir.AxisListType


class Sched:
    EPOCH = 20000
    NSLOT = 12

    def __init__(self, nc, es):
        self.nc = nc
        self.es = es
        self.eng = {'pe': nc.tensor, 'act': nc.scalar, 'dve': nc.vector,
                    'pool': nc.gpsimd, 'sp': nc.sync}
        self.esem = {}
        self.ecnt = {}
        self.nsem = 0
        for k in ['pe', 'act', 'dve', 'pool']:
            self._new_epoch(k)
        self.dsem = {q: [self._sem(f'd_{q}{i}') for i in range(self.NSLOT)]
                     for q in ['sp', 'pool', 'act']}
        self.dcnt = {q: [0] * self.NSLOT for q in self.dsem}
        self.dnext = {q: 0 for q in self.dsem}
        self.known = {e: {} for e in self.eng}
        self.lastw = {}
        self.readers = {}
        self.ninst = 0

    def _sem(self, name):
        self.nsem += 1
        return self.es.enter_context(self.nc.semaphore(f'{name}_{self.nsem}'))

    def _new_epoch(self, k):
        self.esem[k] = self._sem('e_' + k)
        self.ecnt[k] = 0

    def _deps(self, me, reads, writes):
        deps = []
        for k in reads:
            w = self.lastw.get(k)
            if w is not None:
                deps.append((w, False))
        for k in writes:
            w = self.lastw.get(k)
            if w is not None:
                deps.append((w, False))
            for r in self.readers.get(k, ()):
                deps.append((r, True))
        out = []
        for (ev, war) in deps:
            sem, val, owner = ev
            if owner == me:
                if me == 'pe':
                    continue
            out.append(ev)
        return out

    def _wait(self, me, evs):
        kn = self.known[me]
        e = self.eng[me]
        for (sem, val, owner) in evs:
            sid = id(sem)
            if kn.get(sid, 0) >= val:
                continue
            e.wait_ge(sem, val)
            kn[sid] = val

    def _record(self, ev, reads, writes):
        for k in writes:
            self.lastw[k] = ev
            self.readers[k] = []
        for k in reads:
            if k in writes:
                continue
            lst = self.readers.setdefault(k, [])
            if ev[2] is not None:
                lst[:] = [r for r in lst if r[2] != ev[2]]
            lst.append(ev)

    def op(self, me, fn, reads=(), writes=()):
        self._wait(me, self._deps(me, reads, writes))
        if self.ecnt[me] >= self.EPOCH:
            self._new_epoch(me)
        ins = fn()
        self.ecnt[me] += 1
        ins.then_inc(self.esem[me], 1)
        ev = (self.esem[me], self.ecnt[me], me)
        self._record(ev, reads, writes)
        self.ninst += 1
        return ev

    def dma(self, q, out, in_, reads=(), writes=(), **kw):
        s = self.dnext[q]
        self.dnext[q] = (s + 1) % self.NSLOT
        sem = self.dsem[q][s]
        evs = self._deps(q, reads, writes)
        if self.dcnt[q][s] > 0:
            evs.append((sem, self.dcnt[q][s], None))
        self._wait(q, evs)
        ins = self.eng[q].dma_start(out=out, in_=in_, **kw)
        self.dcnt[q][s] += 16
        ins.then_inc(sem, 16)
        ev = (sem, self.dcnt[q][s], None)
        self._record(ev, reads, writes)
        self.ninst += 1
        return ev

    def barrier(self):
        evs = []
        for k in self.esem:
            if self.ecnt[k] > 0:
                evs.append((self.esem[k], self.ecnt[k], k))
        for q in self.dsem:
            for s in range(self.NSLOT):
                if self.dcnt[q][s] > 0:
                    evs.append((self.dsem[q][s], self.dcnt[q][s], None))
        for me in self.eng:
            self._wait(me, evs)
        self.lastw = {}
        self.readers = {}
import math
from concourse.bass_utils import run_bass_kernel_spmd

D = 1024
SEQ = 4096
CTXL = 256
T = SEQ + CTXL
NTT = T // 128
DEPTH = 4
ALPHA = (2 * DEPTH) ** 0.25
LN_EPS = 1e-5
BLOCKS = [(i * 512, 512) for i in range(8)] + [(SEQ, CTXL)]
IDENT = AF.Identity


def seg_of(t0):
    return 0 if t0 < SEQ else 1


def col_layout(v):
    v = np.asarray(v, np.float32).reshape(-1, 128)
    return np.ascontiguousarray(v.T)


class ColPack:
    def __init__(self):
        self.cols = []
        self.off = {}
        self.n = 0

    def add(self, name, v):
        c = col_layout(v)
        self.off[name] = self.n
        self.cols.append(c)
        self.n += c.shape[1]

    def array(self):
        return np.ascontiguousarray(np.concatenate(self.cols, axis=1))


class K:
    pass


_UNIQ = [0]


def sb(k, es, name, shape, dt):
    _UNIQ[0] += 1
    return es.enter_context(k.nc.sbuf_tensor(f's{_UNIQ[0]}_{name}', list(shape), dt))


def ps(k, es, name, shape, dt=F32):
    _UNIQ[0] += 1
    return es.enter_context(k.nc.psum_tensor(f'p{_UNIQ[0]}_{name}', list(shape), dt))


class Ring:
    def __init__(self, k, es, name, n, shape, dt, psum=False):
        self.name = name
        self.n = n
        self.i = 0
        self.t = [(ps if psum else sb)(k, es, f'{name}{j}', shape, dt) for j in range(n)]

    def next(self):
        j = self.i % self.n
        self.i += 1
        return self.t[j], (self.name, j)


def act(k, out, in_, func, reads, writes, bias=None, scale=None, accum_out=None):
    kw = {}
    if bias is not None:
        kw['bias'] = bias
    if scale is not None:
        kw['scale'] = scale
    if accum_out is not None:
        kw['accum_out'] = accum_out
    return k.S.op('act', lambda: k.nc.scalar.activation(out=out, in_=in_, func=func, **kw), reads, writes)


def mm(k, out, lhsT, rhs, start, stop, reads, writes):
    return k.S.op('pe', lambda: k.nc.tensor.matmul(out, lhsT, rhs, start=start, stop=stop), reads, writes)


def tt(k, eng, out, in0, in1, op, reads, writes):
    e = k.nc.vector if eng == 'dve' else k.nc.gpsimd
    return k.S.op(eng, lambda: e.tensor_tensor(out=out, in0=in0, in1=in1, op=op), reads, writes)


def ts(k, eng, out, in0, s1, s2, op0, op1, reads, writes):
    e = k.nc.vector if eng == 'dve' else k.nc.gpsimd
    if s2 is None:
        return k.S.op(eng, lambda: e.tensor_scalar(out=out, in0=in0, scalar1=s1, scalar2=None, op0=op0), reads, writes)
    return k.S.op(eng, lambda: e.tensor_scalar(out=out, in0=in0, scalar1=s1, scalar2=s2, op0=op0, op1=op1), reads, writes)


def stt(k, eng, out, in0, scalar, in1, op0, op1, reads, writes):
    e = k.nc.vector if eng == 'dve' else k.nc.gpsimd
    return k.S.op(eng, lambda: e.scalar_tensor_tensor(out=out, in0=in0, scalar=scalar, in1=in1, op0=op0, op1=op1), reads, writes)


def cp(k, eng, out, in_, reads, writes):
    if eng == 'act':
        return k.S.op('act', lambda: k.nc.scalar.copy(out=out, in_=in_), reads, writes)
    e = k.nc.vector if eng == 'dve' else k.nc.gpsimd
    return k.S.op(eng, lambda: e.tensor_copy(out=out, in_=in_), reads, writes)


def memset(k, eng, ap, val, writes):
    e = k.nc.vector if eng == 'dve' else k.nc.gpsimd
    return k.S.op(eng, lambda: e.memset(ap, val), (), writes)


def transpose(k, out, in_, ident, reads, writes):
    return k.S.op('pe', lambda: k.nc.tensor.transpose(out, in_, ident), reads, writes)


def modcol(k, l, seg, m, dc):
    return k.MOD[:, l, seg, m * 8 + dc:m * 8 + dc + 1]


def prologue(k):
    nc, S = k.nc, k.S
    with ExitStack() as es:
        S.dma('sp', k.colp[:], k.din['colp'], writes=['colp'])
        S.dma('sp', k.ident_f[:], k.din['c_ident'], writes=['ident_f'])
        S.dma('pool', k.ident_b[:], k.din['c_ident'], writes=['ident_b'])
        memset(k, 'dve', k.ones_b[:], 1.0 / 1024.0, ['ones_b'])
        memset(k, 'dve', k.ones_f[:], 1.0, ['ones_f'])
        cv = sb(k, es, 'cv', [128, 16], F32)
        sT = sb(k, es, 'sT', [128, 8, 2], BF16)
        S.dma('sp', cv[:], k.din['cvec'], writes=['cv'])
        act(k, sT[:, :, 0], cv[:, 0:8], AF.Silu, ['cv'], ['sT'])
        act(k, sT[:, :, 1], cv[:, 8:16], AF.Silu, ['cv'], ['sT'])
        wr = Ring(k, es, 'adw', 3, [128, 8, 512], BF16)
        pr = Ring(k, es, 'adp', 2, [128, 4, 2], F32, psum=True)
        for l in range(k.nlayers):
            wv = k.din['ada_w'][l].rearrange("(kc p) f -> p kc f", p=128)
            for fg in range(12):
                w, wk = wr.next()
                S.dma('pool', w[:], wv[:, :, fg * 512:(fg + 1) * 512], writes=[wk])
                p, pk = pr.next()
                for fc in range(4):
                    for kc in range(8):
                        mm(k, p[:, fc, :], w[:, kc, fc * 128:(fc + 1) * 128], sT[:, kc, :],
                           kc == 0, kc == 7, [wk, 'sT'], [pk])
                boff = k.cp.off[f'ada_b{l}'] + fg * 4
                for seg in range(2):
                    tt(k, 'dve', k.MOD[:, l, seg, fg * 4:(fg + 1) * 4], p[:, :, seg],
                       k.colp[:, boff:boff + 4], ALU.add, [pk, 'colp'], ['MOD'])
            for seg in range(2):
                for m in (1, 4):
                    ts(k, 'dve', k.MOD[:, l, seg, m * 8:(m + 1) * 8], k.MOD[:, l, seg, m * 8:(m + 1) * 8],
                       1.0, None, ALU.add, None, ['MOD'], ['MOD'])
        psn = sb(k, es, 'psn', [128, 128], F32)
        S.dma('sp', psn[:], k.din['c_psnake'], writes=['psn'])
        xr = Ring(k, es, 'xin', 2, [128, 4, 1024], F32)
        st = Ring(k, es, 'xst', 2, [128, 8, 512], F32)
        pp = Ring(k, es, 'xps', 4, [128, 512], F32, psum=True)
        hv = k.hT.rearrange("(dc p) t -> p dc t", p=128)
        for (t0, n) in BLOCKS:
            nt = n // 128
            xi, xk = xr.next()
            if t0 < SEQ:
                src = k.din['x'][t0:t0 + n, :]
                perm, permk = psn, 'psn'
            else:
                src = k.din['ctx']
                perm, permk = k.ident_f, 'ident_f'
            S.dma('sp', xi[:, 0:nt, :], src.rearrange("(a p) d -> p a d", p=128), writes=[xk])
            so, sk = st.next()
            for dc in range(8):
                p, pk = pp.next()
                for a in range(nt):
                    mm(k, p[:, a * 128:(a + 1) * 128], xi[:, a, dc * 128:(dc + 1) * 128], perm[:],
                       True, True, [xk, permk], [pk])
                cp(k, 'act' if dc % 2 else 'dve', so[:, dc, 0:n], p[:, 0:n], [pk], [sk])
            S.dma('sp', hv[:, :, t0:t0 + n], so[:, :, 0:n], reads=[sk], writes=[('hT', t0)])
        S.barrier()


def out_w(k, l):
    kind = k.kinds[l]
    j = l // 3
    if kind == 'ssd':
        return k.din['ssd_w_out'][j], 16
    if kind == 'hy':
        return k.din['hy_w_out'][j], 8
    return k.din['gla_w_out'][j if kind == 'gla' else 0], 8


def weight_prep(k):
    S = k.S
    k.wb = {}
    with ExitStack() as es:
        wr = Ring(k, es, 'wpq', 3, [128, 8, 512], BF16)
        for l in range(k.nlayers):
            Wo_, KC = out_w(k, l)
            Wo = Wo_.rearrange("(kc p) f -> p kc f", p=128)
            W1 = k.din['ffn_w1'][l].rearrange("(kc p) f -> p kc f", p=128)
            W2 = k.din['ffn_w2'][l].rearrange("(fc p) d -> p fc d", p=128)
            srcs = []
            for fh in range(2):
                for kh in range(KC // 8):
                    srcs.append(Wo[:, kh * 8:(kh + 1) * 8, fh * 512:(fh + 1) * 512])
            n_out = len(srcs)
            for pw in range(8):
                srcs.append(W1[:, :, pw * 512:(pw + 1) * 512])
            for dh in range(2):
                for g in range(4):
                    srcs.append(W2[:, g * 8:(g + 1) * 8, dh * 512:(dh + 1) * 512])
            dst = k.nc.dram_tensor(f'wb{l}', [len(srcs), 128, 8, 512], BF16).ap()
            for i, src in enumerate(srcs):
                w, wk = wr.next()
                S.dma('pool', w[:], src, writes=[wk])
                S.dma('sp', dst[i], w[:], reads=[wk])
            k.wb[l] = (dst, n_out)
        S.barrier()

def epilogue(k):
    nc, S = k.nc, k.S
    with ExitStack() as es:
        psn = sb(k, es, 'psn2', [128, 128], F32)
        S.dma('sp', psn[:], k.din['c_psnake'], writes=['psn'])
        hr = Ring(k, es, 'eh', 2, [128, 8, 512], F32)
        tr = Ring(k, es, 'et', 2, [128, 1024], F32)
        orr = Ring(k, es, 'eo', 2, [128, 4, 1024], F32)
        p1 = Ring(k, es, 'ep1', 4, [128, 512], F32, psum=True)
        p2 = Ring(k, es, 'ep2', 4, [128, 512], F32, psum=True)
        hv = k.hT.rearrange("(dc p) t -> p dc t", p=128)
        for (t0, n) in BLOCKS:
            if t0 >= SEQ:
                continue
            h, hk = hr.next()
            S.dma('sp', h[:], hv[:, :, t0:t0 + n], reads=[('hT', t0)], writes=[hk])
            o, ok = orr.next()
            for a in range(4):
                tm, tk = tr.next()
                for half in range(2):
                    p, pk = p1.next()
                    for q in range(4):
                        dc = half * 4 + q
                        mm(k, p[:, q * 128:(q + 1) * 128], h[:, dc, a * 128:(a + 1) * 128], k.ident_f[:],
                           True, True, [hk, 'ident_f'], [pk])
                    cp(k, 'act' if half else 'dve', tm[:, half * 512:(half + 1) * 512], p[:], [pk], [tk])
                for half in range(2):
                    p, pk = p2.next()
                    mm(k, p[:], psn[:], tm[:, half * 512:(half + 1) * 512], True, True, ['psn', tk], [pk])
                    cp(k, 'act' if half else 'dve', o[:, a, half * 512:(half + 1) * 512], p[:], [pk], [ok])
            S.dma('sp', k.dout[t0:t0 + n, :].rearrange("(a p) d -> p a d", p=128), o[:], reads=[ok], writes=[('out', t0)])
        S.barrier()


def build_uT(k, es, l):
    S = k.S
    uT = sb(k, es, 'uT', [128, 8, T], BF16)
    hr = Ring(k, es, 'uh', 2, [128, 8, 512], F32)
    hv = k.hT.rearrange("(dc p) t -> p dc t", p=128)
    for (t0, n) in BLOCKS:
        seg = seg_of(t0)
        h, hk = hr.next()
        S.dma('sp', h[:, :, 0:n], hv[:, :, t0:t0 + n], reads=[('hT', t0)], writes=[hk])
        for dc in range(8):
            act(k, uT[:, dc, t0:t0 + n], h[:, dc, 0:n], IDENT, [hk, 'MOD'], [('uT', t0)],
                bias=modcol(k, l, seg, 0, dc), scale=modcol(k, l, seg, 1, dc))
    return uT


def inproj(k, es, uT, W, specs):
    S = k.S
    Wv = W.rearrange("(kc p) f -> p kc f", p=128)
    wr = Ring(k, es, 'ipw', 3, [128, 8, 512], BF16)
    pr = Ring(k, es, 'ipp', 4, [128, 512], F32, psum=True)
    ukeys = [('uT', t0) for (t0, n) in BLOCKS]
    for (mode, f0, fsz, handler) in specs:
        w, wk = wr.next()
        S.dma('pool', w[:, :, 0:fsz], Wv[:, :, f0:f0 + fsz], writes=[wk])
        if mode == 'tm':
            for a in range(NTT):
                p, pk = pr.next()
                uk = ('uT', BLOCKS[min(a // 4, 8)][0])
                for kc in range(8):
                    mm(k, p[:, 0:fsz], uT[:, kc, a * 128:(a + 1) * 128], w[:, kc, 0:fsz], kc == 0, kc == 7,
                       [wk, uk], [pk])
                handler(p[:, 0:fsz], pk, a)
        else:
            nfc = (fsz + 127) // 128
            for fc in range(nfc):
                fw = min(128, fsz - fc * 128)
                for (t0, n) in BLOCKS:
                    p, pk = pr.next()
                    for kc in range(8):
                        mm(k, p[0:fw, 0:n], w[:, kc, fc * 128:fc * 128 + fw], uT[:, kc, t0:t0 + n], kc == 0, kc == 7,
                           [wk, ('uT', t0)], [pk])
                    handler(p[0:fw, 0:n], pk, f0 + fc * 128, fw, t0, n)


def ln_block(k, L, r, rk, n, gcol, bcol, out, outk):
    S = k.S
    rb, sq, pst, mean, msq, rstd = L['rb'], L['sq'], L['pst'], L['mean'], L['msq'], L['rstd']
    act(k, rb[:, :, 0:n], r[:, :, 0:n], IDENT, [rk], ['ln_rb'])
    act(k, sq[:, :, 0:n], r[:, :, 0:n], AF.Square, [rk], ['ln_sq'])
    p1, p1k = pst.next()
    p2, p2k = pst.next()
    for dc in range(8):
        mm(k, p1[:, 0:n], k.ones_b[:], rb[:, dc, 0:n], dc == 0, dc == 7, ['ones_b', 'ln_rb'], [p1k])
    for dc in range(8):
        mm(k, p2[:, 0:n], k.ones_b[:], sq[:, dc, 0:n], dc == 0, dc == 7, ['ones_b', 'ln_sq'], [p2k])
    cp(k, 'act', mean[:, 0:n], p1[:, 0:n], [p1k], ['ln_mean'])
    tt(k, 'dve', msq[:, 0:n], mean[:, 0:n], mean[:, 0:n], ALU.mult, ['ln_mean'], ['ln_msq'])
    tt(k, 'dve', msq[:, 0:n], p2[:, 0:n], msq[:, 0:n], ALU.subtract, [p2k, 'ln_msq'], ['ln_msq'])
    ts(k, 'dve', msq[:, 0:n], msq[:, 0:n], LN_EPS, None, ALU.add, None, ['ln_msq'], ['ln_msq'])
    act(k, msq[:, 0:n], msq[:, 0:n], AF.Ln, ['ln_msq'], ['ln_msq'])
    act(k, rstd[:, 0:n], msq[:, 0:n], AF.Exp, ['ln_msq'], ['ln_rstd'], scale=-0.5)
    mb = mean[:, 0:n].unsqueeze(1).to_broadcast([128, 8, n])
    rsb = rstd[:, 0:n].unsqueeze(1).to_broadcast([128, 8, n])
    tt(k, 'dve', r[:, :, 0:n], r[:, :, 0:n], mb, ALU.subtract, [rk, 'ln_mean'], [rk])
    tt(k, 'pool', r[:, :, 0:n], r[:, :, 0:n], rsb, ALU.mult, [rk, 'ln_rstd'], [rk])
    for dc in range(8):
        act(k, out[:, dc, 0:n], r[:, dc, 0:n], IDENT, [rk, 'colp'], [outk],
            bias=k.colp[:, bcol + dc:bcol + dc + 1], scale=k.colp[:, gcol + dc:gcol + dc + 1])


def post(k, l, W_out, KC):
    nc, S = k.nc, k.S
    with ExitStack() as es:
        hv = k.hT.rearrange("(dc p) t -> p dc t", p=128)
        yv = k.yT[0:KC * 128, :].rearrange("(kc p) t -> p kc t", p=128)
        wbl, n_out = k.wb[l]
        wr = Ring(k, es, 'pw', 4, [128, 8, 512], BF16)
        pr = Ring(k, es, 'pp', 6, [128, 512], F32, psum=True)
        L = dict(rb=sb(k, es, 'ln_rb', [128, 8, 512], BF16), sq=sb(k, es, 'ln_sq', [128, 8, 512], BF16),
                 pst=Ring(k, es, 'lnp', 2, [128, 512], F32, psum=True),
                 mean=sb(k, es, 'ln_mean', [128, 512], F32), msq=sb(k, es, 'ln_msq', [128, 512], F32),
                 rstd=sb(k, es, 'ln_rstd', [128, 512], F32))
        hb = sb(k, es, 'p_h', [128, 8, 512], F32)
        yb = sb(k, es, 'p_y', [128, KC, 512], BF16)
        r = sb(k, es, 'p_r', [128, 8, 512], F32)
        h1 = sb(k, es, 'p_h1', [128, 8, 512], F32)
        u2 = sb(k, es, 'p_u2', [128, 8, 512], BF16)
        hh = sb(k, es, 'p_hh', [128, 32, 512], BF16)
        rl = Ring(k, es, 'p_rl', 3, [128, 512], F32)
        g0, b0 = k.cp.off[f'ln_g{l}_0'], k.cp.off[f'ln_b{l}_0']
        g1, b1 = k.cp.off[f'ln_g{l}_1'], k.cp.off[f'ln_b{l}_1']
        for (t0, n) in BLOCKS:
            seg = seg_of(t0)
            S.dma('sp', hb[:, :, 0:n], hv[:, :, t0:t0 + n], reads=[('hT', t0)], writes=['p_h'])
            S.dma('sp', yb[:, :, 0:n], yv[:, :, t0:t0 + n], reads=[('yT', t0)], writes=['p_y'])
            S.op('act', lambda: nc.scalar.mul(out=hb[:, :, 0:n], in_=hb[:, :, 0:n], mul=ALPHA), ['p_h'], ['p_h'])
            for fh in range(2):
                pieces = []
                for kh in range(KC // 8):
                    w, wk = wr.next()
                    S.dma('sp', w[:], wbl[fh * (KC // 8) + kh], writes=[wk])
                    pieces.append((w, wk))
                for q in range(4):
                    dco = fh * 4 + q
                    p, pk = pr.next()
                    for kc in range(KC):
                        w, wk = pieces[kc // 8]
                        mm(k, p[:, 0:n], w[:, kc % 8, q * 128:(q + 1) * 128], yb[:, kc, 0:n], kc == 0, kc == KC - 1,
                           [wk, 'p_y'], [pk])
                    stt(k, 'dve', r[:, dco, 0:n], p[:, 0:n], modcol(k, l, seg, 2, dco), hb[:, dco, 0:n],
                        ALU.mult, ALU.add, [pk, 'MOD', 'p_h'], ['p_r'])
            ln_block(k, L, r, 'p_r', n, g0, b0, h1, 'p_h1')
            for dc in range(8):
                act(k, u2[:, dc, 0:n], h1[:, dc, 0:n], IDENT, ['p_h1', 'MOD'], ['p_u2'],
                    bias=modcol(k, l, seg, 3, dc), scale=modcol(k, l, seg, 4, dc))
            S.op('act', lambda: nc.scalar.mul(out=h1[:, :, 0:n], in_=h1[:, :, 0:n], mul=ALPHA), ['p_h1'], ['p_h1'])
            for pw in range(8):
                w, wk = wr.next()
                S.dma('sp', w[:], wbl[n_out + pw], writes=[wk])
                for q in range(4):
                    fc = pw * 4 + q
                    p, pk = pr.next()
                    for kc in range(8):
                        mm(k, p[:, 0:n], w[:, kc, q * 128:(q + 1) * 128], u2[:, kc, 0:n], kc == 0, kc == 7,
                           [wk, 'p_u2'], [pk])
                    rt, rtk = rl.next()
                    act(k, rt[:, 0:n], p[:, 0:n], AF.Relu, [pk], [rtk])
                    tt(k, 'pool' if q % 2 else 'dve', hh[:, fc, 0:n], rt[:, 0:n], rt[:, 0:n], ALU.mult, [rtk], [('p_hh', fc)])
            for dh in range(2):
                accs = [pr.next() for _ in range(4)]
                for g in range(4):
                    w, wk = wr.next()
                    S.dma('sp', w[:], wbl[n_out + 8 + dh * 4 + g], writes=[wk])
                    for q in range(4):
                        p, pk = accs[q]
                        for fcl in range(8):
                            fc = g * 8 + fcl
                            mm(k, p[:, 0:n], w[:, fcl, q * 128:(q + 1) * 128], hh[:, fc, 0:n],
                               fc == 0, fc == 31, [wk, ('p_hh', fc)], [pk])
                for q in range(4):
                    dco = dh * 4 + q
                    p, pk = accs[q]
                    stt(k, 'dve', r[:, dco, 0:n], p[:, 0:n], modcol(k, l, seg, 5, dco), h1[:, dco, 0:n],
                        ALU.mult, ALU.add, [pk, 'MOD', 'p_h1'], ['p_r'])
            ln_block(k, L, r, 'p_r', n, g1, b1, hb, 'p_h')
            S.dma('sp', hv[:, :, t0:t0 + n], hb[:, :, 0:n], reads=['p_h'], writes=[('hT', t0)])
        S.barrier()

RMS_EPS = 1e-6


def scr(k, name, shape, dt):
    if name not in k.dbg:
        k.dbg[name] = k.nc.dram_tensor('scr_' + name, list(shape), dt).ap()
    return k.dbg[name]


def rstd_from_ss(k, ss, nfeat, key):
    ts(k, 'dve', ss, ss, 1.0 / nfeat, RMS_EPS, ALU.mult, ALU.add, [key], [key])
    act(k, ss, ss, AF.Ln, [key], [key])
    act(k, ss, ss, AF.Exp, [key], [key], scale=-0.5)


def fm_store(k, st, dst, base, scale=None, eng='dve'):
    def h(p, pk, f_lo, fw, t0, n):
        s, sk = st.next()
        if scale is not None:
            act(k, s[0:fw, 0:n], p, IDENT, [pk], [sk], scale=scale)
        else:
            cp(k, eng, s[0:fw, 0:n], p, [pk], [sk])
        k.S.dma('sp', dst[f_lo - base:f_lo - base + fw, t0:t0 + n], s[0:fw, 0:n], reads=[sk])
    return h


def tm_store(k, st, dst, c0, func=None):
    cnt = [0]

    def h(p, pk, a):
        s, sk = st.next()
        w = p.shape[1]
        if func is not None:
            act(k, s[:, 0:w], p, func, [pk], [sk])
        else:
            cnt[0] += 1
            cp(k, 'dve' if cnt[0] % 2 else 'act', s[:, 0:w], p, [pk], [sk])
        k.S.dma('sp', dst[a * 128:(a + 1) * 128, c0:c0 + w], s[:, 0:w], reads=[sk])
    return h


def transpose_out(k, tp, yb, ybk, ytile_ring, nq, yT_view, c):
    for q0 in range(0, nq, 8):
        for q in range(8):
            transpose(k, tp[:, q, :], yb[:, (q0 + q) * 128:(q0 + q + 1) * 128], k.ident_b[:], [ybk, 'ident_b'], ['tp'])
        yt, ytk = ytile_ring.next()
        cp(k, 'act', yt[:], tp[:], ['tp'], [ytk])
        k.S.dma('sp', yT_view[:, q0:q0 + 8, c * 128:(c + 1) * 128], yt[:], reads=[ytk])


def gla_mixer(k, l, j):
    nc, S = k.nc, k.S
    qT = scr(k, 'qT', [512, T], BF16)
    kT = scr(k, 'kT', [512, T], BF16)
    aT = scr(k, 'aT', [32, T], BF16)
    ktm = scr(k, 'ktm', [T, 512], BF16)
    vtm = scr(k, 'vtm', [T, 1024], BF16)
    gtm = scr(k, 'gtm', [T, 1024], BF16)
    oacc = scr(k, 'oacc', [T, 2048], F32)
    W = k.din['gla_w_in'][j]
    with ExitStack() as es:
        uT = build_uT(k, es, l)
        st = Ring(k, es, 'gst', 4, [128, 512], BF16)
        specs = [
            ('fm', 0, 512, fm_store(k, st, qT, 0, scale=128.0 ** -0.5)),
            ('fm', 512, 512, fm_store(k, st, kT, 512)),
            ('fm', 3072, 32, fm_store(k, st, aT, 3072)),
            ('tm', 512, 512, tm_store(k, st, ktm, 0)),
            ('tm', 1024, 512, tm_store(k, st, vtm, 0)),
            ('tm', 1536, 512, tm_store(k, st, vtm, 512)),
            ('tm', 2048, 512, tm_store(k, st, gtm, 0, AF.Silu)),
            ('tm', 2560, 512, tm_store(k, st, gtm, 512, AF.Silu)),
        ]
        inproj(k, es, uT, W, specs)
        S.barrier()
    with ExitStack() as es:
        ctri = sb(k, es, 'gtri', [128, 2, 128], F32)
        crev = sb(k, es, 'grev', [128, 2, 128], F32)
        cmask = sb(k, es, 'gmask', [128, 2, 128], F32)
        gnorm = sb(k, es, 'gnorm', [128, 256], F32)
        w2a = sb(k, es, 'w2a', [33, 2, 512], BF16)
        for d in range(2):
            S.dma('sp', ctri[:, d, :], k.din['c_tri'][d], writes=['gtri'])
            S.dma('sp', crev[:, d, :], k.din['c_rev'][d], writes=['grev'])
            S.dma('sp', cmask[:, d, :], k.din['c_mask'][d], writes=['gmask'])
        S.op('act', lambda: nc.scalar.mul(out=ctri[:], in_=ctri[:], mul=-1.0 / 16.0), ['gtri'], ['gtri'])
        S.op('act', lambda: nc.scalar.mul(out=crev[:], in_=crev[:], mul=-1.0 / 16.0), ['grev'], ['grev'])
        S.dma('sp', gnorm[:], k.din['gla_norm'][j].partition_broadcast(128), writes=['gnorm'])
        memset(k, 'dve', w2a[:], 0.0, ['w2a'])
        for z in range(2):
            S.dma('pool', w2a[z * 16:(z + 1) * 16, z, :], k.din['gla_w_a2'][j, z], writes=['w2a'])
            S.dma('pool', w2a[32:33, z, :], k.din['gla_b_a2'][j, z:z + 1, :], writes=['w2a'])
        aR = Ring(k, es, 'g_a', 2, [33, 128], BF16)
        for t_, tk_ in [aR.next(), aR.next()]:
            memset(k, 'dve', t_[:], 1.0, [tk_])
        qR = Ring(k, es, 'g_q', 2, [128, 4, 128], BF16)
        kR = Ring(k, es, 'g_k', 2, [128, 4, 128], BF16)
        kmR = Ring(k, es, 'g_km', 2, [128, 512], BF16)
        vR = Ring(k, es, 'g_v', 2, [128, 1024], BF16)
        gR = Ring(k, es, 'g_g', 2, [128, 1024], BF16)
        oaR = Ring(k, es, 'g_oa', 2, [128, 1024], F32)
        e_sb = sb(k, es, 'g_e', [128, 512], F32)
        sp_sb = sb(k, es, 'g_sp', [128, 512], F32)
        Eq = sb(k, es, 'g_Eq', [128, 512], F32)
        Ek = sb(k, es, 'g_Ek', [128, 512], F32)
        Er = sb(k, es, 'g_Er', [128, 512], F32)
        qt = sb(k, es, 'g_qt', [128, 512], BF16)
        kt = sb(k, es, 'g_kt', [128, 512], BF16)
        kp = sb(k, es, 'g_kp', [128, 512], BF16)
        AT = sb(k, es, 'g_AT', [128, 4, 128], BF16)
        Sst = sb(k, es, 'g_S', [128, 4, 256], F32)
        Sbf = sb(k, es, 'g_Sb', [128, 4, 256], BF16)
        osR = Ring(k, es, 'g_os', 2, [128, 4, 256], F32)
        ss = sb(k, es, 'g_ss', [128, 4], F32)
        junk = sb(k, es, 'g_junk', [128, 256], F32)
        yb = sb(k, es, 'g_yb', [128, 1024], BF16)
        ytR = Ring(k, es, 'g_yt', 2, [128, 8, 128], BF16)
        lg_ps = ps(k, es, 'g_lg', [128, 512])
        cum_ps = ps(k, es, 'g_cum', [128, 512])
        sc_ps = ps(k, es, 'g_sc', [128, 512])
        o_ps = ps(k, es, 'g_o', [128, 1024])
        st_ps = ps(k, es, 'g_stp', [128, 1024])
        tp = ps(k, es, 'g_tp', [128, 8, 128], BF16)
        qv = qT.rearrange("(h p) t -> p h t", p=128)
        kv = kT.rearrange("(h p) t -> p h t", p=128)
        yv = k.yT[0:1024, :].rearrange("(dc p) t -> p dc t", p=128)
        for d in range(2):
            memset(k, 'dve', Sst[:], 0.0, ['g_S'])
            memset(k, 'dve', Sbf[:], 0.0, ['g_Sb'])
            order = [32, 33] + list(range(32)) if d == 0 else [33, 32] + list(range(31, -1, -1))
            for c in order:
                tsl = slice(c * 128, (c + 1) * 128)
                a_t, ak = aR.next()
                S.dma('sp', a_t[0:32, :], aT[:, tsl], writes=[ak])
                q_t, qk = qR.next()
                S.dma('sp', q_t[:], qv[:, :, tsl], writes=[qk])
                k_t, kk = kR.next()
                S.dma('sp', k_t[:], kv[:, :, tsl], writes=[kk])
                km, kmk = kmR.next()
                S.dma('sp', km[:], ktm[tsl, :], writes=[kmk])
                v_t, vk = vR.next()
                S.dma('sp', v_t[:], vtm[tsl, :], writes=[vk])
                if d == 1:
                    g_t, gk = gR.next()
                    S.dma('sp', g_t[:], gtm[tsl, :], writes=[gk])
                    oa, oak = oaR.next()
                    S.dma('sp', oa[:], oacc[tsl, 0:1024], writes=[oak])
                mm(k, lg_ps[:], a_t[:], w2a[:, d, :], True, True, [ak, 'w2a'], ['g_lg'])
                act(k, e_sb[:], lg_ps[:], AF.Exp, ['g_lg'], ['g_e'], scale=-1.0)
                act(k, sp_sb[:], e_sb[:], AF.Ln, ['g_e'], ['g_sp'], bias=1.0)
                for h in range(4):
                    mm(k, cum_ps[:, h * 128:(h + 1) * 128], sp_sb[:, h * 128:(h + 1) * 128], ctri[:, d, :],
                       True, True, ['g_sp', 'gtri'], ['g_cum'])
                mm(k, lg_ps[:], crev[:, d, :], sp_sb[:], True, True, ['g_sp', 'grev'], ['g_lg'])
                act(k, Eq[:], cum_ps[:], AF.Exp, ['g_cum'], ['g_Eq'])
                act(k, Ek[:], cum_ps[:], AF.Exp, ['g_cum'], ['g_Ek'], scale=-1.0)
                act(k, Er[:], lg_ps[:], AF.Exp, ['g_lg'], ['g_Er'])
                tt(k, 'dve', qt[:], q_t[:].rearrange("p h t -> p (h t)"), Eq[:], ALU.mult, [qk, 'g_Eq'], ['g_qt'])
                tt(k, 'pool', kt[:], k_t[:].rearrange("p h t -> p (h t)"), Ek[:], ALU.mult, [kk, 'g_Ek'], ['g_kt'])
                tt(k, 'dve', kp[:], km[:], Er[:], ALU.mult, [kmk, 'g_Er'], ['g_kp'])
                for h in range(4):
                    mm(k, sc_ps[:, h * 128:(h + 1) * 128], kt[:, h * 128:(h + 1) * 128], qt[:, h * 128:(h + 1) * 128],
                       True, True, ['g_kt', 'g_qt'], ['g_sc'])
                tt(k, 'dve', AT[:], sc_ps[:].rearrange("p (h t) -> p h t", h=4),
                   cmask[:, d, :].unsqueeze(1).to_broadcast([128, 4, 128]), ALU.mult, ['g_sc', 'gmask'], ['g_AT'])
                for h in range(4):
                    mm(k, o_ps[:, h * 256:(h + 1) * 256], AT[:, h, :], v_t[:, h * 256:(h + 1) * 256], True, False,
                       ['g_AT', vk], ['g_o'])
                    mm(k, o_ps[:, h * 256:(h + 1) * 256], qt[:, h * 128:(h + 1) * 128], Sbf[:, h, :], False, True,
                       ['g_qt', 'g_Sb'], ['g_o'])
                for h in range(4):
                    mm(k, st_ps[:, h * 256:(h + 1) * 256], kp[:, h * 128:(h + 1) * 128], v_t[:, h * 256:(h + 1) * 256],
                       True, True, ['g_kp', vk], ['g_stp'])
                last = 127 if d == 0 else 0
                for h in range(4):
                    col = h * 128 + last
                    stt(k, 'dve', Sst[:, h, :], Sst[:, h, :], Eq[:, col:col + 1], st_ps[:, h * 256:(h + 1) * 256],
                        ALU.mult, ALU.add, ['g_S', 'g_Eq', 'g_stp'], ['g_S'])
                cp(k, 'act', Sbf[:], Sst[:], ['g_S'], ['g_Sb'])
                os_, osk = osR.next()
                if d == 0:
                    cp(k, 'act', os_[:].rearrange("p h v -> p (h v)"), o_ps[:], ['g_o'], [osk])
                    S.dma('sp', oacc[tsl, 0:1024], os_[:].rearrange("p h v -> p (h v)"), reads=[osk])
                else:
                    tt(k, 'dve', os_[:].rearrange("p h v -> p (h v)"), o_ps[:], oa[:], ALU.add, ['g_o', oak], [osk])
                    memset(k, 'dve', ss[:], 0.0, ['g_ss'])
                    for h in range(4):
                        act(k, junk[:], os_[:, h, :], AF.Square, [osk], ['g_junk', 'g_ss'], accum_out=ss[:, h:h + 1])
                    rstd_from_ss(k, ss[:], 256.0, 'g_ss')
                    tt(k, 'dve', os_[:], os_[:], ss[:].unsqueeze(2).to_broadcast([128, 4, 256]), ALU.mult,
                       [osk, 'g_ss'], [osk])
                    tt(k, 'pool', os_[:], os_[:], gnorm[:].unsqueeze(1).to_broadcast([128, 4, 256]), ALU.mult,
                       [osk, 'gnorm'], [osk])
                    tt(k, 'dve', yb[:], os_[:].rearrange("p h v -> p (h v)"), g_t[:], ALU.mult, [osk, gk], ['g_yb'])
                    transpose_out(k, tp, yb, 'g_yb', ytR, 8, yv, c)
            S.barrier()


def conv_fm(k, es, src, dst, nchunks, ntap, wname, bname, func, c_off=0):
    S = k.S
    pad = ntap // 2
    xr = Ring(k, es, 'cv_x', 2, [128, SEQ + 2 * pad], BF16)
    dg = Ring(k, es, 'cv_d', 2, [128, ntap, 128], BF16)
    sr = Ring(k, es, 'cv_s', 3, [128, 512], BF16)
    pr = Ring(k, es, 'cv_p', 3, [128, 512], F32, psum=True)
    boff = k.cp.off[bname]
    for cc in range(nchunks):
        d, dk = dg.next()
        for tap in range(ntap):
            woff = k.cp.off[f'{wname}{tap}'] + c_off + cc
            ts(k, 'dve', d[:, tap, :], k.ident_f[:], k.colp[:, woff:woff + 1], None, ALU.mult, None,
               ['ident_f', 'colp'], [dk])
        for (s0, slen) in ((0, SEQ), (SEQ, CTXL)):
            x, xk = xr.next()
            memset(k, 'dve', x[:, 0:pad], 0.0, [xk])
            memset(k, 'dve', x[:, pad + slen:2 * pad + slen], 0.0, [xk])
            S.dma('sp', x[:, pad:pad + slen], src[cc * 128:(cc + 1) * 128, s0:s0 + slen], writes=[xk])
            for t0 in range(0, slen, 512):
                n = min(512, slen - t0)
                p, pk = pr.next()
                for tap in range(ntap):
                    mm(k, p[:, 0:n], d[:, tap, :], x[:, t0 + tap:t0 + tap + n], tap == 0, tap == ntap - 1, [dk, xk], [pk])
                s, sk = sr.next()
                act(k, s[:, 0:n], p[:, 0:n], func, [pk, 'colp'], [sk],
                    bias=k.colp[:, boff + c_off + cc:boff + c_off + cc + 1])
                S.dma('sp', dst[cc * 128:(cc + 1) * 128, s0 + t0:s0 + t0 + n], s[:, 0:n], reads=[sk])


def fm_to_tm(k, es, src, c0, nch, dst, d0):
    S = k.S
    ir = Ring(k, es, 'f2t_i', 2, [128, 8, 128], BF16)
    orr = Ring(k, es, 'f2t_o', 2, [128, 8, 128], BF16)
    pr = Ring(k, es, 'f2t_p', 2, [128, 8, 128], BF16, psum=True)
    sv = src.rearrange("(c p) t -> p c t", p=128)
    for a in range(NTT):
        for q0 in range(0, nch, 8):
            i_, ik = ir.next()
            S.dma('sp', i_[:], sv[:, c0 + q0:c0 + q0 + 8, a * 128:(a + 1) * 128], writes=[ik])
            p, pk = pr.next()
            for q in range(8):
                transpose(k, p[:, q, :], i_[:, q, :], k.ident_b[:], [ik, 'ident_b'], [pk])
            o, ok = orr.next()
            cp(k, 'act' if (q0 // 8) % 2 else 'dve', o[:], p[:], [pk], [ok])
            S.dma('sp', dst[a * 128:(a + 1) * 128, d0 + q0 * 128:d0 + (q0 + 8) * 128],
                  o[:].rearrange("p q t -> p (q t)"), reads=[ok])


def ssd_mixer(k, l, j):
    nc, S = k.nc, k.S
    ztm = scr(k, 'ztm', [T, 2048], BF16)
    xbcT = scr(k, 'xbcT', [4096, T], BF16)
    cvT = scr(k, 'cvT', [4096, T], BF16)
    xbtm = scr(k, 'xbtm', [T, 3072], BF16)
    dtm = scr(k, 'dtm', [T, 64], F32)
    dtam = scr(k, 'dtam', [T, 64], F32)
    yacc = scr(k, 'oacc', [T, 2048], F32)
    W = k.din['ssd_w_in'][j]
    with ExitStack() as es:
        uT = build_uT(k, es, l)
        st = Ring(k, es, 'sst', 4, [128, 512], BF16)
        dtb = sb(k, es, 's_dtb', [128, 64], F32)
        abc = sb(k, es, 's_abc', [128, 64], F32)
        S.dma('sp', dtb[:], k.din['ssd_dt_bias'][j].rearrange("a h -> (a h)").partition_broadcast(128), writes=['s_dtb'])
        S.dma('sp', abc[:], k.din['ssd_a_log'][j].rearrange("a h -> (a h)").partition_broadcast(128), writes=['s_abc'])
        act(k, abc[:], abc[:], AF.Exp, ['s_abc'], ['s_abc'])
        ts(k, 'dve', abc[:], abc[:], -1.0, None, ALU.mult, None, ['s_abc'], ['s_abc'])
        dr = Ring(k, es, 's_dt', 2, [128, 2, 64], F32)

        def h_dt(p, pk, a):
            d, dk = dr.next()
            tt(k, 'dve', d[:, 0, :], p, dtb[:], ALU.add, [pk, 's_dtb'], [dk])
            act(k, d[:, 0, :], d[:, 0, :], AF.Exp, [dk], [dk])
            act(k, d[:, 0, :], d[:, 0, :], AF.Ln, [dk], [dk], bias=1.0)
            tt(k, 'dve', d[:, 1, :], d[:, 0, :], abc[:], ALU.mult, [dk, 's_abc'], [dk])
            S.dma('sp', dtm[a * 128:(a + 1) * 128, :], d[:, 0, :], reads=[dk])
            S.dma('sp', dtam[a * 128:(a + 1) * 128, :], d[:, 1, :], reads=[dk])
        specs = [('tm', i * 512, 512, tm_store(k, st, ztm, i * 512, AF.Silu)) for i in range(4)]
        specs += [('fm', 2048 + i * 512, 512, fm_store(k, st, xbcT, 2048, eng=('dve' if i % 2 else 'act'))) for i in range(8)]
        specs += [('tm', 6144, 64, h_dt)]
        inproj(k, es, uT, W, specs)
        S.barrier()
    with ExitStack() as es:
        conv_fm(k, es, xbcT, cvT, 32, 5, 'ssd_conv_w', 'ssd_conv_b', AF.Silu)
        S.barrier()
    with ExitStack() as es:
        fm_to_tm(k, es, cvT, 0, 24, xbtm, 0)
        S.barrier()
    with ExitStack() as es:
        stri = sb(k, es, 'stri', [128, 2, 128], F32)
        srev = sb(k, es, 'srev', [128, 2, 128], F32)
        mbias = sb(k, es, 'smb', [128, 2, 128], F32)
        selh = sb(k, es, 'selh', [32, 32 * 128], F32)
        for d in range(2):
            S.dma('sp', stri[:, d, :], k.din['c_tri'][d], writes=['stri'])
            S.dma('sp', srev[:, d, :], k.din['c_rev'][d], writes=['srev'])
            S.dma('sp', mbias[:, d, :], k.din['c_mbias'][d], writes=['smb'])
        S.dma('sp', selh[:], k.din['c_selh'], writes=['selh'])
        dsk = sb(k, es, 's_dsk', [128, 32], F32)
        gn = sb(k, es, 's_gn', [128, 2048], F32)
        S.dma('sp', dsk[:], k.din['ssd_d'][j].partition_broadcast(128), writes=['s_dsk'])
        S.dma('sp', gn[:], k.din['ssd_norm'][j].partition_broadcast(128), writes=['s_gn'])
        BR = Ring(k, es, 's_B', 2, [128, 8, 128], BF16)
        CR = Ring(k, es, 's_C', 2, [128, 8, 128], BF16)
        XR = Ring(k, es, 's_X', 2, [128, 3072], BF16)
        dR = Ring(k, es, 's_d', 2, [128, 64], F32)
        daR = Ring(k, es, 's_da', 2, [128, 64], F32)
        zR = Ring(k, es, 's_z', 2, [128, 2048], BF16)
        yaR = Ring(k, es, 's_ya', 2, [128, 2048], F32)
        exps = sb(k, es, 's_ex', [128, 96], F32)
        negc = sb(k, es, 's_nc', [128, 32], F32)
        cumT = sb(k, es, 's_cT', [32, 128], F32)
        dtw = sb(k, es, 's_dtw', [128, 32], F32)
        xdt = sb(k, es, 's_xdt', [128, 2048], BF16)
        xdtw = sb(k, es, 's_xdtw', [128, 2048], BF16)
        DsR = Ring(k, es, 's_Ds', 2, [128, 4, 128], F32)
        LR = Ring(k, es, 's_L', 2, [128, 4, 128], BF16)
        ysb = sb(k, es, 's_y', [128, 2048], F32)
        Sst = sb(k, es, 's_S', [128, 8, 256], F32)
        Sbf = sb(k, es, 's_Sb', [128, 8, 256], BF16)
        ss = sb(k, es, 's_ss', [128, 1], F32)
        junk = sb(k, es, 's_junk', [128, 2048], F32)
        yb = sb(k, es, 's_yb', [128, 2048], BF16)
        ytR = Ring(k, es, 's_yt', 2, [128, 8, 128], BF16)
        sm_ps = ps(k, es, 's_sm', [128, 512])
        cb_ps = ps(k, es, 's_cb', [128, 512])
        DpR = Ring(k, es, 's_Dp', 2, [128, 4, 128], F32, psum=True)
        YpR = Ring(k, es, 's_Yp', 2, [128, 512], F32, psum=True)
        st_ps = ps(k, es, 's_stp', [128, 512])
        tp = ps(k, es, 's_tp', [128, 8, 128], BF16)
        Bv = cvT[2048:3072, :].rearrange("(g p) t -> p g t", p=128)
        Cv = cvT[3072:4096, :].rearrange("(g p) t -> p g t", p=128)
        yv = k.yT[0:2048, :].rearrange("(dc p) t -> p dc t", p=128)
        for d in range(2):
            dsl = slice(d * 32, (d + 1) * 32)
            memset(k, 'dve', Sst[:], 0.0, ['s_S'])
            memset(k, 'dve', Sbf[:], 0.0, ['s_Sb'])
            order = [32, 33] + list(range(32)) if d == 0 else [33, 32] + list(range(31, -1, -1))
            for c in order:
                tsl = slice(c * 128, (c + 1) * 128)
                B_, Bk = BR.next()
                S.dma('sp', B_[:], Bv[:, :, tsl], writes=[Bk])
                C_, Ck = CR.next()
                S.dma('sp', C_[:], Cv[:, :, tsl], writes=[Ck])
                X_, Xk = XR.next()
                S.dma('sp', X_[:], xbtm[tsl, :], writes=[Xk])
                dt_, dtk = dR.next()
                S.dma('sp', dt_[:], dtm[tsl, :], writes=[dtk])
                da_, dak = daR.next()
                S.dma('sp', da_[:], dtam[tsl, :], writes=[dak])
                if d == 1:
                    z_, zk = zR.next()
                    S.dma('sp', z_[:], ztm[tsl, :], writes=[zk])
                    ya, yak = yaR.next()
                    S.dma('sp', ya[:], yacc[tsl, :], writes=[yak])
                mm(k, sm_ps[:, 0:32], stri[:, d, :], da_[:, dsl], True, True, ['stri', dak], ['s_sm'])
                mm(k, sm_ps[:, 32:64], srev[:, d, :], da_[:, dsl], True, True, ['srev', dak], ['s_sm'])
                mm(k, sm_ps[:, 64:96], k.ones_f[:], da_[:, dsl], True, True, ['ones_f', dak], ['s_sm'])
                mm(k, sm_ps[0:32, 128:256], da_[:, dsl], stri[:, d, :], True, True, ['stri', dak], ['s_sm'])
                act(k, exps[:], sm_ps[:, 0:96], AF.Exp, ['s_sm'], ['s_ex'])
                act(k, negc[:], sm_ps[:, 0:32], IDENT, ['s_sm'], ['s_nc'], scale=-1.0)
                cp(k, 'act', cumT[:], sm_ps[0:32, 128:256], ['s_sm'], ['s_cT'])
                tt(k, 'dve', dtw[:], dt_[:, dsl], exps[:, 32:64], ALU.mult, [dtk, 's_ex'], ['s_dtw'])
                xs3 = X_[:, 0:2048].rearrange("p (h q) -> p h q", q=64)
                tt(k, 'dve', xdt[:].rearrange("p (h q) -> p h q", q=64), xs3,
                   dt_[:, dsl].unsqueeze(2).to_broadcast([128, 32, 64]), ALU.mult, [Xk, dtk], ['s_xdt'])
                tt(k, 'dve', xdtw[:].rearrange("p (h q) -> p h q", q=64), xs3,
                   dtw[:].unsqueeze(2).to_broadcast([128, 32, 64]), ALU.mult, [Xk, 's_dtw'], ['s_xdtw'])
                for g in range(8):
                    mm(k, cb_ps[:, 0:128], B_[:, g, :], C_[:, g, :], True, True, [Bk, Ck], ['s_cb'])
                    Dp, Dpk = DpR.next()
                    for r in range(4):
                        h = g * 4 + r
                        mm(k, Dp[:, r, :], selh[:, h * 128:(h + 1) * 128], cumT[:], True, False, ['selh', 's_cT'], [Dpk])
                        mm(k, Dp[:, r, :], k.ident_f[:], mbias[:, d, :], False, True, ['ident_f', 'smb'], [Dpk])
                    Ds, Dsk = DsR.next()
                    for r in range(4):
                        h = g * 4 + r
                        act(k, Ds[:, r, :], Dp[:, r, :], AF.Exp, [Dpk, 's_nc'], [Dsk], bias=negc[:, h:h + 1])
                    L_, Lk = LR.next()
                    tt(k, 'dve', L_[:], Ds[:], cb_ps[:, 0:128].unsqueeze(1).to_broadcast([128, 4, 128]), ALU.mult,
                       [Dsk, 's_cb'], [Lk])
                    Yp, Ypk = YpR.next()
                    for r in range(4):
                        h = g * 4 + r
                        mm(k, Yp[:, r * 64:(r + 1) * 64], L_[:, r, :], xdt[:, h * 64:(h + 1) * 64], True, True,
                           [Lk, 's_xdt'], [Ypk])
                    mm(k, Yp[:, 256:512], C_[:, g, :], Sbf[:, g, :], True, True, [Ck, ('s_Sb', g)], [Ypk])
                    mm(k, st_ps[:, 0:256], X_[:, 2048 + g * 128:2048 + (g + 1) * 128], xdtw[:, g * 256:(g + 1) * 256],
                       True, True, [Xk, 's_xdtw'], ['s_stp'])
                    yg = ysb[:, g * 256:(g + 1) * 256]
                    tt(k, 'dve', yg.rearrange("p (r q) -> p r q", q=64), Yp[:, 256:512].rearrange("p (r q) -> p r q", q=64),
                       exps[:, g * 4:(g + 1) * 4].unsqueeze(2).to_broadcast([128, 4, 64]), ALU.mult,
                       [Ypk, 's_ex'], [('s_y', g)])
                    tt(k, 'dve', yg, yg, Yp[:, 0:256], ALU.add, [Ypk, ('s_y', g)], [('s_y', g)])
                    sg = Sst[:, g, :]
                    tt(k, 'dve', sg.rearrange("p (r q) -> p r q", q=64), sg.rearrange("p (r q) -> p r q", q=64),
                       exps[:, 64 + g * 4:64 + (g + 1) * 4].unsqueeze(2).to_broadcast([128, 4, 64]), ALU.mult,
                       [('s_S', g), 's_S', 's_ex'], [('s_S', g)])
                    tt(k, 'dve', sg, sg, st_ps[:, 0:256], ALU.add, [('s_S', g), 's_stp'], [('s_S', g)])
                    cp(k, 'act', Sbf[:, g, :], sg, [('s_S', g)], [('s_Sb', g)])
                ykeys = [('s_y', g) for g in range(8)]
                if d == 0:
                    S.dma('sp', yacc[tsl, :], ysb[:], reads=ykeys)
                else:
                    tt(k, 'dve', ysb[:], ysb[:], ya[:], ALU.add, ykeys + [yak], ['s_yf'])
                    tt(k, 'dve', junk[:].rearrange("p (h q) -> p h q", q=64), xs3,
                       dsk[:].unsqueeze(2).to_broadcast([128, 32, 64]), ALU.mult, [Xk, 's_dsk'], ['s_junk'])
                    tt(k, 'dve', ysb[:], ysb[:], junk[:], ALU.add, ['s_yf', 's_junk'] + ykeys, ['s_yf'] + ykeys)
                    tt(k, 'dve', ysb[:], ysb[:], z_[:], ALU.mult, ['s_yf', zk] + ykeys, ['s_yf'] + ykeys)
                    memset(k, 'dve', ss[:], 0.0, ['s_ss'])
                    act(k, junk[:], ysb[:], AF.Square, ['s_yf'] + ykeys, ['s_junk', 's_ss'], accum_out=ss[:])
                    rstd_from_ss(k, ss[:], 2048.0, 's_ss')
                    ts(k, 'dve', ysb[:], ysb[:], ss[:, 0:1], None, ALU.mult, None, ['s_yf', 's_ss'] + ykeys, ['s_yf'] + ykeys)
                    tt(k, 'dve', yb[:], ysb[:], gn[:], ALU.mult, ['s_yf', 's_gn'] + ykeys, ['s_yb'])
                    transpose_out(k, tp, yb, 's_yb', ytR, 16, yv, c)
            S.barrier()

HY_SEGS = {0: dict(s0=0, L=SEQ), 1: dict(s0=SEQ, L=CTXL)}
for _s in HY_SEGS.values():
    _s['N'] = 2 * _s['L']
    _s['N1'] = _s['N'] // 128
    _s['M1'] = _s['N1'] // 2
    _s['K1'] = _s['N1'] // 2 + 1
    _s['R'] = 2 * _s['K1']


def hyena_constants():
    c = {}
    a = np.arange(128)
    th = 2 * np.pi * ((a[:, None] * a[None, :]) % 128) / 128.0
    c['c_cs'] = np.cos(th).astype(np.float32)
    c['c_sn'] = np.sin(th).astype(np.float32)
    c['c_deltas'] = np.abs(np.linspace(math.log(1e-2) / 0.3, math.log(1e-2) / 1.5, 1024)).astype(np.float32)
    for s, g in HY_SEGS.items():
        L, N, N1, M1, K1, R = g['L'], g['N'], g['N1'], g['M1'], g['K1'], g['R']
        m = (128 * np.arange(M1)[:, None] + np.arange(128)[None, :]).astype(np.int64)
        k1 = np.arange(K1, dtype=np.int64)
        th = 2 * np.pi * ((m[:, :, None] * k1[None, None, :]) % N) / float(N)
        ef = np.zeros((M1, 128, R), np.float64)
        ef[:, :, 0::2] = np.cos(th)
        ef[:, :, 1::2] = -np.sin(th)
        c[f'c_efwd{s}'] = ef.astype(np.float32)
        w = np.full(K1, 2.0)
        w[0] = 1.0
        w[-1] = 1.0
        ei = np.zeros((R, 128, M1), np.float64)
        ei[0::2] = (np.cos(th) * w[None, None, :] / N).transpose(2, 1, 0)
        ei[1::2] = (-np.sin(th) * w[None, None, :] / N).transpose(2, 1, 0)
        c[f'c_einv{s}'] = ei.astype(np.float32)
        t = np.linspace(0.0, 1.0, L, dtype=np.float32)[:, None]
        wv = (2 * math.pi * np.arange(L, dtype=np.float32)[:, None] / L).astype(np.float32)
        ang = np.linspace(1e-4, 15, 16, dtype=np.float32)[None, :] * wv
        emb = np.concatenate([t, np.cos(ang), -np.sin(ang)], axis=-1).astype(np.float32)
        c[f'c_emb{s}'] = np.ascontiguousarray(emb.T)
        c[f'c_tcol{s}'] = np.ascontiguousarray(t[:, 0][m.reshape(-1)].reshape(M1, 128))
    return c


def hyena_host_layout(inputs, m):
    cols = [inputs['hy_f_b1'][0], inputs['hy_f_b2'][0], inputs['hy_f_b3'][0], inputs['hy_f_freq'][0]]
    m['hycol'] = np.ascontiguousarray(np.stack([np.asarray(c, np.float32) for c in cols], axis=1))


def hy_common(k, es):
    S = k.S
    H = {}
    for nm, src, neg in (('cs', 'c_cs', False), ('sn', 'c_sn', False), ('ncs', 'c_cs', True), ('nsn', 'c_sn', True)):
        t = sb(k, es, 'hy_' + nm, [128, 128], BF16)
        tf = sb(k, es, 'hyf_' + nm, [128, 128], F32)
        S.dma('sp', tf[:], k.din[src], writes=['hyf_' + nm])
        S.op('act', lambda: k.nc.scalar.mul(out=t[:], in_=tf[:], mul=(-1.0 if neg else 1.0)), ['hyf_' + nm], ['hy_' + nm])
        H[nm] = t
    return H


def load_E(k, es, s):
    g = HY_SEGS[s]
    ef = sb(k, es, f'hy_ef{s}', [g['M1'], 128, g['R']], BF16)
    ei = sb(k, es, f'hy_ei{s}', [g['R'], 128, g['M1']], BF16)
    k.S.dma('pool', ef[:], k.din[f'c_efwd{s}'], writes=['hy_ef'])
    k.S.dma('pool', ei[:], k.din[f'c_einv{s}'], writes=['hy_ei'])
    return ef, ei


def hy_filters(k, j, s, Hspec, BdF):
    nc, S = k.nc, k.S
    g = HY_SEGS[s]
    L, M1, K1, R = g['L'], g['M1'], g['K1'], g['R']
    with ExitStack() as es:
        H = hy_common(k, es)
        ef, ei = load_E(k, es, s)
        hycol = sb(k, es, 'hycol', [64, 4], F32)
        S.dma('sp', hycol[:], k.din['hycol'], writes=['hycol'])
        negpi = sb(k, es, 'negpi', [128, 1], F32)
        memset(k, 'dve', negpi[:], -math.pi, ['negpi'])
        ki = sb(k, es, 'hy_ki', [64, 512], mybir.dt.int32)
        kf = sb(k, es, 'hy_kf', [64, 512], F32)
        embT = sb(k, es, 'hy_emb', [33, L], F32)
        S.dma('sp', embT[:], k.din[f'c_emb{s}'], writes=['hy_emb'])
        w1 = sb(k, es, 'hy_w1', [33, 64], F32)
        w2 = sb(k, es, 'hy_w2', [64, 64], F32)
        w3 = sb(k, es, 'hy_w3', [64, 64], F32)
        w4 = sb(k, es, 'hy_w4', [64, 4096], BF16)
        S.dma('sp', w1[:], k.din['hy_f_w1'][j], writes=['hy_w1'])
        S.dma('sp', w2[:], k.din['hy_f_w2'][j], writes=['hy_w2'])
        S.dma('sp', w3[:], k.din['hy_f_w3'][j], writes=['hy_w3'])
        S.dma('pool', w4[:], k.din['hy_f_w4'][j], writes=['hy_w4'])
        hA = sb(k, es, 'hy_hA', [64, L], F32)
        hB = sb(k, es, 'hy_hB', [64, L], F32)
        h3 = sb(k, es, 'hy_h3', [64, L], BF16)
        arg = Ring(k, es, 'hy_arg', 2, [64, 512], F32)
        mp = Ring(k, es, 'hy_mp', 2, [64, 512], F32, psum=True)
        layers = [(w1, 'hy_w1', embT, 'hy_emb', hA, 'hy_hA', 0), (w2, 'hy_w2', hA, 'hy_hA', hB, 'hy_hB', 1),
                  (w3, 'hy_w3', hB, 'hy_hB', h3, 'hy_h3', 2)]
        for (w, wk, src, srck, dst, dstk, li) in layers:
            for t0 in range(0, L, 512):
                n = min(512, L - t0)
                p, pk = mp.next()
                mm(k, p[:, 0:n], w[:], src[:, t0:t0 + n], True, True, [wk, srck], [pk])
                a_, ak = arg.next()
                ts(k, 'dve', a_[:, 0:n], p[:, 0:n], hycol[:, li:li + 1], hycol[:, 3:4], ALU.add, ALU.mult, [pk, 'hycol'], [ak])
                ts(k, 'dve', a_[:, 0:n], a_[:, 0:n], 1.0 / (2.0 * math.pi), 8.0, ALU.mult, ALU.add, [ak], [ak])
                cp(k, 'dve', ki[:, 0:n], a_[:, 0:n], [ak], ['hy_ki'])
                cp(k, 'dve', kf[:, 0:n], ki[:, 0:n], ['hy_ki'], ['hy_kf'])
                tt(k, 'dve', a_[:, 0:n], a_[:, 0:n], kf[:, 0:n], ALU.subtract, [ak, 'hy_kf'], [ak])
                ts(k, 'dve', kf[:, 0:n], a_[:, 0:n], 0.5, None, ALU.is_gt, None, [ak], ['hy_kf'])
                tt(k, 'dve', a_[:, 0:n], a_[:, 0:n], kf[:, 0:n], ALU.subtract, [ak, 'hy_kf'], [ak])
                act(k, dst[:, t0:t0 + n], a_[:, 0:n], AF.Sin, [ak], [dstk], scale=2.0 * math.pi)
        dl = sb(k, es, 'hy_dl', [M1, 1024], F32)
        tcol = sb(k, es, 'hy_tc', [M1, 128], F32)
        S.dma('sp', dl[:], k.din['c_deltas'].partition_broadcast(M1), writes=['hy_dl'])
        S.dma('sp', tcol[:], k.din[f'c_tcol{s}'], writes=['hy_tc'])
        ts(k, 'dve', tcol[:], tcol[:], -1.0, None, ALU.mult, None, ['hy_tc'], ['hy_tc'])
        wn = Ring(k, es, 'hy_wn', 2, [M1, 1024], F32)
        ft = Ring(k, es, 'hy_ft', 2, [M1, 4096], BF16)
        fp = Ring(k, es, 'hy_fp', 3, [M1, 512], F32, psum=True)
        bp = Ring(k, es, 'hy_bp', 3, [R, 512], F32, psum=True)
        bs = Ring(k, es, 'hy_bs', 2, [R, 4096], BF16)
        h3v = h3[:].rearrange("r (a b) -> r b a", b=128)
        for m2 in range(128):
            wt, wtk = wn.next()
            act(k, wt[:], dl[:], AF.Exp, ['hy_dl', 'hy_tc'], [wtk], scale=tcol[:, m2:m2 + 1])
            f_, fk = ft.next()
            for q in range(8):
                p, pk = fp.next()
                mm(k, p[:], h3v[:, m2, :], w4[:, q * 512:(q + 1) * 512], True, True, ['hy_h3', 'hy_w4'], [pk])
                cw = (q % 2) * 512
                tt(k, 'dve', f_[:, q * 512:(q + 1) * 512], p[:], wt[:, cw:cw + 512], ALU.mult, [pk, wtk], [fk])
            if m2 == 0:
                for o in range(2):
                    memset(k, 'dve', f_[0:1, o * 2048 + 1024:(o + 1) * 2048], 0.0, [fk])
            b_, bk = bs.next()
            for q in range(8):
                p, pk = bp.next()
                mm(k, p[:], ef[:, m2, :], f_[:, q * 512:(q + 1) * 512], True, True, ['hy_ef', fk], [pk])
                cp(k, 'act' if q % 2 else 'dve', b_[:, q * 512:(q + 1) * 512], p[:], [pk], [bk])
            S.dma('sp', BdF[m2, 0:R, :], b_[:], reads=[bk])
        S.barrier()
    with ExitStack() as es:
        H = hy_common(k, es)
        fr = Ring(k, es, 'hy_fr', 2, [128, 2, 2048], BF16)
        hs = Ring(k, es, 'hy_hs', 2, [128, 2, 1024], F32)
        hp = Ring(k, es, 'hy_hp', 4, [128, 512], F32, psum=True)
        for o in range(2):
            for k1 in range(K1):
                f_, fk = fr.next()
                S.dma('sp', f_[:], BdF[:, 2 * k1:2 * k1 + 2, o * 2048:(o + 1) * 2048], writes=[fk])
                h_, hk = hs.next()
                for ch in range(2):
                    Fr = f_[:, 0, ch * 512:(ch + 1) * 512]
                    Fi = f_[:, 1, ch * 512:(ch + 1) * 512]
                    Br = f_[:, 0, 1024 + ch * 512:1024 + (ch + 1) * 512]
                    Bi = f_[:, 1, 1024 + ch * 512:1024 + (ch + 1) * 512]
                    p, pk = hp.next()
                    for i_, (mat, rhs) in enumerate((('cs', Fr), ('sn', Fi), ('cs', Br), ('sn', Bi))):
                        mm(k, p[:], H[mat][:], rhs, i_ == 0, i_ == 3, ['hy_' + mat, fk], [pk])
                    cp(k, 'dve', h_[:, 0, ch * 512:(ch + 1) * 512], p[:], [pk], [hk])
                    p, pk = hp.next()
                    for i_, (mat, rhs) in enumerate((('cs', Fi), ('nsn', Fr), ('ncs', Bi), ('sn', Br))):
                        mm(k, p[:], H[mat][:], rhs, i_ == 0, i_ == 3, ['hy_' + mat, fk], [pk])
                    cp(k, 'act', h_[:, 1, ch * 512:(ch + 1) * 512], p[:], [pk], [hk])
                S.dma('sp', Hspec[s][o][k1], h_[:], reads=[hk])
        S.barrier()


def hy_conv(k, s, o, Hspec, Bd, Gd, xbtm, hb, src_col, mul_col, dst, dst_col, first_B_from=None):
    nc, S = k.nc, k.S
    g = HY_SEGS[s]
    s0, L, M1, K1, R = g['s0'], g['L'], g['M1'], g['K1'], g['R']

    def strided(tm, c0):
        return tm[s0:s0 + L, c0:c0 + 1024].rearrange("(a b) c -> b a c", b=128)
    srcv = strided(xbtm if src_col >= 0 else dst, src_col if src_col >= 0 else 0)
    mulv = strided(xbtm, mul_col)
    dstv = strided(dst, dst_col)
    with ExitStack() as es:
        ef, ei = load_E(k, es, s)
        xr = Ring(k, es, 'hc_x', 3, [M1, 1024], BF16)
        bp = Ring(k, es, 'hc_bp', 4, [R, 512], F32, psum=True)
        bs = Ring(k, es, 'hc_bs', 3, [R, 1024], BF16)
        for m2 in range(128):
            x_, xk = xr.next()
            S.dma('sp', x_[:], srcv[m2], writes=[xk])
            b_, bk = bs.next()
            for ch in range(2):
                p, pk = bp.next()
                mm(k, p[:], ef[:, m2, :], x_[:, ch * 512:(ch + 1) * 512], True, True, ['hy_ef', xk], [pk])
                cp(k, 'act' if ch else 'dve', b_[:, ch * 512:(ch + 1) * 512], p[:], [pk], [bk])
            S.dma('sp', Bd[m2, 0:R, :], b_[:], reads=[bk])
        S.barrier()
    with ExitStack() as es:
        H = hy_common(k, es)
        br = Ring(k, es, 'hc_b', 2, [128, 2, 1024], BF16)
        hr = Ring(k, es, 'hc_h', 2, [128, 2, 1024], F32)
        t1 = sb(k, es, 'hc_t1', [128, 1024], F32)
        t2 = sb(k, es, 'hc_t2', [128, 1024], F32)
        t3 = sb(k, es, 'hc_t3', [128, 1024], F32)
        t4 = sb(k, es, 'hc_t4', [128, 1024], F32)
        Y = sb(k, es, 'hc_Y', [128, 2, 1024], BF16)
        gs = Ring(k, es, 'hc_gs', 2, [128, 2, 1024], BF16)
        xp = ps(k, es, 'hc_xp', [128, 2, 1024])
        gp = ps(k, es, 'hc_gp', [128, 2, 1024])
        for k1 in range(K1):
            b_, bk = br.next()
            S.dma('sp', b_[:], Bd[:, 2 * k1:2 * k1 + 2, :], writes=[bk])
            h_, hk = hr.next()
            S.dma('sp', h_[:], Hspec[s][o][k1], writes=[hk])
            for ch in range(2):
                cs_ = slice(ch * 512, (ch + 1) * 512)
                mm(k, xp[:, 0, cs_], H['cs'][:], b_[:, 0, cs_], True, False, ['hy_cs', bk], ['hc_xp'])
                mm(k, xp[:, 0, cs_], H['sn'][:], b_[:, 1, cs_], False, True, ['hy_sn', bk], ['hc_xp'])
                mm(k, xp[:, 1, cs_], H['cs'][:], b_[:, 1, cs_], True, False, ['hy_cs', bk], ['hc_xp'])
                mm(k, xp[:, 1, cs_], H['nsn'][:], b_[:, 0, cs_], False, True, ['hy_nsn', bk], ['hc_xp'])
            tt(k, 'dve', t1[:], xp[:, 0, :], h_[:, 0, :], ALU.mult, ['hc_xp', hk], ['hc_t1'])
            tt(k, 'dve', t2[:], xp[:, 1, :], h_[:, 1, :], ALU.mult, ['hc_xp', hk], ['hc_t2'])
            tt(k, 'dve', t3[:], xp[:, 0, :], h_[:, 1, :], ALU.mult, ['hc_xp', hk], ['hc_t3'])
            tt(k, 'dve', t4[:], xp[:, 1, :], h_[:, 0, :], ALU.mult, ['hc_xp', hk], ['hc_t4'])
            tt(k, 'pool', Y[:, 0, :], t1[:], t2[:], ALU.subtract, ['hc_t1', 'hc_t2'], ['hc_Y'])
            tt(k, 'pool', Y[:, 1, :], t3[:], t4[:], ALU.add, ['hc_t3', 'hc_t4'], ['hc_Y'])
            for ch in range(2):
                cs_ = slice(ch * 512, (ch + 1) * 512)
                mm(k, gp[:, 0, cs_], H['cs'][:], Y[:, 0, cs_], True, False, ['hy_cs', 'hc_Y'], ['hc_gp'])
                mm(k, gp[:, 0, cs_], H['nsn'][:], Y[:, 1, cs_], False, True, ['hy_nsn', 'hc_Y'], ['hc_gp'])
                mm(k, gp[:, 1, cs_], H['sn'][:], Y[:, 0, cs_], True, False, ['hy_sn', 'hc_Y'], ['hc_gp'])
                mm(k, gp[:, 1, cs_], H['cs'][:], Y[:, 1, cs_], False, True, ['hy_cs', 'hc_Y'], ['hc_gp'])
            g_, gk = gs.next()
            cp(k, 'act', g_[:, 0, :], gp[:, 0, :], ['hc_gp'], [gk])
            cp(k, 'act', g_[:, 1, :], gp[:, 1, :], ['hc_gp'], [gk])
            S.dma('sp', Gd[:, 2 * k1:2 * k1 + 2, :], g_[:], reads=[gk])
        S.barrier()
    with ExitStack() as es:
        ef, ei = load_E(k, es, s)
        bias = sb(k, es, 'hc_bias', [M1, 1024], F32)
        S.dma('sp', bias[:], k.din['hy_bias'][0, o].partition_broadcast(M1), writes=['hc_bias'])
        gr = Ring(k, es, 'hc_g', 3, [R, 1024], BF16)
        vr = Ring(k, es, 'hc_v', 3, [M1, 1024], BF16)
        mr = Ring(k, es, 'hc_m', 3, [M1, 1024], BF16)
        tr = Ring(k, es, 'hc_t', 2, [M1, 1024], F32)
        zr = Ring(k, es, 'hc_z', 3, [M1, 1024], BF16)
        yp = Ring(k, es, 'hc_yp', 3, [M1, 1024], F32, psum=True)
        for n2 in range(128):
            g_, gk = gr.next()
            S.dma('sp', g_[:], Gd[n2, 0:R, :], writes=[gk])
            v_, vk = vr.next()
            S.dma('sp', v_[:], srcv[n2], writes=[vk])
            m_, mk = mr.next()
            S.dma('sp', m_[:], mulv[n2], writes=[mk])
            p, pk = yp.next()
            for ch in range(2):
                mm(k, p[:, ch * 512:(ch + 1) * 512], ei[:, n2, :], g_[:, ch * 512:(ch + 1) * 512], True, True,
                   ['hy_ei', gk], [pk])
            t_, tk = tr.next()
            tt(k, 'pool', t_[:], v_[:], bias[:], ALU.mult, [vk, 'hc_bias'], [tk])
            tt(k, 'dve', t_[:], t_[:], p[:], ALU.add, [tk, pk], [tk])
            z_, zk = zr.next()
            tt(k, 'dve', z_[:], t_[:], m_[:], ALU.mult, [tk, mk], [zk])
            S.dma('sp', dstv[n2], z_[:], reads=[zk])
        S.barrier()


def hyena_mixer(k, l, j):
    nc, S = k.nc, k.S
    preT = scr(k, 'xbcT', [4096, T], BF16)
    cvT = scr(k, 'cvT', [4096, T], BF16)
    xbtm = scr(k, 'xbtm', [T, 3072], BF16)
    otm = scr(k, 'hy_otm', [T, 2048], BF16)
    BdF = scr(k, 'hy_BdF', [128, 66, 4096], BF16)
    Bd = scr(k, 'hy_Bd', [128, 66, 1024], BF16)
    Gd = scr(k, 'hy_Gd', [128, 66, 1024], BF16)
    Hspec = {s: [[scr(k, f'hy_H{s}_{o}_{k1}', [128, 2, 1024], F32) for k1 in range(HY_SEGS[s]['K1'])]
                 for o in range(2)] for s in HY_SEGS}
    W = k.din['hy_w_in'][j]
    with ExitStack() as es:
        uT = build_uT(k, es, l)
        st = Ring(k, es, 'hst', 4, [128, 512], BF16)
        specs = [('fm', i * 512, 512, fm_store(k, st, preT, 0, eng=('dve' if i % 2 else 'act'))) for i in range(6)]
        inproj(k, es, uT, W, specs)
        S.barrier()
    with ExitStack() as es:
        conv_fm(k, es, preT, cvT, 24, 3, 'hy_conv_w', 'hy_conv_b', IDENT)
        S.barrier()
    with ExitStack() as es:
        fm_to_tm(k, es, cvT, 0, 24, xbtm, 0)
        S.barrier()
    for s in HY_SEGS:
        hy_filters(k, j, s, Hspec, BdF)
        hy_conv(k, s, 0, Hspec, Bd, Gd, xbtm, None, 0, 1024, otm, 0)
        hy_conv(k, s, 1, Hspec, Bd, Gd, xbtm, None, -1, 2048, otm, 1024)
    with ExitStack() as es:
        ir = Ring(k, es, 'ho_i', 2, [128, 1024], BF16)
        ytR = Ring(k, es, 'ho_yt', 2, [128, 8, 128], BF16)
        tp = ps(k, es, 'ho_tp', [128, 8, 128], BF16)
        yv = k.yT[0:1024, :].rearrange("(dc p) t -> p dc t", p=128)
        for a in range(NTT):
            i_, ik = ir.next()
            S.dma('sp', i_[:], otm[a * 128:(a + 1) * 128, 1024:2048], writes=[ik])
            transpose_out(k, tp, i_, ik, ytR, 8, yv, a)
        S.barrier()

def mixer_copy(k, l):
    S = k.S
    with ExitStack() as es:
        uT = build_uT(k, es, l)
        yv = k.yT[0:1024, :].rearrange("(kc p) t -> p kc t", p=128)
        for (t0, n) in BLOCKS:
            S.dma('sp', yv[:, :, t0:t0 + n], uT[:, :, t0:t0 + n], reads=[('uT', t0)], writes=[('yT', t0)])
        S.barrier()


WEIGHT_NAMES = ['ada_w', 'ffn_w1', 'ffn_w2', 'gla_w_in', 'gla_w_a2', 'gla_b_a2', 'gla_norm', 'gla_w_out',
                'ssd_w_in', 'ssd_dt_bias', 'ssd_a_log', 'ssd_d', 'ssd_norm', 'ssd_w_out',
                'hy_w_in', 'hy_f_w1', 'hy_f_w2', 'hy_f_w3', 'hy_f_w4', 'hy_bias', 'hy_w_out']


def host_constants():
    c = {}
    c['c_ident'] = np.eye(128, dtype=np.float32)
    perm = np.arange(128)
    perm[64:] = 64 + (127 - perm[64:])
    P = np.zeros((128, 128), np.float32)
    P[perm, np.arange(128)] = 1.0
    c['c_psnake'] = P
    j = np.arange(128)[:, None]
    i = np.arange(128)[None, :]
    le = (j <= i).astype(np.float32)
    ge = (j >= i).astype(np.float32)
    gt = (j > i).astype(np.float32)
    lt = (j < i).astype(np.float32)
    c['c_mask'] = np.stack([le, ge])
    c['c_tri'] = np.stack([le, ge])
    c['c_rev'] = np.stack([gt, lt])
    c['c_mbias'] = np.stack([(le - 1.0) * 30000.0, (ge - 1.0) * 30000.0]).astype(np.float32)
    sel = np.zeros((32, 32, 128), np.float32)
    for h in range(32):
        sel[h, h, :] = 1.0
    c['c_selh'] = sel.reshape(32, 32 * 128)
    c.update(hyena_constants())
    return c


def build_program(shapes, cp, nlayers=DEPTH, mixers=None, debug_out=None):
    nc = bass.Bass("TRN2", target_bir_lowering=False)
    k = K()
    k.nc = nc
    k.cp = cp
    k.nlayers = nlayers
    k.din = {}
    for name, (shape, dt) in shapes.items():
        k.din[name] = nc.dram_tensor(name, list(shape), dt, kind="ExternalInput").ap()
    k.dout = nc.dram_tensor("out", [SEQ, D], F32, kind="ExternalOutput").ap()
    k.hT = nc.dram_tensor("hT", [D, T], F32).ap()
    k.yT = nc.dram_tensor("yT", [2048, T], BF16).ap()
    k.dbg = {}
    with ExitStack() as es:
        k.S = Sched(nc, es)
        k.colp = sb(k, es, 'colp', [128, cp.n], F32)
        k.MOD = sb(k, es, 'MOD', [128, DEPTH, 2, 48], F32)
        k.ident_f = sb(k, es, 'ident_f', [128, 128], F32)
        k.ident_b = sb(k, es, 'ident_b', [128, 128], BF16)
        k.ones_b = sb(k, es, 'ones_b', [128, 128], BF16)
        k.ones_f = sb(k, es, 'ones_f', [128, 128], F32)
        k.kinds = [(mixers[l] if mixers else ['gla', 'ssd', 'hy'][l % 3]) for l in range(nlayers)]
        prologue(k)
        weight_prep(k)
        for l in range(nlayers):
            kind = (mixers[l] if mixers else ['gla', 'ssd', 'hy'][l % 3])
            j = l // 3
            if kind == 'copy':
                mixer_copy(k, l)
                post(k, l, k.din['gla_w_out'][0], 8)
            elif kind == 'gla':
                gla_mixer(k, l, j)
                post(k, l, k.din['gla_w_out'][j], 8)
            elif kind == 'ssd':
                ssd_mixer(k, l, j)
                post(k, l, k.din['ssd_w_out'][j], 16)
            else:
                hyena_mixer(k, l, j)
                post(k, l, k.din['hy_w_out'][j], 8)
        epilogue(k)
    k.ninst = k.S.ninst
    return nc, k


def host_inputs(inputs, b):
    m = {}
    m['x'] = np.ascontiguousarray(inputs['x'][b])
    m['ctx'] = np.ascontiguousarray(inputs['ctx'][b])
    m['cvec'] = np.ascontiguousarray(np.concatenate([col_layout(inputs['c'][b]), col_layout(inputs['c_ctx'])], axis=1))
    for n in WEIGHT_NAMES:
        m[n] = np.ascontiguousarray(np.asarray(inputs[n], np.float32))
    return m


def make_colpack(inputs):
    cp = ColPack()
    for l in range(DEPTH):
        cp.add(f'ada_b{l}', inputs['ada_b'][l])
        for s in range(2):
            cp.add(f'ln_g{l}_{s}', inputs['ln_g'][l, s])
            cp.add(f'ln_b{l}_{s}', inputs['ln_b'][l, s])
    cp.add('ssd_conv_b', inputs['ssd_conv_b'][0])
    for t in range(5):
        cp.add(f'ssd_conv_w{t}', inputs['ssd_conv_w'][0, t])
    cp.add('hy_conv_b', inputs['hy_conv_b'][0])
    for t in range(3):
        cp.add(f'hy_conv_w{t}', inputs['hy_conv_w'][0, t])
    return cp


_CACHE = {}


def kernel(**inputs):
    inputs = {n: np.asarray(v) for n, v in inputs.items()}
    cp = make_colpack(inputs)
    consts = host_constants()
    maps = []
    for b in range(8):
        m = host_inputs(inputs, b)
        m['colp'] = cp.array()
        m.update(consts)
        hyena_host_layout(inputs, m)
        maps.append(m)
    if 'nc' not in _CACHE:
        shapes = {n: (v.shape, F32) for n, v in maps[0].items()}
        _CACHE['nc'] = build_program(shapes, cp)[0]
    res = run_bass_kernel_spmd(_CACHE['nc'], maps, core_ids=list(range(8)))
    return np.stack([np.asarray(r['out'], np.float32) for r in res.results], axis=0)
```

```python
import numpy as np
import concourse.bass as bass
import concourse.mybir as mybir
from contextlib import ExitStack

F32 = mybir.dt.float32
BF16 = mybir.dt.bfloat16
ALU = mybir.AluOpType
AF = mybir.ActivationFunctionType
AX = mybir.AxisListType


class Sched:
    EPOCH = 20000
    NSLOT = 12

    def __init__(self, nc, es):
        self.nc = nc
        self.es = es
        self.eng = {'pe': nc.tensor, 'act': nc.scalar, 'dve': nc.vector,
                    'pool': nc.gpsimd, 'sp': nc.sync}
        self.esem = {}
        self.ecnt = {}
        self.nsem = 0
        for k in ['pe', 'act', 'dve', 'pool']:
            self._new_epoch(k)
        self.dsem = {q: [self._sem(f'd_{q}{i}') for i in range(self.NSLOT)]
                     for q in ['sp', 'pool', 'act']}
        self.dcnt = {q: [0] * self.NSLOT for q in self.dsem}
        self.dnext = {q: 0 for q in self.dsem}
        self.known = {e: {} for e in self.eng}
        self.lastw = {}
        self.readers = {}
        self.ninst = 0

    def _sem(self, name):
        self.nsem += 1
        return self.es.enter_context(self.nc.semaphore(f'{name}_{self.nsem}'))

    def _new_epoch(self, k):
        self.esem[k] = self._sem('e_' + k)
        self.ecnt[k] = 0

    def _deps(self, me, reads, writes):
        deps = []
        for k in reads:
            w = self.lastw.get(k)
            if w is not None:
                deps.append((w, False))
        for k in writes:
            w = self.lastw.get(k)
            if w is not None:
                deps.append((w, False))
            for r in self.readers.get(k, ()):
                deps.append((r, True))
        out = []
        for (ev, war) in deps:
            sem, val, owner = ev
            if owner == me:
                if me == 'pe':
                    continue
            out.append(ev)
        return out

    def _wait(self, me, evs):
        kn = self.known[me]
        e = self.eng[me]
        for (sem, val, owner) in evs:
            sid = id(sem)
            if kn.get(sid, 0) >= val:
                continue
            e.wait_ge(sem, val)
            kn[sid] = val

    def _record(self, ev, reads, writes):
        for k in writes:
            self.lastw[k] = ev
            self.readers[k] = []
        for k in reads:
            if k in writes:
                continue
            lst = self.readers.setdefault(k, [])
            if ev[2] is not None:
                lst[:] = [r for r in lst if r[2] != ev[2]]
            lst.append(ev)

    def op(self, me, fn, reads=(), writes=()):
        self._wait(me, self._deps(me, reads, writes))
        if self.ecnt[me] >= self.EPOCH:
            self._new_epoch(me)
        ins = fn()
        self.ecnt[me] += 1
        ins.then_inc(self.esem[me], 1)
        ev = (self.esem[me], self.ecnt[me], me)
        self._record(ev, reads, writes)
        self.ninst += 1
        return ev

    def dma(self, q, out, in_, reads=(), writes=(), **kw):
        if q == 'sp' and 'DRam' in type(out.tensor).__name__:
            q = 'pool'
        s = self.dnext[q]
        self.dnext[q] = (s + 1) % self.NSLOT
        sem = self.dsem[q][s]
        evs = self._deps(q, reads, writes)
        if self.dcnt[q][s] > 0:
            evs.append((sem, self.dcnt[q][s], None))
        self._wait(q, evs)
        ins = self.eng[q].dma_start(out=out, in_=in_, **kw)
        self.dcnt[q][s] += 16
        ins.then_inc(sem, 16)
        ev = (sem, self.dcnt[q][s], None)
        self._record(ev, reads, writes)
        self.ninst += 1
        return ev

    def barrier(self):
        evs = []
        for k in self.esem:
            if self.ecnt[k] > 0:
                evs.append((self.esem[k], self.ecnt[k], k))
        for q in self.dsem:
            for s in range(self.NSLOT):
                if self.dcnt[q][s] > 0:
                    evs.append((self.dsem[q][s], self.dcnt[q][s], None))
        for me in self.eng:
            self._wait(me, evs)
        self.lastw = {}
        self.readers = {}
import math
from concourse.bass_utils import run_bass_kernel_spmd

D = 1024
SEQ = 4096
CTXL = 256
T = SEQ + CTXL
NTT = T // 128
DEPTH = 4
ALPHA = (2 * DEPTH) ** 0.25
LN_EPS = 1e-5
BLOCKS = [(i * 512, 512) for i in range(8)] + [(SEQ, CTXL)]
IDENT = AF.Identity


def seg_of(t0):
    return 0 if t0 < SEQ else 1


def col_layout(v):
    v = np.asarray(v, np.float32).reshape(-1, 128)
    return np.ascontiguousarray(v.T)


class ColPack:
    def __init__(self):
        self.cols = []
        self.off = {}
        self.n = 0

    def add(self, name, v):
        c = col_layout(v)
        self.off[name] = self.n
        self.cols.append(c)
        self.n += c.shape[1]

    def array(self):
        return np.ascontiguousarray(np.concatenate(self.cols, axis=1))


class K:
    pass


_UNIQ = [0]


def sb(k, es, name, shape, dt):
    _UNIQ[0] += 1
    return es.enter_context(k.nc.sbuf_tensor(f's{_UNIQ[0]}_{name}', list(shape), dt))


def ps(k, es, name, shape, dt=F32):
    _UNIQ[0] += 1
    return es.enter_context(k.nc.psum_tensor(f'p{_UNIQ[0]}_{name}', list(shape), dt))


class Ring:
    def __init__(self, k, es, name, n, shape, dt, psum=False):
        self.name = name
        self.n = n
        self.i = 0
        self.t = [(ps if psum else sb)(k, es, f'{name}{j}', shape, dt) for j in range(n)]

    def next(self):
        j = self.i % self.n
        self.i += 1
        return self.t[j], (self.name, j)


def act(k, out, in_, func, reads, writes, bias=None, scale=None, accum_out=None):
    kw = {}
    if bias is not None:
        kw['bias'] = bias
    if scale is not None:
        kw['scale'] = scale
    if accum_out is not None:
        kw['accum_out'] = accum_out
    return k.S.op('act', lambda: k.nc.scalar.activation(out=out, in_=in_, func=func, **kw), reads, writes)


def mm(k, out, lhsT, rhs, start, stop, reads, writes):
    return k.S.op('pe', lambda: k.nc.tensor.matmul(out, lhsT, rhs, start=start, stop=stop), reads, writes)


def tt(k, eng, out, in0, in1, op, reads, writes):
    e = k.nc.vector if eng == 'dve' else k.nc.gpsimd
    return k.S.op(eng, lambda: e.tensor_tensor(out=out, in0=in0, in1=in1, op=op), reads, writes)


def ts(k, eng, out, in0, s1, s2, op0, op1, reads, writes):
    e = k.nc.vector if eng == 'dve' else k.nc.gpsimd
    if s2 is None:
        return k.S.op(eng, lambda: e.tensor_scalar(out=out, in0=in0, scalar1=s1, scalar2=None, op0=op0), reads, writes)
    return k.S.op(eng, lambda: e.tensor_scalar(out=out, in0=in0, scalar1=s1, scalar2=s2, op0=op0, op1=op1), reads, writes)


def stt(k, eng, out, in0, scalar, in1, op0, op1, reads, writes):
    e = k.nc.vector if eng == 'dve' else k.nc.gpsimd
    return k.S.op(eng, lambda: e.scalar_tensor_tensor(out=out, in0=in0, scalar=scalar, in1=in1, op0=op0, op1=op1), reads, writes)


def cp(k, eng, out, in_, reads, writes):
    if eng == 'act':
        return k.S.op('act', lambda: k.nc.scalar.copy(out=out, in_=in_), reads, writes)
    e = k.nc.vector if eng == 'dve' else k.nc.gpsimd
    return k.S.op(eng, lambda: e.tensor_copy(out=out, in_=in_), reads, writes)


def memset(k, eng, ap, val, writes):
    e = k.nc.vector if eng == 'dve' else k.nc.gpsimd
    return k.S.op(eng, lambda: e.memset(ap, val), (), writes)


def transpose(k, out, in_, ident, reads, writes):
    return k.S.op('pe', lambda: k.nc.tensor.transpose(out, in_, ident), reads, writes)


def modcol(k, l, seg, m, dc):
    return k.MOD[:, l, seg, m * 8 + dc:m * 8 + dc + 1]


def prologue(k):
    nc, S = k.nc, k.S
    with ExitStack() as es:
        S.dma('sp', k.colp[:], k.din['colp'], writes=['colp'])
        S.dma('sp', k.ident_f[:], k.din['c_ident'], writes=['ident_f'])
        S.dma('pool', k.ident_b[:], k.din['c_ident'], writes=['ident_b'])
        memset(k, 'dve', k.ones_b[:], 1.0 / 1024.0, ['ones_b'])
        memset(k, 'dve', k.ones_f[:], 1.0, ['ones_f'])
        cv = sb(k, es, 'cv', [128, 16], F32)
        sT = sb(k, es, 'sT', [128, 8, 2], BF16)
        S.dma('sp', cv[:], k.din['cvec'], writes=['cv'])
        act(k, sT[:, :, 0], cv[:, 0:8], AF.Silu, ['cv'], ['sT'])
        act(k, sT[:, :, 1], cv[:, 8:16], AF.Silu, ['cv'], ['sT'])
        wr = Ring(k, es, 'adw', 3, [128, 8, 512], BF16)
        pr = Ring(k, es, 'adp', 2, [128, 4, 2], F32, psum=True)
        for l in range(k.nlayers):
            wv = k.din['ada_w'][l].rearrange("(kc p) f -> p kc f", p=128)
            for fg in range(12):
                w, wk = wr.next()
                S.dma('pool', w[:], wv[:, :, fg * 512:(fg + 1) * 512], writes=[wk])
                p, pk = pr.next()
                for fc in range(4):
                    for kc in range(8):
                        mm(k, p[:, fc, :], w[:, kc, fc * 128:(fc + 1) * 128], sT[:, kc, :],
                           kc == 0, kc == 7, [wk, 'sT'], [pk])
                boff = k.cp.off[f'ada_b{l}'] + fg * 4
                for seg in range(2):
                    tt(k, 'dve', k.MOD[:, l, seg, fg * 4:(fg + 1) * 4], p[:, :, seg],
                       k.colp[:, boff:boff + 4], ALU.add, [pk, 'colp'], ['MOD'])
            for seg in range(2):
                for m in (1, 4):
                    ts(k, 'dve', k.MOD[:, l, seg, m * 8:(m + 1) * 8], k.MOD[:, l, seg, m * 8:(m + 1) * 8],
                       1.0, None, ALU.add, None, ['MOD'], ['MOD'])
        psn = sb(k, es, 'psn', [128, 128], F32)
        S.dma('sp', psn[:], k.din['c_psnake'], writes=['psn'])
        xr = Ring(k, es, 'xin', 2, [128, 4, 1024], F32)
        st = Ring(k, es, 'xst', 2, [128, 8, 512], F32)
        pp = Ring(k, es, 'xps', 4, [128, 512], F32, psum=True)
        hv = k.hT.rearrange("(dc p) t -> p dc t", p=128)
        for (t0, n) in BLOCKS:
            nt = n // 128
            xi, xk = xr.next()
            if t0 < SEQ:
                src = k.din['x'][t0:t0 + n, :]
                perm, permk = psn, 'psn'
            else:
                src = k.din['ctx']
                perm, permk = k.ident_f, 'ident_f'
            S.dma('sp', xi[:, 0:nt, :], src.rearrange("(a p) d -> p a d", p=128), writes=[xk])
            so, sk = st.next()
            for dc in range(8):
                p, pk = pp.next()
                for a in range(nt):
                    mm(k, p[:, a * 128:(a + 1) * 128], xi[:, a, dc * 128:(dc + 1) * 128], perm[:],
                       True, True, [xk, permk], [pk])
                cp(k, 'act' if dc % 2 else 'dve', so[:, dc, 0:n], p[:, 0:n], [pk], [sk])
            S.dma('sp', hv[:, :, t0:t0 + n], so[:, :, 0:n], reads=[sk], writes=[('hT', t0)])
        S.barrier()


def out_w(k, l):
    kind = k.kinds[l]
    j = l // 3
    if kind == 'ssd':
        return k.din['ssd_w_out'][j], 16
    if kind == 'hy':
        return k.din['hy_w_out'][j], 8
    return k.din['gla_w_out'][j if kind == 'gla' else 0], 8


def weight_prep(k):
    S = k.S
    k.wb = {}
    with ExitStack() as es:
        wr = Ring(k, es, 'wpq', 3, [128, 8, 512], BF16)
        for l in range(k.nlayers):
            Wo_, KC = out_w(k, l)
            Wo = Wo_.rearrange("(kc p) f -> p kc f", p=128)
            W1 = k.din['ffn_w1'][l].rearrange("(kc p) f -> p kc f", p=128)
            W2 = k.din['ffn_w2'][l].rearrange("(fc p) d -> p fc d", p=128)
            srcs = []
            for fh in range(2):
                for kh in range(KC // 8):
                    srcs.append(Wo[:, kh * 8:(kh + 1) * 8, fh * 512:(fh + 1) * 512])
            n_out = len(srcs)
            for pw in range(8):
                srcs.append(W1[:, :, pw * 512:(pw + 1) * 512])
            for dh in range(2):
                for g in range(4):
                    srcs.append(W2[:, g * 8:(g + 1) * 8, dh * 512:(dh + 1) * 512])
            dst = k.nc.dram_tensor(f'wb{l}', [len(srcs), 128, 8, 512], BF16).ap()
            for i, src in enumerate(srcs):
                w, wk = wr.next()
                S.dma('pool', w[:], src, writes=[wk])
                S.dma('sp', dst[i], w[:], reads=[wk])
            k.wb[l] = (dst, n_out)
        S.barrier()

def epilogue(k):
    nc, S = k.nc, k.S
    with ExitStack() as es:
        psn = sb(k, es, 'psn2', [128, 128], F32)
        S.dma('sp', psn[:], k.din['c_psnake'], writes=['psn'])
        hr = Ring(k, es, 'eh', 2, [128, 8, 512], F32)
        tr = Ring(k, es, 'et', 2, [128, 1024], F32)
        orr = Ring(k, es, 'eo', 2, [128, 4, 1024], F32)
        p1 = Ring(k, es, 'ep1', 4, [128, 512], F32, psum=True)
        p2 = Ring(k, es, 'ep2', 4, [128, 512], F32, psum=True)
        hv = k.hT.rearrange("(dc p) t -> p dc t", p=128)
        for (t0, n) in BLOCKS:
            if t0 >= SEQ:
                continue
            h, hk = hr.next()
            S.dma('sp', h[:], hv[:, :, t0:t0 + n], reads=[('hT', t0)], writes=[hk])
            o, ok = orr.next()
            for a in range(4):
                tm, tk = tr.next()
                for half in range(2):
                    p, pk = p1.next()
                    for q in range(4):
                        dc = half * 4 + q
                        mm(k, p[:, q * 128:(q + 1) * 128], h[:, dc, a * 128:(a + 1) * 128], k.ident_f[:],
                           True, True, [hk, 'ident_f'], [pk])
                    cp(k, 'act' if half else 'dve', tm[:, half * 512:(half + 1) * 512], p[:], [pk], [tk])
                for half in range(2):
                    p, pk = p2.next()
                    mm(k, p[:], psn[:], tm[:, half * 512:(half + 1) * 512], True, True, ['psn', tk], [pk])
                    cp(k, 'act' if half else 'dve', o[:, a, half * 512:(half + 1) * 512], p[:], [pk], [ok])
            S.dma('sp', k.dout[t0:t0 + n, :].rearrange("(a p) d -> p a d", p=128), o[:], reads=[ok], writes=[('out', t0)])
        S.barrier()


def build_uT(k, es, l):
    S = k.S
    uT = sb(k, es, 'uT', [128, 8, T], BF16)
    hr = Ring(k, es, 'uh', 2, [128, 8, 512], F32)
    hv = k.hT.rearrange("(dc p) t -> p dc t", p=128)
    for (t0, n) in BLOCKS:
        seg = seg_of(t0)
        h, hk = hr.next()
        S.dma('sp', h[:, :, 0:n], hv[:, :, t0:t0 + n], reads=[('hT', t0)], writes=[hk])
        for dc in range(8):
            act(k, uT[:, dc, t0:t0 + n], h[:, dc, 0:n], IDENT, [hk, 'MOD'], [('uT', t0)],
                bias=modcol(k, l, seg, 0, dc), scale=modcol(k, l, seg, 1, dc))
    return uT


def inproj(k, es, uT, W, specs):
    S = k.S
    Wv = W.rearrange("(kc p) f -> p kc f", p=128)
    wr = Ring(k, es, 'ipw', 3, [128, 8, 512], BF16)
    pr = Ring(k, es, 'ipp', 4, [128, 512], F32, psum=True)
    ukeys = [('uT', t0) for (t0, n) in BLOCKS]
    for (mode, f0, fsz, handler) in specs:
        w, wk = wr.next()
        S.dma('pool', w[:, :, 0:fsz], Wv[:, :, f0:f0 + fsz], writes=[wk])
        if mode == 'tm':
            for a in range(NTT):
                p, pk = pr.next()
                uk = ('uT', BLOCKS[min(a // 4, 8)][0])
                for kc in range(8):
                    mm(k, p[:, 0:fsz], uT[:, kc, a * 128:(a + 1) * 128], w[:, kc, 0:fsz], kc == 0, kc == 7,
                       [wk, uk], [pk])
                handler(p[:, 0:fsz], pk, a)
        else:
            nfc = (fsz + 127) // 128
            for fc in range(nfc):
                fw = min(128, fsz - fc * 128)
                for (t0, n) in BLOCKS:
                    p, pk = pr.next()
                    for kc in range(8):
                        mm(k, p[0:fw, 0:n], w[:, kc, fc * 128:fc * 128 + fw], uT[:, kc, t0:t0 + n], kc == 0, kc == 7,
                           [wk, ('uT', t0)], [pk])
                    handler(p[0:fw, 0:n], pk, f0 + fc * 128, fw, t0, n)


def ln_block(k, L, r, rk, n, gcol, bcol, out, outk):
    S = k.S
    rb, sq, pst, mean, msq, rstd = L['rb'], L['sq'], L['pst'], L['mean'], L['msq'], L['rstd']
    act(k, rb[:, :, 0:n], r[:, :, 0:n], IDENT, [rk], ['ln_rb'])
    act(k, sq[:, :, 0:n], r[:, :, 0:n], AF.Square, [rk], ['ln_sq'])
    p1, p1k = pst.next()
    p2, p2k = pst.next()
    for dc in range(8):
        mm(k, p1[:, 0:n], k.ones_b[:], rb[:, dc, 0:n], dc == 0, dc == 7, ['ones_b', 'ln_rb'], [p1k])
    for dc in range(8):
        mm(k, p2[:, 0:n], k.ones_b[:], sq[:, dc, 0:n], dc == 0, dc == 7, ['ones_b', 'ln_sq'], [p2k])
    cp(k, 'act', mean[:, 0:n], p1[:, 0:n], [p1k], ['ln_mean'])
    tt(k, 'dve', msq[:, 0:n], mean[:, 0:n], mean[:, 0:n], ALU.mult, ['ln_mean'], ['ln_msq'])
    tt(k, 'dve', msq[:, 0:n], p2[:, 0:n], msq[:, 0:n], ALU.subtract, [p2k, 'ln_msq'], ['ln_msq'])
    ts(k, 'dve', msq[:, 0:n], msq[:, 0:n], LN_EPS, None, ALU.add, None, ['ln_msq'], ['ln_msq'])
    act(k, msq[:, 0:n], msq[:, 0:n], AF.Ln, ['ln_msq'], ['ln_msq'])
    act(k, rstd[:, 0:n], msq[:, 0:n], AF.Exp, ['ln_msq'], ['ln_rstd'], scale=-0.5)
    mb = mean[:, 0:n].unsqueeze(1).to_broadcast([128, 8, n])
    rsb = rstd[:, 0:n].unsqueeze(1).to_broadcast([128, 8, n])
    tt(k, 'dve', r[:, :, 0:n], r[:, :, 0:n], mb, ALU.subtract, [rk, 'ln_mean'], [rk])
    tt(k, 'pool', r[:, :, 0:n], r[:, :, 0:n], rsb, ALU.mult, [rk, 'ln_rstd'], [rk])
    for dc in range(8):
        act(k, out[:, dc, 0:n], r[:, dc, 0:n], IDENT, [rk, 'colp'], [outk],
            bias=k.colp[:, bcol + dc:bcol + dc + 1], scale=k.colp[:, gcol + dc:gcol + dc + 1])


def post(k, l, W_out, KC):
    nc, S = k.nc, k.S
    with ExitStack() as es:
        hv = k.hT.rearrange("(dc p) t -> p dc t", p=128)
        yv = k.yT[0:KC * 128, :].rearrange("(kc p) t -> p kc t", p=128)
        wbl, n_out = k.wb[l]
        wr = Ring(k, es, 'pw', 4, [128, 8, 512], BF16)
        pr = Ring(k, es, 'pp', 6, [128, 512], F32, psum=True)
        L = dict(rb=sb(k, es, 'ln_rb', [128, 8, 512], BF16), sq=sb(k, es, 'ln_sq', [128, 8, 512], BF16),
                 pst=Ring(k, es, 'lnp', 2, [128, 512], F32, psum=True),
                 mean=sb(k, es, 'ln_mean', [128, 512], F32), msq=sb(k, es, 'ln_msq', [128, 512], F32),
                 rstd=sb(k, es, 'ln_rstd', [128, 512], F32))
        hb = sb(k, es, 'p_h', [128, 8, 512], F32)
        yb = sb(k, es, 'p_y', [128, KC, 512], BF16)
        r = sb(k, es, 'p_r', [128, 8, 512], F32)
        h1 = sb(k, es, 'p_h1', [128, 8, 512], F32)
        u2 = sb(k, es, 'p_u2', [128, 8, 512], BF16)
        hh = sb(k, es, 'p_hh', [128, 32, 512], BF16)
        rl = Ring(k, es, 'p_rl', 3, [128, 512], F32)
        g0, b0 = k.cp.off[f'ln_g{l}_0'], k.cp.off[f'ln_b{l}_0']
        g1, b1 = k.cp.off[f'ln_g{l}_1'], k.cp.off[f'ln_b{l}_1']
        for (t0, n) in BLOCKS:
            seg = seg_of(t0)
            S.dma('sp', hb[:, :, 0:n], hv[:, :, t0:t0 + n], reads=[('hT', t0)], writes=['p_h'])
            S.dma('sp', yb[:, :, 0:n], yv[:, :, t0:t0 + n], reads=[('yT', t0)], writes=['p_y'])
            S.op('act', lambda: nc.scalar.mul(out=hb[:, :, 0:n], in_=hb[:, :, 0:n], mul=ALPHA), ['p_h'], ['p_h'])
            for fh in range(2):
                pieces = []
                for kh in range(KC // 8):
                    w, wk = wr.next()
                    S.dma('sp', w[:], wbl[fh * (KC // 8) + kh], writes=[wk])
                    pieces.append((w, wk))
                for q in range(4):
                    dco = fh * 4 + q
                    p, pk = pr.next()
                    for kc in range(KC):
                        w, wk = pieces[kc // 8]
                        mm(k, p[:, 0:n], w[:, kc % 8, q * 128:(q + 1) * 128], yb[:, kc, 0:n], kc == 0, kc == KC - 1,
                           [wk, 'p_y'], [pk])
                    stt(k, 'dve', r[:, dco, 0:n], p[:, 0:n], modcol(k, l, seg, 2, dco), hb[:, dco, 0:n],
                        ALU.mult, ALU.add, [pk, 'MOD', 'p_h'], ['p_r'])
            ln_block(k, L, r, 'p_r', n, g0, b0, h1, 'p_h1')
            for dc in range(8):
                act(k, u2[:, dc, 0:n], h1[:, dc, 0:n], IDENT, ['p_h1', 'MOD'], ['p_u2'],
                    bias=modcol(k, l, seg, 3, dc), scale=modcol(k, l, seg, 4, dc))
            S.op('act', lambda: nc.scalar.mul(out=h1[:, :, 0:n], in_=h1[:, :, 0:n], mul=ALPHA), ['p_h1'], ['p_h1'])
            for pw in range(8):
                w, wk = wr.next()
                S.dma('sp', w[:], wbl[n_out + pw], writes=[wk])
                for q in range(4):
                    fc = pw * 4 + q
                    p, pk = pr.next()
                    for kc in range(8):
                        mm(k, p[:, 0:n], w[:, kc, q * 128:(q + 1) * 128], u2[:, kc, 0:n], kc == 0, kc == 7,
                           [wk, 'p_u2'], [pk])
                    rt, rtk = rl.next()
                    act(k, rt[:, 0:n], p[:, 0:n], AF.Relu, [pk], [rtk])
                    tt(k, 'pool' if q % 2 else 'dve', hh[:, fc, 0:n], rt[:, 0:n], rt[:, 0:n], ALU.mult, [rtk], [('p_hh', fc)])
            for dh in range(2):
                accs = [pr.next() for _ in range(4)]
                for g in range(4):
                    w, wk = wr.next()
                    S.dma('sp', w[:], wbl[n_out + 8 + dh * 4 + g], writes=[wk])
                    for q in range(4):
                        p, pk = accs[q]
                        for fcl in range(8):
                            fc = g * 8 + fcl
                            mm(k, p[:, 0:n], w[:, fcl, q * 128:(q + 1) * 128], hh[:, fc, 0:n],
                               fc == 0, fc == 31, [wk, ('p_hh', fc)], [pk])
                for q in range(4):
                    dco = dh * 4 + q
                    p, pk = accs[q]
                    stt(k, 'dve', r[:, dco, 0:n], p[:, 0:n], modcol(k, l, seg, 5, dco), h1[:, dco, 0:n],
                        ALU.mult, ALU.add, [pk, 'MOD', 'p_h1'], ['p_r'])
            ln_block(k, L, r, 'p_r', n, g1, b1, hb, 'p_h')
            S.dma('sp', hv[:, :, t0:t0 + n], hb[:, :, 0:n], reads=['p_h'], writes=[('hT', t0)])
        S.barrier()

RMS_EPS = 1e-6


def scr(k, name, shape, dt):
    if name not in k.dbg:
        k.dbg[name] = k.nc.dram_tensor('scr_' + name, list(shape), dt).ap()
    return k.dbg[name]


def rstd_from_ss(k, ss, nfeat, key):
    ts(k, 'dve', ss, ss, 1.0 / nfeat, RMS_EPS, ALU.mult, ALU.add, [key], [key])
    act(k, ss, ss, AF.Ln, [key], [key])
    act(k, ss, ss, AF.Exp, [key], [key], scale=-0.5)


def fm_store(k, st, dst, base, scale=None, eng='dve'):
    def h(p, pk, f_lo, fw, t0, n):
        s, sk = st.next()
        if scale is not None:
            act(k, s[0:fw, 0:n], p, IDENT, [pk], [sk], scale=scale)
        else:
            cp(k, eng, s[0:fw, 0:n], p, [pk], [sk])
        k.S.dma('sp', dst[f_lo - base:f_lo - base + fw, t0:t0 + n], s[0:fw, 0:n], reads=[sk])
    return h


def tm_store(k, st, dst, c0, func=None):
    cnt = [0]

    def h(p, pk, a):
        s, sk = st.next()
        w = p.shape[1]
        if func is not None:
            act(k, s[:, 0:w], p, func, [pk], [sk])
        else:
            cnt[0] += 1
            cp(k, 'dve' if cnt[0] % 2 else 'act', s[:, 0:w], p, [pk], [sk])
        k.S.dma('sp', dst[a * 128:(a + 1) * 128, c0:c0 + w], s[:, 0:w], reads=[sk])
    return h


def transpose_out(k, tp, yb, ybk, ytile_ring, nq, yT_view, c):
    for q0 in range(0, nq, 8):
        for q in range(8):
            transpose(k, tp[:, q, :], yb[:, (q0 + q) * 128:(q0 + q + 1) * 128], k.ident_b[:], [ybk, 'ident_b'], ['tp'])
        yt, ytk = ytile_ring.next()
        cp(k, 'act', yt[:], tp[:], ['tp'], [ytk])
        k.S.dma('sp', yT_view[:, q0:q0 + 8, c * 128:(c + 1) * 128], yt[:], reads=[ytk])


def gla_mixer(k, l, j):
    nc, S = k.nc, k.S
    qT = scr(k, 'qT', [512, T], BF16)
    kT = scr(k, 'kT', [512, T], BF16)
    aT = scr(k, 'aT', [32, T], BF16)
    ktm = scr(k, 'ktm', [T, 512], BF16)
    vtm = scr(k, 'vtm', [T, 1024], BF16)
    gtm = scr(k, 'gtm', [T, 1024], BF16)
    oacc = scr(k, 'oacc', [T, 2048], F32)
    W = k.din['gla_w_in'][j]
    with ExitStack() as es:
        uT = build_uT(k, es, l)
        st = Ring(k, es, 'gst', 4, [128, 512], BF16)
        specs = [
            ('fm', 0, 512, fm_store(k, st, qT, 0, scale=128.0 ** -0.5)),
            ('fm', 512, 512, fm_store(k, st, kT, 512)),
            ('fm', 3072, 32, fm_store(k, st, aT, 3072)),
            ('tm', 512, 512, tm_store(k, st, ktm, 0)),
            ('tm', 1024, 512, tm_store(k, st, vtm, 0)),
            ('tm', 1536, 512, tm_store(k, st, vtm, 512)),
            ('tm', 2048, 512, tm_store(k, st, gtm, 0, AF.Silu)),
            ('tm', 2560, 512, tm_store(k, st, gtm, 512, AF.Silu)),
        ]
        inproj(k, es, uT, W, specs)
        S.barrier()
    with ExitStack() as es:
        ctri = sb(k, es, 'gtri', [128, 2, 128], F32)
        crev = sb(k, es, 'grev', [128, 2, 128], F32)
        cmask = sb(k, es, 'gmask', [128, 2, 128], F32)
        gnorm = sb(k, es, 'gnorm', [128, 256], F32)
        w2a = sb(k, es, 'w2a', [33, 2, 512], BF16)
        for d in range(2):
            S.dma('sp', ctri[:, d, :], k.din['c_tri'][d], writes=['gtri'])
            S.dma('sp', crev[:, d, :], k.din['c_rev'][d], writes=['grev'])
            S.dma('sp', cmask[:, d, :], k.din['c_mask'][d], writes=['gmask'])
        S.op('act', lambda: nc.scalar.mul(out=ctri[:], in_=ctri[:], mul=-1.0 / 16.0), ['gtri'], ['gtri'])
        S.op('act', lambda: nc.scalar.mul(out=crev[:], in_=crev[:], mul=-1.0 / 16.0), ['grev'], ['grev'])
        S.dma('sp', gnorm[:], k.din['gla_norm'][j].partition_broadcast(128), writes=['gnorm'])
        memset(k, 'dve', w2a[:], 0.0, ['w2a'])
        for z in range(2):
            S.dma('pool', w2a[z * 16:(z + 1) * 16, z, :], k.din['gla_w_a2'][j, z], writes=['w2a'])
            S.dma('pool', w2a[32:33, z, :], k.din['gla_b_a2'][j, z:z + 1, :], writes=['w2a'])
        aR = Ring(k, es, 'g_a', 2, [33, 128], BF16)
        for t_, tk_ in [aR.next(), aR.next()]:
            memset(k, 'dve', t_[:], 1.0, [tk_])
        qR = Ring(k, es, 'g_q', 2, [128, 4, 128], BF16)
        kR = Ring(k, es, 'g_k', 2, [128, 4, 128], BF16)
        kmR = Ring(k, es, 'g_km', 2, [128, 512], BF16)
        vR = Ring(k, es, 'g_v', 2, [128, 1024], BF16)
        gR = Ring(k, es, 'g_g', 2, [128, 1024], BF16)
        oaR = Ring(k, es, 'g_oa', 2, [128, 1024], F32)
        e_sb = sb(k, es, 'g_e', [128, 512], F32)
        sp_sb = sb(k, es, 'g_sp', [128, 512], F32)
        Eq = sb(k, es, 'g_Eq', [128, 512], F32)
        Ek = sb(k, es, 'g_Ek', [128, 512], F32)
        Er = sb(k, es, 'g_Er', [128, 512], F32)
        qt = sb(k, es, 'g_qt', [128, 512], BF16)
        kt = sb(k, es, 'g_kt', [128, 512], BF16)
        kp = sb(k, es, 'g_kp', [128, 512], BF16)
        AT = sb(k, es, 'g_AT', [128, 4, 128], BF16)
        Sst = sb(k, es, 'g_S', [128, 4, 256], F32)
        Sbf = sb(k, es, 'g_Sb', [128, 4, 256], BF16)
        osR = Ring(k, es, 'g_os', 2, [128, 4, 256], F32)
        ss = sb(k, es, 'g_ss', [128, 4], F32)
        junk = sb(k, es, 'g_junk', [128, 256], F32)
        yb = sb(k, es, 'g_yb', [128, 1024], BF16)
        ytR = Ring(k, es, 'g_yt', 2, [128, 8, 128], BF16)
        lg_ps = ps(k, es, 'g_lg', [128, 512])
        cum_ps = ps(k, es, 'g_cum', [128, 512])
        sc_ps = ps(k, es, 'g_sc', [128, 512])
        o_ps = ps(k, es, 'g_o', [128, 1024])
        st_ps = ps(k, es, 'g_stp', [128, 1024])
        tp = ps(k, es, 'g_tp', [128, 8, 128], BF16)
        qv = qT.rearrange("(h p) t -> p h t", p=128)
        kv = kT.rearrange("(h p) t -> p h t", p=128)
        yv = k.yT[0:1024, :].rearrange("(dc p) t -> p dc t", p=128)
        for d in range(2):
            memset(k, 'dve', Sst[:], 0.0, ['g_S'])
            memset(k, 'dve', Sbf[:], 0.0, ['g_Sb'])
            order = [32, 33] + list(range(32)) if d == 0 else [33, 32] + list(range(31, -1, -1))
            for c in order:
                tsl = slice(c * 128, (c + 1) * 128)
                a_t, ak = aR.next()
                S.dma('sp', a_t[0:32, :], aT[:, tsl], writes=[ak])
                q_t, qk = qR.next()
                S.dma('sp', q_t[:], qv[:, :, tsl], writes=[qk])
                k_t, kk = kR.next()
                S.dma('sp', k_t[:], kv[:, :, tsl], writes=[kk])
                km, kmk = kmR.next()
                S.dma('sp', km[:], ktm[tsl, :], writes=[kmk])
                v_t, vk = vR.next()
                S.dma('sp', v_t[:], vtm[tsl, :], writes=[vk])
                if d == 1:
                    g_t, gk = gR.next()
                    S.dma('sp', g_t[:], gtm[tsl, :], writes=[gk])
                    oa, oak = oaR.next()
                    S.dma('sp', oa[:], oacc[tsl, 0:1024], writes=[oak])
                mm(k, lg_ps[:], a_t[:], w2a[:, d, :], True, True, [ak, 'w2a'], ['g_lg'])
                act(k, e_sb[:], lg_ps[:], AF.Exp, ['g_lg'], ['g_e'], scale=-1.0)
                act(k, sp_sb[:], e_sb[:], AF.Ln, ['g_e'], ['g_sp'], bias=1.0)
                for h in range(4):
                    mm(k, cum_ps[:, h * 128:(h + 1) * 128], sp_sb[:, h * 128:(h + 1) * 128], ctri[:, d, :],
                       True, True, ['g_sp', 'gtri'], ['g_cum'])
                mm(k, lg_ps[:], crev[:, d, :], sp_sb[:], True, True, ['g_sp', 'grev'], ['g_lg'])
                act(k, Eq[:], cum_ps[:], AF.Exp, ['g_cum'], ['g_Eq'])
                act(k, Ek[:], cum_ps[:], AF.Exp, ['g_cum'], ['g_Ek'], scale=-1.0)
                act(k, Er[:], lg_ps[:], AF.Exp, ['g_lg'], ['g_Er'])
                tt(k, 'dve', qt[:], q_t[:].rearrange("p h t -> p (h t)"), Eq[:], ALU.mult, [qk, 'g_Eq'], ['g_qt'])
                tt(k, 'pool', kt[:], k_t[:].rearrange("p h t -> p (h t)"), Ek[:], ALU.mult, [kk, 'g_Ek'], ['g_kt'])
                tt(k, 'dve', kp[:], km[:], Er[:], ALU.mult, [kmk, 'g_Er'], ['g_kp'])
                for h in range(4):
                    mm(k, sc_ps[:, h * 128:(h + 1) * 128], kt[:, h * 128:(h + 1) * 128], qt[:, h * 128:(h + 1) * 128],
                       True, True, ['g_kt', 'g_qt'], ['g_sc'])
                tt(k, 'dve', AT[:], sc_ps[:].rearrange("p (h t) -> p h t", h=4),
                   cmask[:, d, :].unsqueeze(1).to_broadcast([128, 4, 128]), ALU.mult, ['g_sc', 'gmask'], ['g_AT'])
                for h in range(4):
                    mm(k, o_ps[:, h * 256:(h + 1) * 256], AT[:, h, :], v_t[:, h * 256:(h + 1) * 256], True, False,
                       ['g_AT', vk], ['g_o'])
                    mm(k, o_ps[:, h * 256:(h + 1) * 256], qt[:, h * 128:(h + 1) * 128], Sbf[:, h, :], False, True,
                       ['g_qt', 'g_Sb'], ['g_o'])
                for h in range(4):
                    mm(k, st_ps[:, h * 256:(h + 1) * 256], kp[:, h * 128:(h + 1) * 128], v_t[:, h * 256:(h + 1) * 256],
                       True, True, ['g_kp', vk], ['g_stp'])
                last = 127 if d == 0 else 0
                for h in range(4):
                    col = h * 128 + last
                    stt(k, 'dve', Sst[:, h, :], Sst[:, h, :], Eq[:, col:col + 1], st_ps[:, h * 256:(h + 1) * 256],
                        ALU.mult, ALU.add, ['g_S', 'g_Eq', 'g_stp'], ['g_S'])
                cp(k, 'act', Sbf[:], Sst[:], ['g_S'], ['g_Sb'])
                os_, osk = osR.next()
                if d == 0:
                    cp(k, 'act', os_[:].rearrange("p h v -> p (h v)"), o_ps[:], ['g_o'], [osk])
                    S.dma('sp', oacc[tsl, 0:1024], os_[:].rearrange("p h v -> p (h v)"), reads=[osk])
                else:
                    tt(k, 'dve', os_[:].rearrange("p h v -> p (h v)"), o_ps[:], oa[:], ALU.add, ['g_o', oak], [osk])
                    memset(k, 'dve', ss[:], 0.0, ['g_ss'])
                    for h in range(4):
                        act(k, junk[:], os_[:, h, :], AF.Square, [osk], ['g_junk', 'g_ss'], accum_out=ss[:, h:h + 1])
                    rstd_from_ss(k, ss[:], 256.0, 'g_ss')
                    tt(k, 'dve', os_[:], os_[:], ss[:].unsqueeze(2).to_broadcast([128, 4, 256]), ALU.mult,
                       [osk, 'g_ss'], [osk])
                    tt(k, 'pool', os_[:], os_[:], gnorm[:].unsqueeze(1).to_broadcast([128, 4, 256]), ALU.mult,
                       [osk, 'gnorm'], [osk])
                    tt(k, 'dve', yb[:], os_[:].rearrange("p h v -> p (h v)"), g_t[:], ALU.mult, [osk, gk], ['g_yb'])
                    transpose_out(k, tp, yb, 'g_yb', ytR, 8, yv, c)
            S.barrier()


def conv_fm(k, es, src, dst, nchunks, ntap, wname, bname, func, c_off=0):
    S = k.S
    pad = ntap // 2
    xr = Ring(k, es, 'cv_x', 2, [128, SEQ + 2 * pad], BF16)
    dg = Ring(k, es, 'cv_d', 2, [128, ntap, 128], BF16)
    sr = Ring(k, es, 'cv_s', 3, [128, 512], BF16)
    pr = Ring(k, es, 'cv_p', 3, [128, 512], F32, psum=True)
    boff = k.cp.off[bname]
    for cc in range(nchunks):
        d, dk = dg.next()
        for tap in range(ntap):
            woff = k.cp.off[f'{wname}{tap}'] + c_off + cc
            ts(k, 'dve', d[:, tap, :], k.ident_f[:], k.colp[:, woff:woff + 1], None, ALU.mult, None,
               ['ident_f', 'colp'], [dk])
        for (s0, slen) in ((0, SEQ), (SEQ, CTXL)):
            x, xk = xr.next()
            memset(k, 'dve', x[:, 0:pad], 0.0, [xk])
            memset(k, 'dve', x[:, pad + slen:2 * pad + slen], 0.0, [xk])
            S.dma('sp', x[:, pad:pad + slen], src[cc * 128:(cc + 1) * 128, s0:s0 + slen], writes=[xk])
            for t0 in range(0, slen, 512):
                n = min(512, slen - t0)
                p, pk = pr.next()
                for tap in range(ntap):
                    mm(k, p[:, 0:n], d[:, tap, :], x[:, t0 + tap:t0 + tap + n], tap == 0, tap == ntap - 1, [dk, xk], [pk])
                s, sk = sr.next()
                act(k, s[:, 0:n], p[:, 0:n], func, [pk, 'colp'], [sk],
                    bias=k.colp[:, boff + c_off + cc:boff + c_off + cc + 1])
                S.dma('sp', dst[cc * 128:(cc + 1) * 128, s0 + t0:s0 + t0 + n], s[:, 0:n], reads=[sk])


def fm_to_tm(k, es, src, c0, nch, dst, d0):
    S = k.S
    ir = Ring(k, es, 'f2t_i', 2, [128, 8, 128], BF16)
    orr = Ring(k, es, 'f2t_o', 2, [128, 8, 128], BF16)
    pr = Ring(k, es, 'f2t_p', 2, [128, 8, 128], BF16, psum=True)
    sv = src.rearrange("(c p) t -> p c t", p=128)
    for a in range(NTT):
        for q0 in range(0, nch, 8):
            i_, ik = ir.next()
            S.dma('sp', i_[:], sv[:, c0 + q0:c0 + q0 + 8, a * 128:(a + 1) * 128], writes=[ik])
            p, pk = pr.next()
            for q in range(8):
                transpose(k, p[:, q, :], i_[:, q, :], k.ident_b[:], [ik, 'ident_b'], [pk])
            o, ok = orr.next()
            cp(k, 'act' if (q0 // 8) % 2 else 'dve', o[:], p[:], [pk], [ok])
            S.dma('sp', dst[a * 128:(a + 1) * 128, d0 + q0 * 128:d0 + (q0 + 8) * 128],
                  o[:].rearrange("p q t -> p (q t)"), reads=[ok])


def ssd_mixer(k, l, j):
    nc, S = k.nc, k.S
    ztm = scr(k, 'ztm', [T, 2048], BF16)
    xbcT = scr(k, 'xbcT', [4096, T], BF16)
    cvT = scr(k, 'cvT', [4096, T], BF16)
    xbtm = scr(k, 'xbtm', [T, 3072], BF16)
    dtm = scr(k, 'dtm', [T, 64], F32)
    dtam = scr(k, 'dtam', [T, 64], F32)
    yacc = scr(k, 'oacc', [T, 2048], F32)
    W = k.din['ssd_w_in'][j]
    with ExitStack() as es:
        uT = build_uT(k, es, l)
        st = Ring(k, es, 'sst', 4, [128, 512], BF16)
        dtb = sb(k, es, 's_dtb', [128, 64], F32)
        abc = sb(k, es, 's_abc', [128, 64], F32)
        S.dma('sp', dtb[:], k.din['ssd_dt_bias'][j].rearrange("a h -> (a h)").partition_broadcast(128), writes=['s_dtb'])
        S.dma('sp', abc[:], k.din['ssd_a_log'][j].rearrange("a h -> (a h)").partition_broadcast(128), writes=['s_abc'])
        act(k, abc[:], abc[:], AF.Exp, ['s_abc'], ['s_abc'])
        ts(k, 'dve', abc[:], abc[:], -1.0, None, ALU.mult, None, ['s_abc'], ['s_abc'])
        dr = Ring(k, es, 's_dt', 2, [128, 2, 64], F32)

        def h_dt(p, pk, a):
            d, dk = dr.next()
            tt(k, 'dve', d[:, 0, :], p, dtb[:], ALU.add, [pk, 's_dtb'], [dk])
            act(k, d[:, 0, :], d[:, 0, :], AF.Exp, [dk], [dk])
            act(k, d[:, 0, :], d[:, 0, :], AF.Ln, [dk], [dk], bias=1.0)
            tt(k, 'dve', d[:, 1, :], d[:, 0, :], abc[:], ALU.mult, [dk, 's_abc'], [dk])
            S.dma('sp', dtm[a * 128:(a + 1) * 128, :], d[:, 0, :], reads=[dk])
            S.dma('sp', dtam[a * 128:(a + 1) * 128, :], d[:, 1, :], reads=[dk])
        specs = [('tm', i * 512, 512, tm_store(k, st, ztm, i * 512, AF.Silu)) for i in range(4)]
        specs += [('fm', 2048 + i * 512, 512, fm_store(k, st, xbcT, 2048, eng=('dve' if i % 2 else 'act'))) for i in range(8)]
        specs += [('tm', 6144, 64, h_dt)]
        inproj(k, es, uT, W, specs)
        S.barrier()
    with ExitStack() as es:
        conv_fm(k, es, xbcT, cvT, 32, 5, 'ssd_conv_w', 'ssd_conv_b', AF.Silu)
        S.barrier()
    with ExitStack() as es:
        fm_to_tm(k, es, cvT, 0, 24, xbtm, 0)
        S.barrier()
    with ExitStack() as es:
        stri = sb(k, es, 'stri', [128, 2, 128], F32)
        srev = sb(k, es, 'srev', [128, 2, 128], F32)
        mbias = sb(k, es, 'smb', [128, 2, 128], F32)
        selh = sb(k, es, 'selh', [32, 32 * 128], F32)
        for d in range(2):
            S.dma('sp', stri[:, d, :], k.din['c_tri'][d], writes=['stri'])
            S.dma('sp', srev[:, d, :], k.din['c_rev'][d], writes=['srev'])
            S.dma('sp', mbias[:, d, :], k.din['c_mbias'][d], writes=['smb'])
        S.dma('sp', selh[:], k.din['c_selh'], writes=['selh'])
        dsk = sb(k, es, 's_dsk', [128, 32], F32)
        gn = sb(k, es, 's_gn', [128, 2048], F32)
        S.dma('sp', dsk[:], k.din['ssd_d'][j].partition_broadcast(128), writes=['s_dsk'])
        S.dma('sp', gn[:], k.din['ssd_norm'][j].partition_broadcast(128), writes=['s_gn'])
        BR = Ring(k, es, 's_B', 2, [128, 8, 128], BF16)
        CR = Ring(k, es, 's_C', 2, [128, 8, 128], BF16)
        XR = Ring(k, es, 's_X', 2, [128, 3072], BF16)
        dR = Ring(k, es, 's_d', 2, [128, 64], F32)
        daR = Ring(k, es, 's_da', 2, [128, 64], F32)
        zR = Ring(k, es, 's_z', 2, [128, 2048], BF16)
        yaR = Ring(k, es, 's_ya', 2, [128, 2048], F32)
        exps = sb(k, es, 's_ex', [128, 96], F32)
        negc = sb(k, es, 's_nc', [128, 32], F32)
        cumT = sb(k, es, 's_cT', [32, 128], F32)
        dtw = sb(k, es, 's_dtw', [128, 32], F32)
        xdt = sb(k, es, 's_xdt', [128, 2048], BF16)
        xdtw = sb(k, es, 's_xdtw', [128, 2048], BF16)
        DsR = Ring(k, es, 's_Ds', 2, [128, 4, 128], F32)
        LR = Ring(k, es, 's_L', 2, [128, 4, 128], BF16)
        ysb = sb(k, es, 's_y', [128, 2048], F32)
        Sst = sb(k, es, 's_S', [128, 8, 256], F32)
        Sbf = sb(k, es, 's_Sb', [128, 8, 256], BF16)
        ss = sb(k, es, 's_ss', [128, 1], F32)
        junk = sb(k, es, 's_junk', [128, 2048], F32)
        yb = sb(k, es, 's_yb', [128, 2048], BF16)
        ytR = Ring(k, es, 's_yt', 2, [128, 8, 128], BF16)
        sm_ps = ps(k, es, 's_sm', [128, 512])
        cb_ps = ps(k, es, 's_cb', [128, 512])
        DpR = Ring(k, es, 's_Dp', 2, [128, 4, 128], F32, psum=True)
        YpR = Ring(k, es, 's_Yp', 2, [128, 512], F32, psum=True)
        st_ps = ps(k, es, 's_stp', [128, 512])
        tp = ps(k, es, 's_tp', [128, 8, 128], BF16)
        Bv = cvT[2048:3072, :].rearrange("(g p) t -> p g t", p=128)
        Cv = cvT[3072:4096, :].rearrange("(g p) t -> p g t", p=128)
        yv = k.yT[0:2048, :].rearrange("(dc p) t -> p dc t", p=128)
        for d in range(2):
            dsl = slice(d * 32, (d + 1) * 32)
            memset(k, 'dve', Sst[:], 0.0, ['s_S'])
            memset(k, 'dve', Sbf[:], 0.0, ['s_Sb'])
            order = [32, 33] + list(range(32)) if d == 0 else [33, 32] + list(range(31, -1, -1))
            for c in order:
                tsl = slice(c * 128, (c + 1) * 128)
                B_, Bk = BR.next()
                S.dma('sp', B_[:], Bv[:, :, tsl], writes=[Bk])
                C_, Ck = CR.next()
                S.dma('sp', C_[:], Cv[:, :, tsl], writes=[Ck])
                X_, Xk = XR.next()
                S.dma('sp', X_[:], xbtm[tsl, :], writes=[Xk])
                dt_, dtk = dR.next()
                S.dma('sp', dt_[:], dtm[tsl, :], writes=[dtk])
                da_, dak = daR.next()
                S.dma('sp', da_[:], dtam[tsl, :], writes=[dak])
                if d == 1:
                    z_, zk = zR.next()
                    S.dma('sp', z_[:], ztm[tsl, :], writes=[zk])
                    ya, yak = yaR.next()
                    S.dma('sp', ya[:], yacc[tsl, :], writes=[yak])
                mm(k, sm_ps[:, 0:32], stri[:, d, :], da_[:, dsl], True, True, ['stri', dak], ['s_sm'])
                mm(k, sm_ps[:, 32:64], srev[:, d, :], da_[:, dsl], True, True, ['srev', dak], ['s_sm'])
                mm(k, sm_ps[:, 64:96], k.ones_f[:], da_[:, dsl], True, True, ['ones_f', dak], ['s_sm'])
                mm(k, sm_ps[0:32, 128:256], da_[:, dsl], stri[:, d, :], True, True, ['stri', dak], ['s_sm'])
                act(k, exps[:], sm_ps[:, 0:96], AF.Exp, ['s_sm'], ['s_ex'])
                act(k, negc[:], sm_ps[:, 0:32], IDENT, ['s_sm'], ['s_nc'], scale=-1.0)
                cp(k, 'act', cumT[:], sm_ps[0:32, 128:256], ['s_sm'], ['s_cT'])
                tt(k, 'dve', dtw[:], dt_[:, dsl], exps[:, 32:64], ALU.mult, [dtk, 's_ex'], ['s_dtw'])
                xs3 = X_[:, 0:2048].rearrange("p (h q) -> p h q", q=64)
                tt(k, 'dve', xdt[:].rearrange("p (h q) -> p h q", q=64), xs3,
                   dt_[:, dsl].unsqueeze(2).to_broadcast([128, 32, 64]), ALU.mult, [Xk, dtk], ['s_xdt'])
                tt(k, 'dve', xdtw[:].rearrange("p (h q) -> p h q", q=64), xs3,
                   dtw[:].unsqueeze(2).to_broadcast([128, 32, 64]), ALU.mult, [Xk, 's_dtw'], ['s_xdtw'])
                for g in range(8):
                    mm(k, cb_ps[:, 0:128], B_[:, g, :], C_[:, g, :], True, True, [Bk, Ck], ['s_cb'])
                    Dp, Dpk = DpR.next()
                    for r in range(4):
                        h = g * 4 + r
                        mm(k, Dp[:, r, :], selh[:, h * 128:(h + 1) * 128], cumT[:], True, False, ['selh', 's_cT'], [Dpk])
                        mm(k, Dp[:, r, :], k.ident_f[:], mbias[:, d, :], False, True, ['ident_f', 'smb'], [Dpk])
                    Ds, Dsk = DsR.next()
                    for r in range(4):
                        h = g * 4 + r
                        act(k, Ds[:, r, :], Dp[:, r, :], AF.Exp, [Dpk, 's_nc'], [Dsk], bias=negc[:, h:h + 1])
                    L_, Lk = LR.next()
                    tt(k, 'dve', L_[:], Ds[:], cb_ps[:, 0:128].unsqueeze(1).to_broadcast([128, 4, 128]), ALU.mult,
                       [Dsk, 's_cb'], [Lk])
                    Yp, Ypk = YpR.next()
                    for r in range(4):
                        h = g * 4 + r
                        mm(k, Yp[:, r * 64:(r + 1) * 64], L_[:, r, :], xdt[:, h * 64:(h + 1) * 64], True, True,
                           [Lk, 's_xdt'], [Ypk])
                    mm(k, Yp[:, 256:512], C_[:, g, :], Sbf[:, g, :], True, True, [Ck, ('s_Sb', g)], [Ypk])
                    mm(k, st_ps[:, 0:256], X_[:, 2048 + g * 128:2048 + (g + 1) * 128], xdtw[:, g * 256:(g + 1) * 256],
                       True, True, [Xk, 's_xdtw'], ['s_stp'])
                    yg = ysb[:, g * 256:(g + 1) * 256]
                    tt(k, 'dve', yg.rearrange("p (r q) -> p r q", q=64), Yp[:, 256:512].rearrange("p (r q) -> p r q", q=64),
                       exps[:, g * 4:(g + 1) * 4].unsqueeze(2).to_broadcast([128, 4, 64]), ALU.mult,
                       [Ypk, 's_ex'], [('s_y', g)])
                    tt(k, 'dve', yg, yg, Yp[:, 0:256], ALU.add, [Ypk, ('s_y', g)], [('s_y', g)])
                    sg = Sst[:, g, :]
                    tt(k, 'dve', sg.rearrange("p (r q) -> p r q", q=64), sg.rearrange("p (r q) -> p r q", q=64),
                       exps[:, 64 + g * 4:64 + (g + 1) * 4].unsqueeze(2).to_broadcast([128, 4, 64]), ALU.mult,
                       [('s_S', g), 's_S', 's_ex'], [('s_S', g)])
                    tt(k, 'dve', sg, sg, st_ps[:, 0:256], ALU.add, [('s_S', g), 's_stp'], [('s_S', g)])
                    cp(k, 'act', Sbf[:, g, :], sg, [('s_S', g)], [('s_Sb', g)])
                ykeys = [('s_y', g) for g in range(8)]
                if d == 0:
                    S.dma('sp', yacc[tsl, :], ysb[:], reads=ykeys)
                else:
                    tt(k, 'dve', ysb[:], ysb[:], ya[:], ALU.add, ykeys + [yak], ['s_yf'])
                    tt(k, 'dve', junk[:].rearrange("p (h q) -> p h q", q=64), xs3,
                       dsk[:].unsqueeze(2).to_broadcast([128, 32, 64]), ALU.mult, [Xk, 's_dsk'], ['s_junk'])
                    tt(k, 'dve', ysb[:], ysb[:], junk[:], ALU.add, ['s_yf', 's_junk'] + ykeys, ['s_yf'] + ykeys)
                    tt(k, 'dve', ysb[:], ysb[:], z_[:], ALU.mult, ['s_yf', zk] + ykeys, ['s_yf'] + ykeys)
                    memset(k, 'dve', ss[:], 0.0, ['s_ss'])
                    act(k, junk[:], ysb[:], AF.Square, ['s_yf'] + ykeys, ['s_junk', 's_ss'], accum_out=ss[:])
                    rstd_from_ss(k, ss[:], 2048.0, 's_ss')
                    ts(k, 'dve', ysb[:], ysb[:], ss[:, 0:1], None, ALU.mult, None, ['s_yf', 's_ss'] + ykeys, ['s_yf'] + ykeys)
                    tt(k, 'dve', yb[:], ysb[:], gn[:], ALU.mult, ['s_yf', 's_gn'] + ykeys, ['s_yb'])
                    transpose_out(k, tp, yb, 's_yb', ytR, 16, yv, c)
            S.barrier()

HY_SEGS = {0: dict(s0=0, L=SEQ), 1: dict(s0=SEQ, L=CTXL)}
for _s in HY_SEGS.values():
    _s['N'] = 2 * _s['L']
    _s['N1'] = _s['N'] // 128
    _s['M1'] = _s['N1'] // 2
    _s['K1'] = _s['N1'] // 2 + 1
    _s['R'] = 2 * _s['K1']


def hyena_constants():
    c = {}
    a = np.arange(128)
    th = 2 * np.pi * ((a[:, None] * a[None, :]) % 128) / 128.0
    c['c_cs'] = np.cos(th).astype(np.float32)
    c['c_sn'] = np.sin(th).astype(np.float32)
    c['c_deltas'] = np.abs(np.linspace(math.log(1e-2) / 0.3, math.log(1e-2) / 1.5, 1024)).astype(np.float32)
    for s, g in HY_SEGS.items():
        L, N, N1, M1, K1, R = g['L'], g['N'], g['N1'], g['M1'], g['K1'], g['R']
        m = (128 * np.arange(M1)[:, None] + np.arange(128)[None, :]).astype(np.int64)
        k1 = np.arange(K1, dtype=np.int64)
        th = 2 * np.pi * ((m[:, :, None] * k1[None, None, :]) % N) / float(N)
        ef = np.zeros((M1, 128, R), np.float64)
        ef[:, :, 0::2] = np.cos(th)
        ef[:, :, 1::2] = -np.sin(th)
        c[f'c_efwd{s}'] = ef.astype(np.float32)
        w = np.full(K1, 2.0)
        w[0] = 1.0
        w[-1] = 1.0
        ei = np.zeros((R, 128, M1), np.float64)
        ei[0::2] = (np.cos(th) * w[None, None, :] / N).transpose(2, 1, 0)
        ei[1::2] = (-np.sin(th) * w[None, None, :] / N).transpose(2, 1, 0)
        c[f'c_einv{s}'] = ei.astype(np.float32)
        t = np.linspace(0.0, 1.0, L, dtype=np.float32)[:, None]
        wv = (2 * math.pi * np.arange(L, dtype=np.float32)[:, None] / L).astype(np.float32)
        ang = np.linspace(1e-4, 15, 16, dtype=np.float32)[None, :] * wv
        emb = np.concatenate([t, np.cos(ang), -np.sin(ang)], axis=-1).astype(np.float32)
        c[f'c_emb{s}'] = np.ascontiguousarray(emb.T)
        c[f'c_tcol{s}'] = np.ascontiguousarray(t[:, 0][m.reshape(-1)].reshape(M1, 128))
    return c


def hyena_host_layout(inputs, m):
    cols = [inputs['hy_f_b1'][0], inputs['hy_f_b2'][0], inputs['hy_f_b3'][0], inputs['hy_f_freq'][0]]
    m['hycol'] = np.ascontiguousarray(np.stack([np.asarray(c, np.float32) for c in cols], axis=1))


def hy_common(k, es):
    S = k.S
    H = {}
    for nm, src, neg in (('cs', 'c_cs', False), ('sn', 'c_sn', False), ('ncs', 'c_cs', True), ('nsn', 'c_sn', True)):
        t = sb(k, es, 'hy_' + nm, [128, 128], BF16)
        tf = sb(k, es, 'hyf_' + nm, [128, 128], F32)
        S.dma('sp', tf[:], k.din[src], writes=['hyf_' + nm])
        S.op('act', lambda: k.nc.scalar.mul(out=t[:], in_=tf[:], mul=(-1.0 if neg else 1.0)), ['hyf_' + nm], ['hy_' + nm])
        H[nm] = t
    return H


def load_E(k, es, s):
    g = HY_SEGS[s]
    ef = sb(k, es, f'hy_ef{s}', [g['M1'], 128, g['R']], BF16)
    ei = sb(k, es, f'hy_ei{s}', [g['R'], 128, g['M1']], BF16)
    k.S.dma('pool', ef[:], k.din[f'c_efwd{s}'], writes=['hy_ef'])
    k.S.dma('pool', ei[:], k.din[f'c_einv{s}'], writes=['hy_ei'])
    return ef, ei


def hy_filters(k, j, s, Hspec, BdF):
    nc, S = k.nc, k.S
    g = HY_SEGS[s]
    L, M1, K1, R = g['L'], g['M1'], g['K1'], g['R']
    with ExitStack() as es:
        H = hy_common(k, es)
        ef, ei = load_E(k, es, s)
        hycol = sb(k, es, 'hycol', [64, 4], F32)
        S.dma('sp', hycol[:], k.din['hycol'], writes=['hycol'])
        negpi = sb(k, es, 'negpi', [128, 1], F32)
        memset(k, 'dve', negpi[:], -math.pi, ['negpi'])
        ki = sb(k, es, 'hy_ki', [64, 512], mybir.dt.int32)
        kf = sb(k, es, 'hy_kf', [64, 512], F32)
        embT = sb(k, es, 'hy_emb', [33, L], F32)
        S.dma('sp', embT[:], k.din[f'c_emb{s}'], writes=['hy_emb'])
        w1 = sb(k, es, 'hy_w1', [33, 64], F32)
        w2 = sb(k, es, 'hy_w2', [64, 64], F32)
        w3 = sb(k, es, 'hy_w3', [64, 64], F32)
        w4 = sb(k, es, 'hy_w4', [64, 4096], BF16)
        S.dma('sp', w1[:], k.din['hy_f_w1'][j], writes=['hy_w1'])
        S.dma('sp', w2[:], k.din['hy_f_w2'][j], writes=['hy_w2'])
        S.dma('sp', w3[:], k.din['hy_f_w3'][j], writes=['hy_w3'])
        S.dma('pool', w4[:], k.din['hy_f_w4'][j], writes=['hy_w4'])
        hA = sb(k, es, 'hy_hA', [64, L], F32)
        hB = sb(k, es, 'hy_hB', [64, L], F32)
        h3 = sb(k, es, 'hy_h3', [64, L], BF16)
        arg = Ring(k, es, 'hy_arg', 2, [64, 512], F32)
        mp = Ring(k, es, 'hy_mp', 2, [64, 512], F32, psum=True)
        layers = [(w1, 'hy_w1', embT, 'hy_emb', hA, 'hy_hA', 0), (w2, 'hy_w2', hA, 'hy_hA', hB, 'hy_hB', 1),
                  (w3, 'hy_w3', hB, 'hy_hB', h3, 'hy_h3', 2)]
        for (w, wk, src, srck, dst, dstk, li) in layers:
            for t0 in range(0, L, 512):
                n = min(512, L - t0)
                p, pk = mp.next()
                mm(k, p[:, 0:n], w[:], src[:, t0:t0 + n], True, True, [wk, srck], [pk])
                a_, ak = arg.next()
                ts(k, 'dve', a_[:, 0:n], p[:, 0:n], hycol[:, li:li + 1], hycol[:, 3:4], ALU.add, ALU.mult, [pk, 'hycol'], [ak])
                ts(k, 'dve', a_[:, 0:n], a_[:, 0:n], 1.0 / (2.0 * math.pi), 8.0, ALU.mult, ALU.add, [ak], [ak])
                cp(k, 'dve', ki[:, 0:n], a_[:, 0:n], [ak], ['hy_ki'])
                cp(k, 'dve', kf[:, 0:n], ki[:, 0:n], ['hy_ki'], ['hy_kf'])
                tt(k, 'dve', a_[:, 0:n], a_[:, 0:n], kf[:, 0:n], ALU.subtract, [ak, 'hy_kf'], [ak])
                ts(k, 'dve', kf[:, 0:n], a_[:, 0:n], 0.5, None, ALU.is_gt, None, [ak], ['hy_kf'])
                tt(k, 'dve', a_[:, 0:n], a_[:, 0:n], kf[:, 0:n], ALU.subtract, [ak, 'hy_kf'], [ak])
                act(k, dst[:, t0:t0 + n], a_[:, 0:n], AF.Sin, [ak], [dstk], scale=2.0 * math.pi)
        dl = sb(k, es, 'hy_dl', [M1, 1024], F32)
        tcol = sb(k, es, 'hy_tc', [M1, 128], F32)
        S.dma('sp', dl[:], k.din['c_deltas'].partition_broadcast(M1), writes=['hy_dl'])
        S.dma('sp', tcol[:], k.din[f'c_tcol{s}'], writes=['hy_tc'])
        ts(k, 'dve', tcol[:], tcol[:], -1.0, None, ALU.mult, None, ['hy_tc'], ['hy_tc'])
        wn = Ring(k, es, 'hy_wn', 2, [M1, 1024], F32)
        ft = Ring(k, es, 'hy_ft', 2, [M1, 4096], BF16)
        fp = Ring(k, es, 'hy_fp', 3, [M1, 512], F32, psum=True)
        bp = Ring(k, es, 'hy_bp', 3, [R, 512], F32, psum=True)
        bs = Ring(k, es, 'hy_bs', 2, [R, 4096], BF16)
        h3v = h3[:].rearrange("r (a b) -> r b a", b=128)
        for m2 in range(128):
            wt, wtk = wn.next()
            act(k, wt[:], dl[:], AF.Exp, ['hy_dl', 'hy_tc'], [wtk], scale=tcol[:, m2:m2 + 1])
            f_, fk = ft.next()
            for q in range(8):
                p, pk = fp.next()
                mm(k, p[:], h3v[:, m2, :], w4[:, q * 512:(q + 1) * 512], True, True, ['hy_h3', 'hy_w4'], [pk])
                cw = (q % 2) * 512
                tt(k, 'dve', f_[:, q * 512:(q + 1) * 512], p[:], wt[:, cw:cw + 512], ALU.mult, [pk, wtk], [fk])
            if m2 == 0:
                for o in range(2):
                    memset(k, 'dve', f_[0:1, o * 2048 + 1024:(o + 1) * 2048], 0.0, [fk])
            b_, bk = bs.next()
            for q in range(8):
                p, pk = bp.next()
                mm(k, p[:], ef[:, m2, :], f_[:, q * 512:(q + 1) * 512], True, True, ['hy_ef', fk], [pk])
                cp(k, 'act' if q % 2 else 'dve', b_[:, q * 512:(q + 1) * 512], p[:], [pk], [bk])
            S.dma('sp', BdF[m2, 0:R, :], b_[:], reads=[bk])
        S.barrier()
    with ExitStack() as es:
        H = hy_common(k, es)
        fr = Ring(k, es, 'hy_fr', 2, [128, 2, 2048], BF16)
        hs = Ring(k, es, 'hy_hs', 2, [128, 2, 1024], F32)
        hp = Ring(k, es, 'hy_hp', 4, [128, 512], F32, psum=True)
        for o in range(2):
            for k1 in range(K1):
                f_, fk = fr.next()
                S.dma('sp', f_[:], BdF[:, 2 * k1:2 * k1 + 2, o * 2048:(o + 1) * 2048], writes=[fk])
                h_, hk = hs.next()
                for ch in range(2):
                    Fr = f_[:, 0, ch * 512:(ch + 1) * 512]
                    Fi = f_[:, 1, ch * 512:(ch + 1) * 512]
                    Br = f_[:, 0, 1024 + ch * 512:1024 + (ch + 1) * 512]
                    Bi = f_[:, 1, 1024 + ch * 512:1024 + (ch + 1) * 512]
                    p, pk = hp.next()
                    for i_, (mat, rhs) in enumerate((('cs', Fr), ('sn', Fi), ('cs', Br), ('sn', Bi))):
                        mm(k, p[:], H[mat][:], rhs, i_ == 0, i_ == 3, ['hy_' + mat, fk], [pk])
                    cp(k, 'dve', h_[:, 0, ch * 512:(ch + 1) * 512], p[:], [pk], [hk])
                    p, pk = hp.next()
                    for i_, (mat, rhs) in enumerate((('cs', Fi), ('nsn', Fr), ('ncs', Bi), ('sn', Br))):
                        mm(k, p[:], H[mat][:], rhs, i_ == 0, i_ == 3, ['hy_' + mat, fk], [pk])
                    cp(k, 'act', h_[:, 1, ch * 512:(ch + 1) * 512], p[:], [pk], [hk])
                S.dma('sp', Hspec[s][o][k1], h_[:], reads=[hk])
        S.barrier()


def hy_conv(k, s, o, Hspec, Bd, Gd, xbtm, hb, src_col, mul_col, dst, dst_col, first_B_from=None):
    nc, S = k.nc, k.S
    g = HY_SEGS[s]
    s0, L, M1, K1, R = g['s0'], g['L'], g['M1'], g['K1'], g['R']

    def strided(tm, c0):
        return tm[s0:s0 + L, c0:c0 + 1024].rearrange("(a b) c -> b a c", b=128)
    srcv = strided(xbtm if src_col >= 0 else dst, src_col if src_col >= 0 else 0)
    mulv = strided(xbtm, mul_col)
    dstv = strided(dst, dst_col)
    with ExitStack() as es:
        ef, ei = load_E(k, es, s)
        xr = Ring(k, es, 'hc_x', 3, [M1, 1024], BF16)
        bp = Ring(k, es, 'hc_bp', 4, [R, 512], F32, psum=True)
        bs = Ring(k, es, 'hc_bs', 3, [R, 1024], BF16)
        for m2 in range(128):
            x_, xk = xr.next()
            S.dma('sp', x_[:], srcv[m2], writes=[xk])
            b_, bk = bs.next()
            for ch in range(2):
                p, pk = bp.next()
                mm(k, p[:], ef[:, m2, :], x_[:, ch * 512:(ch + 1) * 512], True, True, ['hy_ef', xk], [pk])
                cp(k, 'act' if ch else 'dve', b_[:, ch * 512:(ch + 1) * 512], p[:], [pk], [bk])
            S.dma('sp', Bd[m2, 0:R, :], b_[:], reads=[bk])
        S.barrier()
    with ExitStack() as es:
        H = hy_common(k, es)
        br = Ring(k, es, 'hc_b', 2, [128, 2, 1024], BF16)
        hr = Ring(k, es, 'hc_h', 2, [128, 2, 1024], F32)
        t1 = sb(k, es, 'hc_t1', [128, 1024], F32)
        t2 = sb(k, es, 'hc_t2', [128, 1024], F32)
        t3 = sb(k, es, 'hc_t3', [128, 1024], F32)
        t4 = sb(k, es, 'hc_t4', [128, 1024], F32)
        Y = sb(k, es, 'hc_Y', [128, 2, 1024], BF16)
        gs = Ring(k, es, 'hc_gs', 2, [128, 2, 1024], BF16)
        xp = ps(k, es, 'hc_xp', [128, 2, 1024])
        gp = ps(k, es, 'hc_gp', [128, 2, 1024])
        for k1 in range(K1):
            b_, bk = br.next()
            S.dma('sp', b_[:], Bd[:, 2 * k1:2 * k1 + 2, :], writes=[bk])
            h_, hk = hr.next()
            S.dma('sp', h_[:], Hspec[s][o][k1], writes=[hk])
            for ch in range(2):
                cs_ = slice(ch * 512, (ch + 1) * 512)
                mm(k, xp[:, 0, cs_], H['cs'][:], b_[:, 0, cs_], True, False, ['hy_cs', bk], ['hc_xp'])
                mm(k, xp[:, 0, cs_], H['sn'][:], b_[:, 1, cs_], False, True, ['hy_sn', bk], ['hc_xp'])
                mm(k, xp[:, 1, cs_], H['cs'][:], b_[:, 1, cs_], True, False, ['hy_cs', bk], ['hc_xp'])
                mm(k, xp[:, 1, cs_], H['nsn'][:], b_[:, 0, cs_], False, True, ['hy_nsn', bk], ['hc_xp'])
            tt(k, 'dve', t1[:], xp[:, 0, :], h_[:, 0, :], ALU.mult, ['hc_xp', hk], ['hc_t1'])
            tt(k, 'dve', t2[:], xp[:, 1, :], h_[:, 1, :], ALU.mult, ['hc_xp', hk], ['hc_t2'])
            tt(k, 'dve', t3[:], xp[:, 0, :], h_[:, 1, :], ALU.mult, ['hc_xp', hk], ['hc_t3'])
            tt(k, 'dve', t4[:], xp[:, 1, :], h_[:, 0, :], ALU.mult, ['hc_xp', hk], ['hc_t4'])
            tt(k, 'pool', Y[:, 0, :], t1[:], t2[:], ALU.subtract, ['hc_t1', 'hc_t2'], ['hc_Y'])
            tt(k, 'pool', Y[:, 1, :], t3[:], t4[:], ALU.add, ['hc_t3', 'hc_t4'], ['hc_Y'])
            for ch in range(2):
                cs_ = slice(ch * 512, (ch + 1) * 512)
                mm(k, gp[:, 0, cs_], H['cs'][:], Y[:, 0, cs_], True, False, ['hy_cs', 'hc_Y'], ['hc_gp'])
                mm(k, gp[:, 0, cs_], H['nsn'][:], Y[:, 1, cs_], False, True, ['hy_nsn', 'hc_Y'], ['hc_gp'])
                mm(k, gp[:, 1, cs_], H['sn'][:], Y[:, 0, cs_], True, False, ['hy_sn', 'hc_Y'], ['hc_gp'])
                mm(k, gp[:, 1, cs_], H['cs'][:], Y[:, 1, cs_], False, True, ['hy_cs', 'hc_Y'], ['hc_gp'])
            g_, gk = gs.next()
            cp(k, 'act', g_[:, 0, :], gp[:, 0, :], ['hc_gp'], [gk])
            cp(k, 'act', g_[:, 1, :], gp[:, 1, :], ['hc_gp'], [gk])
            S.dma('sp', Gd[:, 2 * k1:2 * k1 + 2, :], g_[:], reads=[gk])
        S.barrier()
    with ExitStack() as es:
        ef, ei = load_E(k, es, s)
        bias = sb(k, es, 'hc_bias', [M1, 1024], F32)
        S.dma('sp', bias[:], k.din['hy_bias'][0, o].partition_broadcast(M1), writes=['hc_bias'])
        gr = Ring(k, es, 'hc_g', 3, [R, 1024], BF16)
        vr = Ring(k, es, 'hc_v', 3, [M1, 1024], BF16)
        mr = Ring(k, es, 'hc_m', 3, [M1, 1024], BF16)
        tr = Ring(k, es, 'hc_t', 2, [M1, 1024], F32)
        zr = Ring(k, es, 'hc_z', 3, [M1, 1024], BF16)
        yp = Ring(k, es, 'hc_yp', 3, [M1, 1024], F32, psum=True)
        for n2 in range(128):
            g_, gk = gr.next()
            S.dma('sp', g_[:], Gd[n2, 0:R, :], writes=[gk])
            v_, vk = vr.next()
            S.dma('sp', v_[:], srcv[n2], writes=[vk])
            m_, mk = mr.next()
            S.dma('sp', m_[:], mulv[n2], writes=[mk])
            p, pk = yp.next()
            for ch in range(2):
                mm(k, p[:, ch * 512:(ch + 1) * 512], ei[:, n2, :], g_[:, ch * 512:(ch + 1) * 512], True, True,
                   ['hy_ei', gk], [pk])
            t_, tk = tr.next()
            tt(k, 'pool', t_[:], v_[:], bias[:], ALU.mult, [vk, 'hc_bias'], [tk])
            tt(k, 'dve', t_[:], t_[:], p[:], ALU.add, [tk, pk], [tk])
            z_, zk = zr.next()
            tt(k, 'dve', z_[:], t_[:], m_[:], ALU.mult, [tk, mk], [zk])
            S.dma('sp', dstv[n2], z_[:], reads=[zk])
        S.barrier()


def hyena_mixer(k, l, j):
    nc, S = k.nc, k.S
    preT = scr(k, 'xbcT', [4096, T], BF16)
    cvT = scr(k, 'cvT', [4096, T], BF16)
    xbtm = scr(k, 'xbtm', [T, 3072], BF16)
    otm = scr(k, 'hy_otm', [T, 2048], BF16)
    BdF = scr(k, 'hy_BdF', [128, 66, 4096], BF16)
    Bd = scr(k, 'hy_Bd', [128, 66, 1024], BF16)
    Gd = scr(k, 'hy_Gd', [128, 66, 1024], BF16)
    Hspec = {s: [[scr(k, f'hy_H{s}_{o}_{k1}', [128, 2, 1024], F32) for k1 in range(HY_SEGS[s]['K1'])]
                 for o in range(2)] for s in HY_SEGS}
    W = k.din['hy_w_in'][j]
    with ExitStack() as es:
        uT = build_uT(k, es, l)
        st = Ring(k, es, 'hst', 4, [128, 512], BF16)
        specs = [('fm', i * 512, 512, fm_store(k, st, preT, 0, eng=('dve' if i % 2 else 'act'))) for i in range(6)]
        inproj(k, es, uT, W, specs)
        S.barrier()
    with ExitStack() as es:
        conv_fm(k, es, preT, cvT, 24, 3, 'hy_conv_w', 'hy_conv_b', IDENT)
        S.barrier()
    with ExitStack() as es:
        fm_to_tm(k, es, cvT, 0, 24, xbtm, 0)
        S.barrier()
    for s in HY_SEGS:
        hy_filters(k, j, s, Hspec, BdF)
        hy_conv(k, s, 0, Hspec, Bd, Gd, xbtm, None, 0, 1024, otm, 0)
        hy_conv(k, s, 1, Hspec, Bd, Gd, xbtm, None, -1, 2048, otm, 1024)
    with ExitStack() as es:
        ir = Ring(k, es, 'ho_i', 2, [128, 1024], BF16)
        ytR = Ring(k, es, 'ho_yt', 2, [128, 8, 128], BF16)
        tp = ps(k, es, 'ho_tp', [128, 8, 128], BF16)
        yv = k.yT[0:1024, :].rearrange("(dc p) t -> p dc t", p=128)
        for a in range(NTT):
            i_, ik = ir.next()
            S.dma('sp', i_[:], otm[a * 128:(a + 1) * 128, 1024:2048], writes=[ik])
            transpose_out(k, tp, i_, ik, ytR, 8, yv, a)
        S.barrier()

def mixer_copy(k, l):
    S = k.S
    with ExitStack() as es:
        uT = build_uT(k, es, l)
        yv = k.yT[0:1024, :].rearrange("(kc p) t -> p kc t", p=128)
        for (t0, n) in BLOCKS:
            S.dma('sp', yv[:, :, t0:t0 + n], uT[:, :, t0:t0 + n], reads=[('uT', t0)], writes=[('yT', t0)])
        S.barrier()


WEIGHT_NAMES = ['ada_w', 'ffn_w1', 'ffn_w2', 'gla_w_in', 'gla_w_a2', 'gla_b_a2', 'gla_norm', 'gla_w_out',
                'ssd_w_in', 'ssd_dt_bias', 'ssd_a_log', 'ssd_d', 'ssd_norm', 'ssd_w_out',
                'hy_w_in', 'hy_f_w1', 'hy_f_w2', 'hy_f_w3', 'hy_f_w4', 'hy_bias', 'hy_w_out']


def host_constants():
    c = {}
    c['c_ident'] = np.eye(128, dtype=np.float32)
    perm = np.arange(128)
    perm[64:] = 64 + (127 - perm[64:])
    P = np.zeros((128, 128), np.float32)
    P[perm, np.arange(128)] = 1.0
    c['c_psnake'] = P
    j = np.arange(128)[:, None]
    i = np.arange(128)[None, :]
    le = (j <= i).astype(np.float32)
    ge = (j >= i).astype(np.float32)
    gt = (j > i).astype(np.float32)
    lt = (j < i).astype(np.float32)
    c['c_mask'] = np.stack([le, ge])
    c['c_tri'] = np.stack([le, ge])
    c['c_rev'] = np.stack([gt, lt])
    c['c_mbias'] = np.stack([(le - 1.0) * 30000.0, (ge - 1.0) * 30000.0]).astype(np.float32)
    sel = np.zeros((32, 32, 128), np.float32)
    for h in range(32):
        sel[h, h, :] = 1.0
    c['c_selh'] = sel.reshape(32, 32 * 128)
    c.update(hyena_constants())
    return c


def build_program(shapes, cp, nlayers=DEPTH, mixers=None, debug_out=None):
    nc = bass.Bass("TRN2", target_bir_lowering=False)
    k = K()
    k.nc = nc
    k.cp = cp
    k.nlayers = nlayers
    k.din = {}
    for name, (shape, dt) in shapes.items():
        k.din[name] = nc.dram_tensor(name, list(shape), dt, kind="ExternalInput").ap()
    k.dout = nc.dram_tensor("out", [SEQ, D], F32, kind="ExternalOutput").ap()
    k.hT = nc.dram_tensor("hT", [D, T], F32).ap()
    k.yT = nc.dram_tensor("yT", [2048, T], BF16).ap()
    k.dbg = {}
    with ExitStack() as es:
        k.S = Sched(nc, es)
        k.colp = sb(k, es, 'colp', [128, cp.n], F32)
        k.MOD = sb(k, es, 'MOD', [128, DEPTH, 2, 48], F32)
        k.ident_f = sb(k, es, 'ident_f', [128, 128], F32)
        k.ident_b = sb(k, es, 'ident_b', [128, 128], BF16)
        k.ones_b = sb(k, es, 'ones_b', [128, 128], BF16)
        k.ones_f = sb(k, es, 'ones_f', [128, 128], F32)
        k.kinds = [(mixers[l] if mixers else ['gla', 'ssd', 'hy'][l % 3]) for l in range(nlayers)]
        prologue(k)
        weight_prep(k)
        for l in range(nlayers):
            kind = (mixers[l] if mixers else ['gla', 'ssd', 'hy'][l % 3])
            j = l // 3
            if kind == 'copy':
                mixer_copy(k, l)
                post(k, l, k.din['gla_w_out'][0], 8)
            elif kind == 'gla':
                gla_mixer(k, l, j)
                post(k, l, k.din['gla_w_out'][j], 8)
            elif kind == 'ssd':
                ssd_mixer(k, l, j)
                post(k, l, k.din['ssd_w_out'][j], 16)
            else:
                hyena_mixer(k, l, j)
                post(k, l, k.din['hy_w_out'][j], 8)
        epilogue(k)
    k.ninst = k.S.ninst
    return nc, k


def host_inputs(inputs, b):
    m = {}
    m['x'] = np.ascontiguousarray(inputs['x'][b])
    m['ctx'] = np.ascontiguousarray(inputs['ctx'][b])
    m['cvec'] = np.ascontiguousarray(np.concatenate([col_layout(inputs['c'][b]), col_layout(inputs['c_ctx'])], axis=1))
    for n in WEIGHT_NAMES:
        m[n] = np.ascontiguousarray(np.asarray(inputs[n], np.float32))
    return m


def make_colpack(inputs):
    cp = ColPack()
    for l in range(DEPTH):
        cp.add(f'ada_b{l}', inputs['ada_b'][l])
        for s in range(2):
            cp.add(f'ln_g{l}_{s}', inputs['ln_g'][l, s])
            cp.add(f'ln_b{l}_{s}', inputs['ln_b'][l, s])
    cp.add('ssd_conv_b', inputs['ssd_conv_b'][0])
    for t in range(5):
        cp.add(f'ssd_conv_w{t}', inputs['ssd_conv_w'][0, t])
    cp.add('hy_conv_b', inputs['hy_conv_b'][0])
    for t in range(3):
        cp.add(f'hy_conv_w{t}', inputs['hy_conv_w'][0, t])
    return cp


_CACHE = {}


def kernel(**inputs):
    inputs = {n: np.asarray(v) for n, v in inputs.items()}
    cp = make_colpack(inputs)
    consts = host_constants()
    maps = []
    for b in range(8):
        m = host_inputs(inputs, b)
        m['colp'] = cp.array()
        m.update(consts)
        hyena_host_layout(inputs, m)
        maps.append(m)
    if 'nc' not in _CACHE:
        shapes = {n: (v.shape, F32) for n, v in maps[0].items()}
        _CACHE['nc'] = build_program(shapes, cp)[0]
    res = run_bass_kernel_spmd(_CACHE['nc'], maps, core_ids=list(range(8)))
    return np.stack([np.asarray(r['out'], np.float32) for r in res.results], axis=0)
```
